# Optimizing a Trainium2 kernel written in Bass

```python
import math
import jax, jax.numpy as jnp
from jax import lax
import numpy as np

D_MODEL = 2048
BATCH = 8
SEQ = 4096
DEPTH = 2
DEC_BATCH = 32
DEC_SEQ = 32
PAST_LEN = 1024

CHUNK = 64
H_A = 8
DK_A = 64
DV_A = 2 * DK_A
H_B = 8
D_B = 128
BAND_PREV_CHUNKS = 8
BAND_REACH = BAND_PREV_CHUNKS * CHUNK
BAND_LEN = BAND_REACH + CHUNK
REL_CLIP = 128
T5_BUCKETS = 32
T5_MAX_DIST = 128
D_FF = -(-8 * D_MODEL // (3 * 256)) * 256
ALPHA = (2 * DEPTH) ** 0.25
BETA = (8 * DEPTH) ** -0.25
QBLK = 128
NEG = -1e30
LN_EPS = 1e-5
QA_W = H_A * 2 * DK_A
VA_W = H_A * DV_A
B_W = H_B * D_B
IN_WIDTHS = (QA_W, QA_W, VA_W, B_W, B_W, B_W, D_MODEL, D_MODEL)
IN_W = sum(IN_WIDTHS)
IN_SPLITS = tuple(int(v) for v in np.cumsum(IN_WIDTHS)[:-1])

kernel_name = "streaming_hybrid_diffattn_chunkband"


def layer_norm(x, g=None, b=None):
    xf = x.astype(jnp.float32)
    xc = xf - jnp.mean(xf, -1, keepdims=True)
    y = xc * lax.rsqrt(jnp.mean(xc * xc, -1, keepdims=True) + LN_EPS)
    if g is not None:
        y = y * g.astype(jnp.float32) + b.astype(jnp.float32)
    return y.astype(x.dtype)


def head_rms_norm(x, g):
    xf = x.astype(jnp.float32)
    y = xf * lax.rsqrt(jnp.mean(xf * xf, -1, keepdims=True) + LN_EPS) * g.astype(jnp.float32)
    return y.astype(x.dtype)


def t5_bucket(rel):
    nb = T5_BUCKETS // 2
    max_exact = nb // 2
    bucket = (rel > 0).astype(jnp.int32) * nb
    n = jnp.abs(rel)
    nf = jnp.maximum(n, 1).astype(jnp.float32)
    large = max_exact + (jnp.log(nf / max_exact) / math.log(T5_MAX_DIST / max_exact)
                         * (nb - max_exact)).astype(jnp.int32)
    large = jnp.minimum(large, nb - 1)
    return bucket + jnp.where(n < max_exact, n, large)


def diff_attention(q, k, v, q_pos, k_pos, t5_bias, lam, subln_g, lam_init):
    s = jnp.einsum("bqhtd,bkhtd->bthqk", q, k).astype(jnp.float32) * (DK_A ** -0.5)
    rel = k_pos[None, :] - q_pos[:, None]
    bias = jnp.moveaxis(t5_bias[t5_bucket(rel)], -1, 0).astype(jnp.float32)
    visible = (k_pos[None, :] // CHUNK) <= (q_pos[:, None] // CHUNK)
    p = jax.nn.softmax(jnp.where(visible, s + bias, NEG), axis=-1)
    a = (p[:, 0] - lam * p[:, 1]).astype(v.dtype)
    o = jnp.einsum("bhqk,bkhd->bqhd", a, v)
    o = head_rms_norm(o, subln_g) * (1.0 - lam_init)
    return o.reshape(o.shape[0], o.shape[1], H_A * DV_A)


def band_attention(q, k, v, q_pos, k_pos, rel_bias):
    s = jnp.einsum("bqhd,bkhd->bhqk", q, k).astype(jnp.float32) * (D_B ** -0.5)
    rel = jnp.clip(k_pos[None, :] - q_pos[:, None], -REL_CLIP, REL_CLIP) + REL_CLIP
    bias = jnp.moveaxis(rel_bias[rel], -1, 0).astype(jnp.float32)
    dchunk = q_pos[:, None] // CHUNK - k_pos[None, :] // CHUNK
    visible = (dchunk >= 0) & (dchunk <= BAND_PREV_CHUNKS) & (k_pos[None, :] >= 0)
    p = jax.nn.softmax(jnp.where(visible, s + bias, NEG), axis=-1).astype(v.dtype)
    o = jnp.einsum("bhqk,bkhd->bqhd", p, v)
    return o.reshape(o.shape[0], o.shape[1], H_B * D_B)


def diff_attention_prompt(q, k, v, t5_bias, lam, subln_g, lam_init):
    b, s = q.shape[:2]
    nblk = s // QBLK
    pos = jnp.arange(s)
    qb = jnp.moveaxis(q.reshape(b, nblk, QBLK, H_A, 2, DK_A), 1, 0)

    def one_block(args):
        q_blk, i = args
        return diff_attention(q_blk, k, v, i * QBLK + jnp.arange(QBLK), pos,
                              t5_bias, lam, subln_g, lam_init)

    o = lax.map(one_block, (qb, jnp.arange(nblk)))
    return jnp.moveaxis(o, 0, 1).reshape(b, s, H_A * DV_A)


def band_attention_prompt(q, k, v, rel_bias):
    b, s = q.shape[:2]
    nc = s // CHUNK
    pad = ((0, 0), (BAND_REACH, 0), (0, 0), (0, 0))
    k_pad = jnp.pad(k, pad)
    v_pad = jnp.pad(v, pad)
    band_off = jnp.arange(BAND_LEN) - BAND_REACH

    def one_chunk(n):
        start = n * CHUNK
        q_n = lax.dynamic_slice_in_dim(q, start, CHUNK, axis=1)
        k_n = lax.dynamic_slice_in_dim(k_pad, start, BAND_LEN, axis=1)
        v_n = lax.dynamic_slice_in_dim(v_pad, start, BAND_LEN, axis=1)
        return band_attention(q_n, k_n, v_n, start + jnp.arange(CHUNK), start + band_off, rel_bias)

    o = lax.map(one_chunk, jnp.arange(nc))
    return jnp.moveaxis(o, 0, 1).reshape(b, s, H_B * D_B)


def block_forward(x, c, attend, w_mod, b_mod, w_in, w_oa, w_ob, w_out, ln1_g, ln1_b,
                  w1, w3, w2, ln2_g, ln2_b):
    b, t, _ = x.shape
    mod = jax.nn.silu(c) @ w_mod + b_mod
    sh_m, sc_m, g_m, sh_f, sc_f, g_f = jnp.split(mod[:, None, :], 6, axis=-1)
    h = layer_norm(x) * (1 + sc_m) + sh_m
    qa, ka, va, qb, kb, vb, ga, gb = jnp.split(h @ w_in, IN_SPLITS, axis=-1)
    qa = qa.reshape(b, t, H_A, 2, DK_A)
    ka = ka.reshape(b, t, H_A, 2, DK_A)
    va = va.reshape(b, t, H_A, DV_A)
    qb = qb.reshape(b, t, H_B, D_B)
    kb = kb.reshape(b, t, H_B, D_B)
    vb = vb.reshape(b, t, H_B, D_B)
    oa, ob = attend(qa, ka, va, qb, kb, vb)
    merged = jax.nn.sigmoid(ga) * (oa @ w_oa) + jax.nn.sigmoid(gb) * (ob @ w_ob)
    x = layer_norm(ALPHA * x + g_m * (merged @ w_out), ln1_g, ln1_b)
    h = layer_norm(x) * (1 + sc_f) + sh_f
    f = (jax.nn.silu(h @ w1) * (h @ w3)) @ w2
    x = layer_norm(ALPHA * x + g_f * f, ln2_g, ln2_b)
    return x, ka, va, kb, vb


def setup_inputs(seed: int = 0) -> dict:
    key = jax.random.key(seed)
    ks = jax.random.split(key, 32)

    def nrm(k, shape, s):
        return jax.random.normal(k, shape, jnp.float32) * s

    band_past = min(BAND_REACH, PAST_LEN)
    D = D_MODEL
    return {
        "x_prompt": nrm(ks[0], (BATCH, SEQ, D), 1.0),
        "x_sample": nrm(ks[1], (DEC_BATCH, DEC_SEQ, D), 1.0),
        "cache_a_k": nrm(ks[2], (DEPTH, DEC_BATCH, PAST_LEN, H_A, 2, DK_A), 1.0),
        "cache_a_v": nrm(ks[3], (DEPTH, DEC_BATCH, PAST_LEN, H_A, DV_A), 1.0),
        "cache_b_k": nrm(ks[4], (DEPTH, DEC_BATCH, band_past, H_B, D_B), 1.0),
        "cache_b_v": nrm(ks[5], (DEPTH, DEC_BATCH, band_past, H_B, D_B), 1.0),
        "c_prompt": nrm(ks[6], (BATCH, D), 1.0),
        "c_sample": nrm(ks[7], (DEC_BATCH, D), 1.0),
        "w_mod": nrm(ks[8], (DEPTH, D, 6 * D), 0.5 * D ** -0.5),
        "b_mod": nrm(ks[9], (DEPTH, 6 * D), 0.02),
        "w_in": nrm(ks[10], (DEPTH, D, IN_W), D ** -0.5),
        "lambda_q1": nrm(ks[11], (DEPTH, DK_A), 0.1),
        "lambda_k1": nrm(ks[12], (DEPTH, DK_A), 0.1),
        "lambda_q2": nrm(ks[13], (DEPTH, DK_A), 0.1),
        "lambda_k2": nrm(ks[14], (DEPTH, DK_A), 0.1),
        "subln_g": 1.0 + nrm(ks[15], (DEPTH, DV_A), 0.02),
        "t5_bias": nrm(ks[16], (T5_BUCKETS, H_A), 0.5),
        "rel_bias": nrm(ks[17], (DEPTH, 2 * REL_CLIP + 1, H_B), 0.5),
        "w_oa": nrm(ks[18], (DEPTH, VA_W, D), BETA * VA_W ** -0.5),
        "w_ob": nrm(ks[19], (DEPTH, B_W, D), BETA * B_W ** -0.5),
        "w_out": nrm(ks[20], (DEPTH, D, D), BETA * D ** -0.5),
        "ln1_g": 1.0 + nrm(ks[21], (DEPTH, D), 0.02),
        "ln1_b": nrm(ks[22], (DEPTH, D), 0.02),
        "w1": nrm(ks[23], (DEPTH, D, D_FF), D ** -0.5),
        "w3": nrm(ks[24], (DEPTH, D, D_FF), D ** -0.5),
        "w2": nrm(ks[25], (DEPTH, D_FF, D), BETA * D_FF ** -0.5),
        "ln2_g": 1.0 + nrm(ks[26], (DEPTH, D), 0.02),
        "ln2_b": nrm(ks[27], (DEPTH, D), 0.02),
    }


def reference(x_prompt, x_sample, cache_a_k, cache_a_v, cache_b_k, cache_b_v, c_prompt, c_sample,
              w_mod, b_mod, w_in, lambda_q1, lambda_k1, lambda_q2, lambda_k2, subln_g, t5_bias,
              rel_bias, w_oa, w_ob, w_out, ln1_g, ln1_b, w1, w3, w2, ln2_g, ln2_b):
    past = cache_a_k.shape[2]
    band_past = cache_b_k.shape[2]
    t_new = x_sample.shape[1]
    prompt_band = min(BAND_REACH, x_prompt.shape[1])
    q_pos_s = past + jnp.arange(t_new)
    ka_pos_s = jnp.arange(past + t_new)
    kb_pos_s = past - band_past + jnp.arange(band_past + t_new)

    xp, xs = x_prompt, x_sample
    akp, avp, bkp, bvp, aks, avs, bks, bvs = [], [], [], [], [], [], [], []
    for l in range(DEPTH):
        lam_init = 0.8 - 0.6 * math.exp(-0.3 * l)
        f32 = jnp.float32
        lam = (jnp.exp(jnp.sum(lambda_q1[l].astype(f32) * lambda_k1[l].astype(f32)))
               - jnp.exp(jnp.sum(lambda_q2[l].astype(f32) * lambda_k2[l].astype(f32))) + lam_init)
        layer_w = (w_mod[l], b_mod[l], w_in[l], w_oa[l], w_ob[l], w_out[l], ln1_g[l], ln1_b[l],
                   w1[l], w3[l], w2[l], ln2_g[l], ln2_b[l])

        def attend_prompt(qa, ka, va, qb, kb, vb):
            oa = diff_attention_prompt(qa, ka, va, t5_bias, lam, subln_g[l], lam_init)
            ob = band_attention_prompt(qb, kb, vb, rel_bias[l])
            return oa, ob

        def attend_sample(qa, ka, va, qb, kb, vb):
            ka_all = jnp.concatenate([cache_a_k[l].astype(ka.dtype), ka], axis=1)
            va_all = jnp.concatenate([cache_a_v[l].astype(va.dtype), va], axis=1)
            oa = diff_attention(qa, ka_all, va_all, q_pos_s, ka_pos_s, t5_bias, lam, subln_g[l], lam_init)
            kb_all = jnp.concatenate([cache_b_k[l].astype(kb.dtype), kb], axis=1)
            vb_all = jnp.concatenate([cache_b_v[l].astype(vb.dtype), vb], axis=1)
            ob = band_attention(qb, kb_all, vb_all, q_pos_s, kb_pos_s, rel_bias[l])
            return oa, ob

        xp, ka, va, kb, vb = block_forward(xp, c_prompt, attend_prompt, *layer_w)
        akp.append(ka)
        avp.append(va)
        bkp.append(kb[:, kb.shape[1] - prompt_band:])
        bvp.append(vb[:, vb.shape[1] - prompt_band:])
        xs, ka, va, kb, vb = block_forward(xs, c_sample, attend_sample, *layer_w)
        aks.append(ka)
        avs.append(va)
        bks.append(kb)
        bvs.append(vb)

    return (xp, xs, jnp.stack(akp), jnp.stack(avp), jnp.stack(bkp), jnp.stack(bvp),
            jnp.stack(aks), jnp.stack(avs), jnp.stack(bks), jnp.stack(bvs))
```

```python
import math
import numpy as np
from contextlib import ExitStack
import concourse.bass as bass
import concourse.mybir as mybir
from concourse.bass_utils import run_bass_kernel_spmd

F32 = mybir.dt.float32
BF16 = mybir.dt.bfloat16
AF = mybir.ActivationFunctionType
ALU = mybir.AluOpType
AX = mybir.AxisListType

D = 2048
KC = 16
DFF = 5632
FC = 44
INW = 10240
DEPTH = 2
ALPHA = float((2 * DEPTH) ** 0.25)
LN_EPS = 1e-5
CHUNK = 64
NCORES = 8

ENGS = ("pe", "act", "dve", "pool", "sp")
N_DMA_SEMS = {"sp": 30, "act": 8, "pool": 30}
SEM_WRAP = 30000
SQ = "pool"


class Op:
    __slots__ = ("idx", "eng", "fn", "dma", "deps", "needs_inc", "ticket", "dsem", "dval", "pre")

    def __init__(self, idx, eng, fn, dma):
        self.idx = idx
        self.eng = eng
        self.fn = fn
        self.dma = dma
        self.deps = ()
        self.needs_inc = False
        self.ticket = None
        self.dsem = None
        self.dval = 0
        self.pre = None


class Fw:
    def __init__(self, nc):
        self.nc = nc
        self.ops = []
        self.lastw = {}
        self.readers = {}

    def op(self, eng, fn, reads=(), writes=(), dma=False, barrier=False):
        if not barrier and "ARENA" in writes:
            writes = [w for w in writes if w != "ARENA"]
            reads = list(reads) + ["ARENA"]
        idx = len(self.ops)
        o = Op(idx, eng, fn, dma)
        deps = set()
        for r in reads:
            lw = self.lastw.get(r)
            if lw is not None:
                deps.add(lw)
        for w in writes:
            lw = self.lastw.get(w)
            if lw is not None:
                deps.add(lw)
            rd = self.readers.get(w)
            if rd:
                deps.update(rd[0].values())
                deps.update(rd[1])
        for r in reads:
            rd = self.readers.get(r)
            if rd is None:
                rd = self.readers[r] = ({}, [])
            if dma:
                rd[1].append(idx)
            else:
                rd[0][eng] = idx
        for w in writes:
            self.lastw[w] = idx
            self.readers[w] = ({}, [])
        keep = []
        for d in deps:
            y = self.ops[d]
            if not y.dma:
                if y.eng == eng and not dma and eng == "pe":
                    continue
                y.needs_inc = True
            keep.append(d)
        o.deps = keep
        self.ops.append(o)
        return o

    def pe(self, fn, reads=(), writes=()):
        return self.op("pe", fn, reads, writes)

    def act(self, fn, reads=(), writes=()):
        return self.op("act", fn, reads, writes)

    def dve(self, fn, reads=(), writes=()):
        return self.op("dve", fn, reads, writes)

    def pool(self, fn, reads=(), writes=()):
        return self.op("pool", fn, reads, writes)

    def dma(self, q, out, in_, reads=(), writes=()):
        return self.op(q, lambda e: e.dma_start(out=out, in_=in_), reads, writes, dma=True)

    def emit(self):
        nc = self.nc
        with ExitStack() as es:
            counts = {e: 0 for e in ENGS}
            for o in self.ops:
                if not o.dma and o.needs_inc:
                    counts[o.eng] += 1
                    o.ticket = counts[o.eng]
            esems = {}
            for e in ENGS:
                n = counts[e] // SEM_WRAP + 1
                esems[e] = [es.enter_context(nc.semaphore(f"s_{e}_{i}")) for i in range(n)]
            dsems = {q: [es.enter_context(nc.semaphore(f"d_{q}_{i}")) for i in range(n)]
                     for q, n in N_DMA_SEMS.items()}
            dcount = {q: 0 for q in N_DMA_SEMS}
            dhist = {q: [] for q in N_DMA_SEMS}
            for o in self.ops:
                if o.dma:
                    q = o.eng
                    k = dcount[q]
                    P = N_DMA_SEMS[q]
                    o.dsem = dsems[q][k % P]
                    o.dval = 16 * (k // P + 1)
                    if k >= P:
                        o.pre = dhist[q][k - P]
                    dhist[q].append(o)
                    dcount[q] += 1

            def target(y):
                if y.dma:
                    return y.dsem, y.dval
                t = y.ticket
                ep = (t - 1) // SEM_WRAP
                return esems[y.eng][ep], t - ep * SEM_WRAP

            per_eng = {e: [] for e in ENGS}
            for o in self.ops:
                per_eng[o.eng].append(o)
            all_dma = [o for o in self.ops if o.dma]
            ops = self.ops

            def run_engine(ename, eng):
                waited = {}
                for o in per_eng[ename]:
                    w = {}
                    deps = [ops[d] for d in o.deps]
                    if o.pre is not None:
                        deps.append(o.pre)
                    for y in deps:
                        s, v = target(y)
                        key = id(s)
                        if waited.get(key, 0) >= v:
                            continue
                        if key not in w or w[key][1] < v:
                            w[key] = (s, v)
                    for key, (s, v) in w.items():
                        eng.wait_ge(s, v)
                        waited[key] = v
                    inst = o.fn(eng)
                    if o.dma:
                        inst.then_inc(o.dsem, 16)
                    elif o.needs_inc:
                        s, v = target(o)
                        inst.then_inc(s, 1)
                if ename == "sp":
                    last = {}
                    for y in all_dma:
                        k = id(y.dsem)
                        if k not in last or last[k][1] < y.dval:
                            last[k] = (y.dsem, y.dval)
                    for key, (s, v) in last.items():
                        if waited.get(key, 0) < v:
                            eng.wait_ge(s, v)

            with nc.Block() as block:
                @block.tensor
                def _(e):
                    run_engine("pe", e)

                @block.scalar
                def _(e):
                    run_engine("act", e)

                @block.vector
                def _(e):
                    run_engine("dve", e)

                @block.gpsimd
                def _(e):
                    run_engine("pool", e)

                @block.sync
                def _(e):
                    run_engine("sp", e)
        return counts, dcount


class Tl:
    __slots__ = ("t", "r")

    def __init__(self, t, r):
        self.t = t
        self.r = r


SB_BASE = 16640
CONST_BYTES = 11 * 1024
NWB = 3
WB_BYTES = 16 * 1024
WB0 = SB_BASE + CONST_BYTES
ARENA0 = WB0 + NWB * WB_BYTES
SBUF_LIMIT = 229376 - 128


class Builder:
    def __init__(self, S=4096, TA=1024, TCD=512, PAST=1024, BPAST=512, NSB=4, TS=32):
        assert NSB * TS == 128
        self.S, self.TA, self.TCD = S, TA, TCD
        self.PAST, self.BPAST, self.NSB, self.TS = PAST, BPAST, NSB, TS
        self.NPB = S // 128
        self.NB = self.NPB + 1
        self.NT = S + 128
        self.BOUT = min(512, S)
        self.nc = bass.Bass("TRN2", target_bir_lowering=False)
        self.f = Fw(self.nc)
        self.uid = 0
        self.wcnt = 0
        self.rot = {}
        self.declare()

    def din(self, name, shape, dt=F32):
        return self.nc.dram_tensor(name, list(shape), dt, kind="ExternalInput").ap()

    def dout(self, name, shape):
        return self.nc.dram_tensor(name, list(shape), F32, kind="ExternalOutput").ap()

    def dscr(self, name, shape, dt):
        return self.nc.dram_tensor(name, list(shape), dt, kind="Internal").ap()

    def declare(self):
        S, NT = self.S, self.NT
        L = DEPTH
        self.xp = self.din("xp", [S, D])
        self.xs = self.din("xs", [128, D])
        self.cak = self.din("cak", [L, self.NSB, self.PAST, 1024])
        self.cav = self.din("cav", [L, self.NSB, self.PAST, 1024])
        self.cbk = self.din("cbk", [L, self.NSB, self.BPAST, 1024])
        self.cbv = self.din("cbv", [L, self.NSB, self.BPAST, 1024])
        self.cbc = self.din("cbc", [2, 128, D])
        self.w_mod = self.din("w_mod", [L, D, 6 * D])
        self.bmod = self.din("bmod", [L, 128, 6 * D])
        self.w_in = self.din("w_in", [L, D, INW])
        self.lamv = self.din("lamv", [L, 4, 128, 64])
        self.subln = self.din("subln", [L, 128, 128])
        self.biasA = self.din("biasA", [128, 8, 256])
        self.c15A = self.din("c15A", [128, 8])
        self.biasB = self.din("biasB", [L, 128, 8, 256])
        self.c0B = self.din("c0B", [L, 128, 8])
        self.mask01 = self.din("mask01", [128, 256])
        self.identd = self.din("ident", [128, 128])
        self.w_oa = self.din("w_oa", [L, 1024, D])
        self.w_ob = self.din("w_ob", [L, 1024, D])
        self.w_out = self.din("w_out", [L, D, D])
        self.ln1g = self.din("ln1g", [L, 128, D])
        self.ln1b = self.din("ln1b", [L, 128, D])
        self.w1 = self.din("w1", [L, D, DFF])
        self.w3 = self.din("w3", [L, D, DFF])
        self.w2 = self.din("w2", [L, DFF, D])
        self.ln2g = self.din("ln2g", [L, 128, D])
        self.ln2b = self.din("ln2b", [L, 128, D])
        self.y_p = self.dout("y_p", [S, D])
        self.y_s = self.dout("y_s", [128, D])
        self.akp = self.dout("akp", [L, S, 1024])
        self.avp = self.dout("avp", [L, S, 1024])
        self.bkp = self.dout("bkp", [L, self.BOUT, 1024])
        self.bvp = self.dout("bvp", [L, self.BOUT, 1024])
        self.aks = self.dout("aks", [L, 128, 1024])
        self.avs = self.dout("avs", [L, 128, 1024])
        self.bks = self.dout("bks", [L, 128, 1024])
        self.bvs = self.dout("bvs", [L, 128, 1024])
        self.modbc = self.dscr("modbc", [L, 2, 128, 6 * D], F32)
        self.qaT = self.dscr("qaT", [L, 1024, NT], BF16)
        self.kaT = self.dscr("kaT", [L, 1024, NT], BF16)
        self.qbT = self.dscr("qbT", [L, 1024, NT], BF16)
        self.kbT = self.dscr("kbT", [L, 1024, NT], BF16)
        self.va = self.dscr("va", [L, NT, 1024], BF16)
        self.vb = self.dscr("vb", [L, NT, 1024], BF16)
        self.gaT = self.dscr("gaT", [L, D, NT], BF16)
        self.gbT = self.dscr("gbT", [L, D, NT], BF16)
        self.oaT = self.dscr("oaT", [L, 1024, NT], BF16)
        self.obT = self.dscr("obT", [L, 1024, NT], BF16)
        self.x1 = self.dscr("x1", [L, NT, D], F32)
        self.x2 = self.dscr("x2", [L, NT, D], F32)
        self.wsc = {"w_oa": (self.w_oa, self.dscr("w_oa_b", [L, 1024, D], BF16)),
                    "w_ob": (self.w_ob, self.dscr("w_ob_b", [L, 1024, D], BF16)),
                    "w_out": (self.w_out, self.dscr("w_out_b", [L, D, D], BF16)),
                    "w1": (self.w1, self.dscr("w1_b", [L, D, DFF], BF16)),
                    "w3": (self.w3, self.dscr("w3_b", [L, D, DFF], BF16)),
                    "w2": (self.w2, self.dscr("w2_b", [L, DFF, D], BF16))}
        self.ps = [self.nc.alloc_psum_tensor(f"psb{i}", [128, 512], F32) for i in range(8)]
        self.coff = SB_BASE
        self.ident = self.ctile("ident", [128, 128], BF16)
        self.c15 = self.ctile("c15", [128, 8], F32)
        self.c0b = self.ctile("c0b", [128, L * 8], F32)
        self.zero = self.ctile("zero", [128, 1], F32)
        self.negl = self.ctile("negl", [128, L], F32)
        self.gsub = self.ctile("gsub", [128, L, 128], F32)
        self.cT = [self.ctile(f"cT{g}", [128, KC, 128], BF16) for g in range(2)]
        self.dummy = self.ctile("dummy", [128, 8], F32)
        assert self.coff <= WB0, self.coff
        self.wb = []
        for i in range(NWB):
            t = self.nc.alloc_sbuf_tensor_at(f"wb{i}", [128, 16, 512], BF16, offset=WB0 + i * WB_BYTES)
            self.wb.append(Tl(t, [f"wb{i}"]))
        self.aoff = ARENA0

    def _alloc(self, name, shape, dt, off):
        self.uid += 1
        return self.nc.alloc_sbuf_tensor_at(f"{name}_{self.uid}", list(shape), dt, offset=off)

    @staticmethod
    def _nbytes(shape, dt):
        n = 1
        for s in shape[1:]:
            n *= s
        return n * (2 if dt == BF16 else 4)

    def ctile(self, name, shape, dt):
        nb = (self._nbytes(shape, dt) + 31) // 32 * 32
        t = self._alloc(name, shape, dt, self.coff)
        self.coff += nb
        return Tl(t, [name])

    def atile(self, name, shape, dt):
        nb = (self._nbytes(shape, dt) + 31) // 32 * 32
        assert self.aoff + nb <= SBUF_LIMIT, (name, self.aoff, nb)
        t = self._alloc(name, shape, dt, self.aoff)
        self.aoff += nb
        self.uid += 1
        return Tl(t, [f"{name}#{self.uid}", "ARENA"])

    def atiles(self, name, n, shape, dt):
        return [self.atile(f"{name}{i}", shape, dt) for i in range(n)]

    def nxt(self, lst):
        k = id(lst)
        i = self.rot.get(k, 0)
        self.rot[k] = i + 1
        return lst[i % len(lst)]

    def phase(self):
        f = self.f
        d = self.dummy
        f.op("dve", lambda e: e.memset(d.t[:, 0:1], 0.0), reads=[], writes=["ARENA"] + d.r, barrier=True)
        self.aoff = ARENA0

    def mm(self, out, lhsT, rhs, start, stop, reads, writes):
        self.f.pe(lambda e: e.matmul(out, lhsT, rhs, start=start, stop=stop), reads, writes)

    def tr(self, out, in_, reads, writes):
        idn = self.ident
        npart = in_.shape[0]
        self.f.pe(lambda e: e.transpose(out, in_, idn.t[0:npart, 0:npart]), list(reads) + idn.r, writes)

    def act(self, out, in_, func, reads, writes, bias=0.0, scale=1.0):
        self.f.act(lambda e: e.activation(out=out, in_=in_, func=func, bias=bias, scale=scale), reads, writes)

    def tt(self, eng, out, in0, in1, op, reads, writes):
        self.f.op(eng, lambda e: e.tensor_tensor(out, in0, in1, op), reads, writes)

    def ts(self, eng, out, in0, s1, s2, op0, op1, reads, writes):
        if op1 is None:
            self.f.op(eng, lambda e: e.tensor_scalar(out, in0, s1, None, op0), reads, writes)
        else:
            self.f.op(eng, lambda e: e.tensor_scalar(out, in0, s1, s2, op0, op1), reads, writes)

    def stt(self, eng, out, in0, scalar, in1, op0, op1, reads, writes):
        self.f.op(eng, lambda e: e.scalar_tensor_tensor(out, in0, scalar, in1, op0, op1), reads, writes)

    def cp(self, eng, out, in_, reads, writes):
        if eng == "act":
            self.f.act(lambda e: e.copy(out, in_), reads, writes)
        else:
            self.f.op(eng, lambda e: e.tensor_copy(out, in_), reads, writes)

    def psr(self, i):
        return [f"ps{i}"]

    def load_w(self, W, k0, nk, c0, ncols):
        wb = self.wb[self.wcnt % NWB]
        self.wcnt += 1
        src = W[k0 * 128:(k0 + nk) * 128, c0:c0 + ncols].rearrange("(kc p) n -> p kc n", p=128)
        self.f.dma("pool", wb.t[:, 0:nk, 0:ncols], src, reads=[], writes=wb.r)
        return wb

    def convert_weights(self, l):
        for name in ("w_oa", "w_ob", "w_out", "w1", "w3", "w2"):
            src, dst = self.wsc[name]
            ncol = src.shape[2]
            for c in range(ncol // 512):
                self.f.dma("pool", dst[l, :, c * 512:(c + 1) * 512], src[l, :, c * 512:(c + 1) * 512],
                           reads=[], writes=[(name + "_b", l, c)])

    def load_wb(self, name, l, k0, nk, c0, ncols):
        wb = self.wb[self.wcnt % NWB]
        self.wcnt += 1
        W = self.wsc[name][1][l]
        src = W[k0 * 128:(k0 + nk) * 128, c0:c0 + ncols].rearrange("(kc p) n -> p kc n", p=128)
        self.f.dma("pool", wb.t[:, 0:nk, 0:ncols], src, reads=[(name + "_b", l, c0 // 512)], writes=wb.r)
        return wb

    def grp(self, blk):
        return 0 if blk < self.NPB else 1

    def x_src(self, l, blk):
        if l == 0:
            if blk < self.NPB:
                return self.xp[blk * 128:(blk + 1) * 128, :], []
            return self.xs[:, :], []
        return self.x2[l - 1, blk * 128:(blk + 1) * 128, :], [("x2", l - 1, blk, c) for c in range(4)]

    def supertiles(self, T):
        out = []
        nb = T // 128
        b = 0
        while b < self.NPB:
            n = min(nb, self.NPB - b)
            out.append((b, n))
            b += n
        out.append((self.NPB, 1))
        return out

    def ln_stats(self, xt, sm):
        st, mv, rs, nm = sm["st"], sm["mv"], sm["rs"], sm["nm"]
        for c in range(4):
            self.f.dve(lambda e, c=c: e.bn_stats(st.t[:, c, :], xt.t[:, c * 512:(c + 1) * 512]), xt.r, st.r)
        self.f.dve(lambda e: e.bn_aggr(mv.t[:, :], st.t[:, :, :]), st.r, mv.r)
        self.ts("dve", rs.t[:, :], mv.t[:, 1:2], LN_EPS, None, ALU.add, None, mv.r, rs.r)
        self.act(rs.t[:, :], rs.t[:, :], AF.Sqrt, rs.r, rs.r)
        self.f.dve(lambda e: e.reciprocal(rs.t[:, :], rs.t[:, :]), rs.r, rs.r)
        self.stt("dve", nm.t[:, :], mv.t[:, 0:1], -1.0, rs.t[:, :], ALU.mult, ALU.mult, mv.r + rs.r, nm.r)
        return rs, nm

    def small_set(self, name):
        return {"st": self.atile(name + "st", [128, 4, 6], F32), "mv": self.atile(name + "mv", [128, 2], F32),
                "rs": self.atile(name + "rs", [128, 1], F32), "nm": self.atile(name + "nm", [128, 1], F32)}

    def tok2feat(self, src, src_ap_fn, n, dst_fn, dst_r):
        c = 0
        while c < n:
            m = min(8, n - c)
            bank = 6 + (self.rot.setdefault("trb", 0) % 2)
            self.rot["trb"] += 1
            pt = self.ps[bank][:, :].bitcast(BF16).rearrange("p (a b) -> p a b", b=128)
            for j in range(m):
                a = src_ap_fn(c + j)
                npart = a.shape[0]
                self.tr(pt[:, j, 0:npart], a, src.r, self.psr(bank))
            npart = src_ap_fn(c).shape[0]
            eng = "act" if (self.rot["trb"] % 2) else "dve"
            self.cp(eng, dst_fn(c, m), pt[:, 0:m, 0:npart], self.psr(bank), dst_r)
            c += m

    def setup(self):
        f = self.f
        L = DEPTH
        self.phase()
        idf = self.atile("idf", [128, 128], F32)
        f.dma("sp", idf.t[:, :], self.identd[:, :], writes=idf.r)
        self.cp("dve", self.ident.t[:, :], idf.t[:, :], idf.r, self.ident.r)
        f.dma("sp", self.c15.t[:, :], self.c15A[:, :], writes=self.c15.r)
        for l in range(L):
            f.dma("sp", self.c0b.t[:, l * 8:(l + 1) * 8], self.c0B[l], writes=self.c0b.r)
        f.dve(lambda e: e.memset(self.zero.t[:, :], 0.0), [], self.zero.r)
        lv = self.atile("lv", [128, 4, 64], F32)
        pr = self.atile("pr", [128, 2, 64], F32)
        sm2 = self.atile("sm2", [128, 2], F32)
        ex2 = self.atile("ex2", [128, 2], F32)
        sg = self.atile("sg", [128, 128], F32)
        for l in range(L):
            lam_init = 0.8 - 0.6 * math.exp(-0.3 * l)
            for j in range(4):
                f.dma("sp", lv.t[:, j, :], self.lamv[l, j], writes=lv.r)
            self.tt("dve", pr.t[:, 0, :], lv.t[:, 0, :], lv.t[:, 1, :], ALU.mult, lv.r, pr.r)
            self.tt("dve", pr.t[:, 1, :], lv.t[:, 2, :], lv.t[:, 3, :], ALU.mult, lv.r, pr.r)
            f.dve(lambda e: e.reduce_sum(sm2.t[:, :], pr.t[:, :, :], axis=AX.X), pr.r, sm2.r)
            self.act(ex2.t[:, :], sm2.t[:, :], AF.Exp, sm2.r, ex2.r)
            self.tt("dve", sm2.t[:, 0:1], ex2.t[:, 0:1], ex2.t[:, 1:2], ALU.subtract, ex2.r, sm2.r)
            self.ts("dve", self.negl.t[:, l:l + 1], sm2.t[:, 0:1], lam_init, -1.0, ALU.add, ALU.mult,
                    sm2.r, self.negl.r)
            f.dma("sp", sg.t[:, :], self.subln[l], writes=sg.r)
            self.ts("dve", self.gsub.t[:, l, :], sg.t[:, :], 1.0 - lam_init, None, ALU.mult, None,
                    sg.r, self.gsub.r)
        crow = self.atiles("crow", 2, [128, D], F32)
        cbf = self.atiles("cbf", 2, [128, D], BF16)
        for g in range(2):
            f.dma("sp", crow[g].t[:, :], self.cbc[g], writes=crow[g].r)
            self.act(cbf[g].t[:, :], crow[g].t[:, :], AF.Silu, crow[g].r, cbf[g].r)
            cb, ct = cbf[g], self.cT[g]
            self.tok2feat(cb, lambda c, cb=cb: cb.t[:, c * 128:(c + 1) * 128], KC,
                          lambda c0, m, ct=ct: ct.t[:, c0:c0 + m, :], ct.r)
        bt = self.atiles("bt", 2, [128, 512], F32)
        ms = self.atiles("ms", 3, [128, 512], F32)
        for l in range(L):
            for n in range(24):
                wt = self.load_w(self.w_mod[l], 0, KC, n * 512, 512)
                b = self.nxt(bt)
                f.dma("sp", b.t[:, :], self.bmod[l, :, n * 512:(n + 1) * 512], writes=b.r)
                piece = n // 4
                for g in range(2):
                    bank = self.rot.setdefault("gb", 0) % 4
                    self.rot["gb"] += 1
                    ps = self.ps[bank]
                    for kc in range(KC):
                        self.mm(ps[:, :], self.cT[g].t[:, kc, :], wt.t[:, kc, :], kc == 0, kc == KC - 1,
                                self.cT[g].r + wt.r, self.psr(bank))
                    m = self.nxt(ms)
                    if piece in (1, 4):
                        self.stt("dve", m.t[:, :], ps[:, :], 1.0, b.t[:, :], ALU.add, ALU.add,
                                 self.psr(bank) + b.r, m.r)
                    else:
                        self.tt("dve", m.t[:, :], ps[:, :], b.t[:, :], ALU.add, self.psr(bank) + b.r, m.r)
                    f.dma(SQ, self.modbc[l, g, :, n * 512:(n + 1) * 512], m.t[:, :], reads=m.r,
                          writes=[("modbc", l, g, n)])

    def mod_piece(self, l, g, piece, c4):
        n = piece * 4 + c4
        return self.modbc[l, g, :, n * 512:(n + 1) * 512], [("modbc", l, g, n)]

    def ln_mod_T(self, l, blk, xt, sm, pieces, tmp, hb, piece_sc, piece_sh, dst_fn, dst_r):
        f = self.f
        g = self.grp(blk)
        rs, nm = self.ln_stats(xt, sm)
        for c4 in range(4):
            cs = slice(c4 * 512, (c4 + 1) * 512)
            psc, psh = self.nxt(pieces), self.nxt(pieces)
            a, r = self.mod_piece(l, g, piece_sc, c4)
            f.dma("sp", psc.t[:, :], a, reads=r, writes=psc.r)
            a, r = self.mod_piece(l, g, piece_sh, c4)
            f.dma("sp", psh.t[:, :], a, reads=r, writes=psh.r)
            t = self.nxt(tmp)
            self.act(t.t[:, :], xt.t[:, cs], AF.Identity, xt.r + rs.r + nm.r, t.r,
                     bias=nm.t[:, 0:1], scale=rs.t[:, 0:1])
            self.tt("dve", t.t[:, :], t.t[:, :], psc.t[:, :], ALU.mult, t.r + psc.r, t.r)
            self.tt("dve", hb.t[:, cs], t.t[:, :], psh.t[:, :], ALU.add, t.r + psh.r, hb.r)
        self.tok2feat(hb, lambda c: hb.t[:, c * 128:(c + 1) * 128], KC, dst_fn, dst_r)

    def phase_A(self, l):
        f = self.f
        segs = [("qa", 0, 1024), ("ka", 1024, 2048), ("va", 2048, 3072), ("qb", 3072, 4096),
                ("kb", 4096, 5120), ("vb", 5120, 6144), ("ga", 6144, 8192), ("gb", 8192, 10240)]
        fm_dst = {"qa": self.qaT, "qb": self.qbT, "ga": self.gaT, "gb": self.gbT}
        self.phase()
        hTs = self.atiles("hT", 2, [128, KC, self.TA], BF16)
        xrow = self.atiles("xrow", 2, [128, D], F32)
        hbs = self.atiles("hb", 2, [128, D], BF16)
        pieces = self.atiles("pc", 4, [128, 512], F32)
        tmp = self.atiles("tmp", 2, [128, 512], F32)
        sms = [self.small_set(f"sm{i}") for i in range(2)]
        st32 = self.atiles("st32", 3, [128, 512], F32)
        st16 = self.atiles("st16", 4, [128, 512], BF16)
        kst = self.atiles("kst", 2, [128, 4, 128], BF16)
        for (b0, nb) in self.supertiles(self.TA):
            T = nb * 128
            hT = self.nxt(hTs)
            for tb in range(nb):
                blk = b0 + tb
                xt = self.nxt(xrow)
                a, r = self.x_src(l, blk)
                f.dma("sp", xt.t[:, :], a, reads=r, writes=xt.r)
                self.ln_mod_T(l, blk, xt, self.nxt(sms), pieces, tmp, self.nxt(hbs), 1, 0,
                              lambda c0, m, tb=tb: hT.t[:, c0:c0 + m, tb * 128:(tb + 1) * 128], hT.r)
            for ctile in range(INW // 512):
                c0 = ctile * 512
                name, s0, s1 = [s for s in segs if s[1] <= c0 < s[2]][0]
                wt = self.load_w(self.w_in[l], 0, KC, c0, 512)
                if name in fm_dst:
                    dst = fm_dst[name]
                    for sub in range(4):
                        row0 = c0 - s0 + sub * 128
                        rc = row0 // 128
                        for t0 in range(0, T, 512):
                            N = min(512, T - t0)
                            bank = self.rot.setdefault("gb", 0) % 4
                            self.rot["gb"] += 1
                            ps = self.ps[bank]
                            for kc in range(KC):
                                self.mm(ps[:, 0:N], wt.t[:, kc, sub * 128:(sub + 1) * 128], hT.t[:, kc, t0:t0 + N],
                                        kc == 0, kc == KC - 1, wt.r + hT.r, self.psr(bank))
                            s = self.nxt(st16)
                            if name in ("ga", "gb"):
                                self.act(s.t[:, 0:N], ps[:, 0:N], AF.Sigmoid, self.psr(bank), s.r)
                            else:
                                self.cp("dve", s.t[:, 0:N], ps[:, 0:N], self.psr(bank), s.r)
                            tok0 = b0 * 128 + t0
                            res = [(name + "T", l, rc, tok0 // 128 + i) for i in range(N // 128)]
                            f.dma(SQ, dst[l, row0:row0 + 128, tok0:tok0 + N], s.t[:, 0:N], reads=s.r, writes=res)
                else:
                    cc = (c0 - s0) // 512
                    for tb in range(nb):
                        blk = b0 + tb
                        bank = self.rot.setdefault("gb", 0) % 4
                        self.rot["gb"] += 1
                        ps = self.ps[bank]
                        for kc in range(KC):
                            self.mm(ps[:, :], hT.t[:, kc, tb * 128:(tb + 1) * 128], wt.t[:, kc, :],
                                    kc == 0, kc == KC - 1, wt.r + hT.r, self.psr(bank))
                        odst = None
                        if blk < self.NPB:
                            if name == "ka":
                                odst = self.akp[l, blk * 128:(blk + 1) * 128, c0 - s0:c0 - s0 + 512]
                            elif name == "va":
                                odst = self.avp[l, blk * 128:(blk + 1) * 128, c0 - s0:c0 - s0 + 512]
                            elif blk * 128 >= self.S - self.BOUT:
                                r0 = blk * 128 - (self.S - self.BOUT)
                                o = self.bkp if name == "kb" else self.bvp
                                odst = o[l, r0:r0 + 128, c0 - s0:c0 - s0 + 512]
                        else:
                            o = {"ka": self.aks, "va": self.avs, "kb": self.bks, "vb": self.bvs}[name]
                            odst = o[l, :, c0 - s0:c0 - s0 + 512]
                        if odst is not None:
                            s32 = self.nxt(st32)
                            self.cp("act", s32.t[:, :], ps[:, :], self.psr(bank), s32.r)
                            f.dma(SQ, odst, s32.t[:, :], reads=s32.r)
                            s = self.nxt(st16)
                            self.cp("dve", s.t[:, :], s32.t[:, :], s32.r, s.r)
                        else:
                            s = self.nxt(st16)
                            self.cp("dve", s.t[:, :], ps[:, :], self.psr(bank), s.r)
                        if name in ("va", "vb"):
                            dst = self.va if name == "va" else self.vb
                            f.dma(SQ, dst[l, blk * 128:(blk + 1) * 128, c0 - s0:c0 - s0 + 512], s.t[:, :],
                                  reads=s.r, writes=[(name, l, blk, cc)])
                        else:
                            k = self.nxt(kst)
                            self.tok2feat(s, lambda c, s=s: s.t[:, c * 128:(c + 1) * 128], 4,
                                          lambda c0_, m, k=k: k.t[:, c0_:c0_ + m, :], k.r)
                            dst = self.kaT if name == "ka" else self.kbT
                            rows = c0 - s0
                            f.dma(SQ, dst[l, rows:rows + 512, blk * 128:(blk + 1) * 128]
                                  .rearrange("(c p) s -> p c s", p=128), k.t[:, :, :], reads=k.r,
                                  writes=[(name + "T", l, rows // 128 + i, blk) for i in range(4)])

    def load_E(self, l, which):
        f = self.f
        nt = 2 if which == "A" else 1
        E = self.atile("E" + which, [128, 8, 2, nt, 128], F32)
        braw = self.atile("braw", [128, 8, 256], F32)
        msk = self.atile("msk", [128, 256], F32)
        src = self.biasA if which == "A" else self.biasB[l]
        f.dma("sp", braw.t[:, :, :], src, writes=braw.r)
        f.dma("sp", msk.t[:, :], self.mask01[:, :], writes=msk.r)
        self.act(braw.t[:, :, :], braw.t[:, :, :], AF.Exp, braw.r, braw.r)
        for h in range(8):
            for t in range(nt):
                self.tt("dve", E.t[:, h, :, t, :], braw.t[:, h, :].rearrange("p (b q) -> p b q", q=128),
                        msk.t[:, :].rearrange("p (b q) -> p b q", q=128), ALU.mult, braw.r + msk.r, E.r)
        return E

    def attn_core(self, which, l, h, qT_ap, q_r, nq, blocks, E, cfar_ap, par, o_dst, o_r, P3, wk):
        f = self.f
        nt = 2 if which == "A" else 1
        scale = 0.125 if which == "A" else 128 ** -0.5
        near = [b for b in blocks if b["kind"] in ("diag", "prev")]
        far = [b for b in blocks if b["kind"] in ("far", "far4")]
        per = 2 if which == "A" else 4
        groups = []
        for i in range(0, len(far), per):
            groups.append(("far", far[i:i + per]))
        if near:
            nks = sorted(set(b["nk"] for b in near))
            for nk in nks:
                groups.append(("near", [b for b in near if b["nk"] == nk]))
        accb = [2 + 2 * par + t for t in range(nt)]
        ntot = len(blocks)
        seen = [0]

        def s_view(bank):
            return self.ps[bank][:, :].rearrange("p (b t q) -> p b t q", t=nt, q=128)

        def do_qk(gi):
            kind, grp = groups[gi]
            bank = gi % 2
            S4 = s_view(bank)
            for bi, b in enumerate(grp):
                nk = b["nk"]
                for t in range(nt):
                    lhsT = b["kT"][:, 0:nk]
                    rhs = qT_ap[:, t, 0:nq]
                    self.mm(S4[0:nk, bi, t, 0:nq], lhsT, rhs, True, True, b["r"] + q_r, self.psr(bank))

        def do_rest(gi):
            kind, grp = groups[gi]
            bank = gi % 2
            S4 = s_view(bank)
            P = self.nxt(P3)
            P4 = P.t[:, :].rearrange("p (b t q) -> p b t q", t=nt, q=128)
            nk = grp[0]["nk"]
            nbk = len(grp)
            if kind == "far":
                self.act(P4[0:nk, 0:nbk, :, 0:nq], S4[0:nk, 0:nbk, :, 0:nq], AF.Exp, self.psr(bank), P.r,
                         bias=cfar_ap[0:nk, :], scale=scale)
                for bi, b in enumerate(grp):
                    if b["kind"] == "far4" and nq > 64:
                        f.op("dve", lambda e, bi=bi: e.memset(P4[0:64, bi, :, 64:nq], 0.0), P.r, P.r)
            else:
                self.act(P4[0:nk, 0:nbk, :, 0:nq], S4[0:nk, 0:nbk, :, 0:nq], AF.Exp, self.psr(bank), P.r,
                         bias=self.zero.t[0:nk, :], scale=scale)
                for bi, b in enumerate(grp):
                    eb = 0 if b["kind"] == "diag" else 1
                    self.tt("dve", P4[0:nk, bi, :, 0:nq], P4[0:nk, bi, :, 0:nq], E.t[0:nk, h, eb, :, 0:nq],
                            ALU.mult, P.r + E.r, P.r)
            for bi, b in enumerate(grp):
                nk = b["nk"]
                for t in range(nt):
                    self.mm(self.ps[accb[t]][0:nq, 0:129], P4[0:nk, bi, t, 0:nq], b["v"][0:nk, 0:129],
                            seen[0] == 0, seen[0] == ntot - 1, P.r + b["r"], self.psr(accb[t]))
                seen[0] += 1

        ng = len(groups)
        do_qk(0)
        for gi in range(ng):
            if gi + 1 < ng:
                do_qk(gi + 1)
            do_rest(gi)
        rz, t1, o, ss, on = wk["rz"], wk["t1"], wk["o"], wk["ss"], wk["on"]
        accr = [r for t in range(nt) for r in self.psr(accb[t])]
        for t in range(nt):
            f.dve(lambda e, t=t: e.reciprocal(rz.t[0:nq, t:t + 1], self.ps[accb[t]][0:nq, 128:129]), accr, rz.r)
        if which == "A":
            self.ts("dve", t1.t[0:nq, :], self.ps[accb[1]][0:nq, 0:128], rz.t[0:nq, 1:2], self.negl.t[0:nq, l:l + 1],
                    ALU.mult, ALU.mult, accr + rz.r + self.negl.r, t1.r)
            self.stt("dve", o.t[0:nq, :], self.ps[accb[0]][0:nq, 0:128], rz.t[0:nq, 0:1], t1.t[0:nq, :],
                     ALU.mult, ALU.add, accr + rz.r + t1.r, o.r)
            self.tt("dve", t1.t[0:nq, :], o.t[0:nq, :], o.t[0:nq, :], ALU.mult, o.r + t1.r, t1.r)
            f.dve(lambda e: e.reduce_sum(ss.t[0:nq, 0:1], t1.t[0:nq, :], axis=AX.X), t1.r + ss.r, ss.r)
            self.ts("dve", ss.t[0:nq, 1:2], ss.t[0:nq, 0:1], 1.0 / 128, LN_EPS, ALU.mult, ALU.add, ss.r, ss.r)
            self.act(ss.t[0:nq, 1:2], ss.t[0:nq, 1:2], AF.Sqrt, ss.r, ss.r)
            f.dve(lambda e: e.reciprocal(ss.t[0:nq, 1:2], ss.t[0:nq, 1:2]), ss.r, ss.r)
            self.stt("dve", on.t[0:nq, :], o.t[0:nq, :], ss.t[0:nq, 1:2], self.gsub.t[0:nq, l, :],
                     ALU.mult, ALU.mult, o.r + ss.r + self.gsub.r, on.r)
        else:
            self.ts("dve", on.t[0:nq, :], self.ps[accb[0]][0:nq, 0:128], rz.t[0:nq, 0:1], None, ALU.mult, None,
                    accr + rz.r, on.r)
        bank = 6 + (self.rot.setdefault("trb", 0) % 2)
        self.rot["trb"] += 1
        pt = self.ps[bank][:, :].bitcast(BF16)
        self.tr(pt[:, 0:nq], on.t[0:nq, :], on.r, self.psr(bank))
        self.cp("act", o_dst, pt[:, 0:nq], self.psr(bank), o_r)

    def attn_work(self):
        wk = []
        for i in range(2):
            wk.append({"rz": self.atile("rz", [128, 2], F32), "t1": self.atile("t1", [128, 128], F32),
                       "o": self.atile("o", [128, 128], F32), "ss": self.atile("ss", [128, 2], F32),
                       "on": self.atile("on", [128, 128], BF16)})
        return wk

    def phase_attn_prompt(self, l, which):
        f = self.f
        NPB = self.NPB
        S = self.S
        A = which == "A"
        kTd, vd, qTd, oTd = (self.kaT, self.va, self.qaT, self.oaT) if A else (self.kbT, self.vb, self.qbT, self.obT)
        kn, vn, qn, on_ = ("kaT", "va", "qaT", "oaT") if A else ("kbT", "vb", "qbT", "obT")
        for hh in range(2):
            self.phase()
            E = self.load_E(l, which)
            kT = self.atile("kT", [128, 4, S], BF16)
            V1 = self.atile("V1", [128, NPB, 4, 130], BF16)
            f.op("dve", lambda e, V1=V1: e.memset(V1.t[:, :, :, 128:129], 1.0), V1.r, V1.r)
            nt = 2 if A else 1
            qTs = self.atiles("qT", 2, [128, 4, nt, 128], BF16)
            if A:
                for q_ in qTs:
                    f.op("dve", lambda e, q_=q_: e.memset(q_.t[64:128, :, 0, :], 0.0), q_.r, q_.r)
                    f.op("dve", lambda e, q_=q_: e.memset(q_.t[0:64, :, 1, :], 0.0), q_.r, q_.r)
            oTs = self.atiles("oT", 2, [128, 4, 128], BF16)
            P3 = self.atiles("P", 3, [128, 512], BF16)
            wk = self.attn_work()
            kres = [[f"kTb{hh}_{i}"] for i in range(NPB)]
            vres = [[f"vTb{hh}_{i}"] for i in range(NPB)]
            cnt = 0
            for i in range(NPB):
                f.dma("sp", kT.t[:, :, i * 128:(i + 1) * 128],
                      kTd[l, hh * 512:(hh + 1) * 512, i * 128:(i + 1) * 128].rearrange("(h p) s -> p h s", p=128),
                      reads=[(kn, l, hh * 4 + c, i) for c in range(4)] + ["ARENA"], writes=kres[i])
                f.dma("sp", V1.t[:, i, :, 0:128],
                      vd[l, i * 128:(i + 1) * 128, hh * 512:(hh + 1) * 512].rearrange("s (h d) -> s h d", d=128),
                      reads=[(vn, l, i, hh), "ARENA"], writes=vres[i])
                qT = self.nxt(qTs)
                qsrc = qTd[l, hh * 512:(hh + 1) * 512, i * 128:(i + 1) * 128].rearrange("(h p) s -> p h s", p=128)
                qrd = [(qn, l, hh * 4 + c, i) for c in range(4)]
                if A:
                    f.dma("sp", qT.t[0:64, :, 0, :], qsrc[0:64], reads=qrd, writes=qT.r)
                    f.dma("sp", qT.t[64:128, :, 1, :], qsrc[64:128], reads=qrd, writes=qT.r)
                else:
                    f.dma("sp", qT.t[:, :, 0, :], qsrc, reads=qrd, writes=qT.r)
                oT = self.nxt(oTs)
                for h4 in range(4):
                    h = hh * 4 + h4
                    blocks = []
                    jlo = 0 if A else max(0, i - 4)
                    for j in range(jlo, i + 1):
                        if j == i:
                            kind = "diag"
                        elif j == i - 1:
                            kind = "prev"
                        elif (not A) and j == i - 4:
                            kind = "far4"
                        else:
                            kind = "far"
                        blocks.append({"kT": kT.t[:, h4, j * 128:(j + 1) * 128], "v": V1.t[:, j, h4, :],
                                       "nk": 128, "kind": kind, "r": kres[j] + vres[j] + V1.r})
                    cfar = self.c15.t[:, h:h + 1] if A else self.c0b.t[:, l * 8 + h:l * 8 + h + 1]
                    self.attn_core(which, l, h, qT.t[:, h4, :, :], qT.r, 128, blocks, E, cfar, cnt % 2,
                                   oT.t[:, h4, :], oT.r, P3, wk[cnt % 2])
                    cnt += 1
                f.dma(SQ, oTd[l, hh * 512:(hh + 1) * 512, i * 128:(i + 1) * 128].rearrange("(h p) s -> p h s", p=128),
                      oT.t[:, :, :], reads=oT.r, writes=[(on_, l, hh * 4 + c, i) for c in range(4)])

    def phase_attn_sample(self, l, which):
        f = self.f
        A = which == "A"
        NPB, TS = self.NPB, self.TS
        past = self.PAST if A else self.BPAST
        nkb = past // 128
        kTd, vd, qTd, oTd = (self.kaT, self.va, self.qaT, self.oaT) if A else (self.kbT, self.vb, self.qbT, self.obT)
        kn, vn, qn, on_ = ("kaT", "va", "qaT", "oaT") if A else ("kbT", "vb", "qbT", "obT")
        ck, cv = (self.cak, self.cav) if A else (self.cbk, self.cbv)
        self.phase()
        E = self.load_E(l, which)
        kTs = self.atiles("kTs", 2, [128, 8, past + TS], BF16)
        V1s = self.atiles("V1s", 2, [128, nkb + 1, 8, 130], BF16)
        for v in V1s:
            f.op("dve", lambda e, v=v: e.memset(v.t[:, :, :, 128:129], 1.0), v.r, v.r)
        craw = self.atiles("craw", 2, [128, 1024], BF16)
        nt = 2 if A else 1
        qT = self.atile("qTs", [128, 8, nt, 128], BF16)
        if A:
            f.op("dve", lambda e: e.memset(qT.t[64:128, :, 0, :], 0.0), qT.r, qT.r)
            f.op("dve", lambda e: e.memset(qT.t[0:64, :, 1, :], 0.0), qT.r, qT.r)
        oT = self.atile("oTs", [128, 8, 128], BF16)
        P3 = self.atiles("P", 3, [128, 512], BF16)
        wk = self.attn_work()
        tokc = NPB * 128
        qsrc = qTd[l, :, tokc:tokc + 128].rearrange("(h p) s -> p h s", p=128)
        qrd = [(qn, l, c, NPB) for c in range(8)]
        if A:
            f.dma("sp", qT.t[0:64, :, 0, :], qsrc[0:64], reads=qrd, writes=qT.r)
            f.dma("sp", qT.t[64:128, :, 1, :], qsrc[64:128], reads=qrd, writes=qT.r)
        else:
            f.dma("sp", qT.t[:, :, 0, :], qsrc, reads=qrd, writes=qT.r)
        cnt = 0
        for sb in range(self.NSB):
            kT, V1 = self.nxt(kTs), self.nxt(V1s)
            for kb in range(nkb):
                cr = self.nxt(craw)
                f.dma("pool", cr.t[:, :], ck[l, sb, kb * 128:(kb + 1) * 128, :], writes=cr.r)
                self.tok2feat(cr, lambda c, cr=cr: cr.t[:, c * 128:(c + 1) * 128], 8,
                              lambda c0, m, kT=kT, kb=kb: kT.t[:, c0:c0 + m, kb * 128:(kb + 1) * 128], kT.r)
                f.dma("pool", V1.t[:, kb, :, 0:128],
                      cv[l, sb, kb * 128:(kb + 1) * 128, :].rearrange("s (h d) -> s h d", d=128), writes=V1.r)
            f.dma("sp", kT.t[:, :, past:past + TS],
                  kTd[l, :, tokc + sb * TS:tokc + (sb + 1) * TS].rearrange("(h p) s -> p h s", p=128),
                  reads=[(kn, l, c, NPB) for c in range(8)], writes=kT.r)
            f.dma("sp", V1.t[0:TS, nkb, :, 0:128],
                  vd[l, tokc + sb * TS:tokc + (sb + 1) * TS, :].rearrange("s (h d) -> s h d", d=128),
                  reads=[(vn, l, NPB, c) for c in range(2)], writes=V1.r)
            for h in range(8):
                blocks = []
                for j in range(nkb + 1):
                    if j == nkb:
                        kind, nk = "diag", TS
                    elif j == nkb - 1:
                        kind, nk = "prev", 128
                    else:
                        kind, nk = "far", 128
                    blocks.append({"kT": kT.t[:, h, j * 128:j * 128 + nk], "v": V1.t[:, j, h, :], "nk": nk,
                                   "kind": kind, "r": kT.r[:1] + V1.r})
                cfar = self.c15.t[:, h:h + 1] if A else self.c0b.t[:, l * 8 + h:l * 8 + h + 1]
                self.attn_core(which, l, h, qT.t[:, h, :, sb * TS:(sb + 1) * TS], qT.r, TS, blocks, E, cfar, cnt % 2,
                               oT.t[:, h, sb * TS:(sb + 1) * TS], oT.r, P3, wk[cnt % 2])
                cnt += 1
        f.dma(SQ, oTd[l, :, tokc:tokc + 128].rearrange("(h p) s -> p h s", p=128), oT.t[:, :, :],
              reads=oT.r, writes=[(on_, l, c, NPB) for c in range(8)])

    def ln_affine_store(self, xt, sm, gsrc, bsrc, pieces, dst_fn):
        f = self.f
        rs, nm = self.ln_stats(xt, sm)
        for c4 in range(4):
            cs = slice(c4 * 512, (c4 + 1) * 512)
            pg, pb = self.nxt(pieces), self.nxt(pieces)
            f.dma("sp", pg.t[:, :], gsrc[:, cs], writes=pg.r)
            f.dma("sp", pb.t[:, :], bsrc[:, cs], writes=pb.r)
            self.act(xt.t[:, cs], xt.t[:, cs], AF.Identity, xt.r + rs.r + nm.r, xt.r,
                     bias=nm.t[:, 0:1], scale=rs.t[:, 0:1])
            self.tt("dve", xt.t[:, cs], xt.t[:, cs], pg.t[:, :], ALU.mult, xt.r + pg.r, xt.r)
            self.tt("dve", xt.t[:, cs], xt.t[:, cs], pb.t[:, :], ALU.add, xt.r + pb.r, xt.r)
        for (dst, wres) in dst_fn():
            f.dma(SQ, dst, xt.t[:, :], reads=xt.r, writes=wres)

    def phase_C(self, l):
        f = self.f
        self.phase()
        TM = self.TCD
        oaTs = self.atiles("oaTt", 2, [128, 8, TM], BF16)
        obTs = self.atiles("obTt", 2, [128, 8, TM], BF16)
        mTs = self.atiles("mT", 2, [128, KC, TM], BF16)
        gts = self.atiles("gt", 4, [128, TM], BF16)
        tmps = self.atiles("tmpc", 4, [128, TM], F32)
        y1all = self.atiles("y1", TM // 128 + 2, [128, D], F32)
        pieces = self.atiles("pcc", 6, [128, 512], F32)
        xrs = self.atiles("xr", 3, [128, 512], F32)
        t1s = self.atiles("t1c", 2, [128, 512], F32)
        sms = [self.small_set(f"smc{i}") for i in range(2)]
        for (b0, nb) in self.supertiles(self.TCD):
            T = nb * 128
            tok0 = b0 * 128
            g = self.grp(b0)
            oaT, obT, mT = self.nxt(oaTs), self.nxt(obTs), self.nxt(mTs)
            y1 = [self.nxt(y1all) for _ in range(nb)]
            blks = list(range(b0, b0 + nb))
            f.dma("sp", oaT.t[:, :, 0:T], self.oaT[l, :, tok0:tok0 + T].rearrange("(kc p) s -> p kc s", p=128),
                  reads=[("oaT", l, c, b) for c in range(8) for b in blks], writes=oaT.r)
            f.dma("sp", obT.t[:, :, 0:T], self.obT[l, :, tok0:tok0 + T].rearrange("(kc p) s -> p kc s", p=128),
                  reads=[("obT", l, c, b) for c in range(8) for b in blks], writes=obT.r)
            for ct4 in range(4):
                wa = self.load_wb("w_oa", l, 0, 8, ct4 * 512, 512)
                wb_ = self.load_wb("w_ob", l, 0, 8, ct4 * 512, 512)
                for sub in range(4):
                    ct = ct4 * 4 + sub
                    gA, gB = self.nxt(gts), self.nxt(gts)
                    f.dma("sp", gA.t[:, 0:T], self.gaT[l, ct * 128:(ct + 1) * 128, tok0:tok0 + T],
                          reads=[("gaT", l, ct, b) for b in blks], writes=gA.r)
                    f.dma("sp", gB.t[:, 0:T], self.gbT[l, ct * 128:(ct + 1) * 128, tok0:tok0 + T],
                          reads=[("gbT", l, ct, b) for b in blks], writes=gB.r)
                    bA = self.rot.setdefault("gb", 0) % 4
                    bB = (bA + 1) % 4
                    self.rot["gb"] += 2
                    for kc in range(8):
                        self.mm(self.ps[bA][:, 0:T], wa.t[:, kc, sub * 128:(sub + 1) * 128], oaT.t[:, kc, 0:T],
                                kc == 0, kc == 7, wa.r + oaT.r, self.psr(bA))
                    for kc in range(8):
                        self.mm(self.ps[bB][:, 0:T], wb_.t[:, kc, sub * 128:(sub + 1) * 128], obT.t[:, kc, 0:T],
                                kc == 0, kc == 7, wb_.r + obT.r, self.psr(bB))
                    ta, tb_ = self.nxt(tmps), self.nxt(tmps)
                    self.tt("dve", ta.t[:, 0:T], self.ps[bA][:, 0:T], gA.t[:, 0:T], ALU.mult, self.psr(bA) + gA.r, ta.r)
                    self.tt("dve", tb_.t[:, 0:T], self.ps[bB][:, 0:T], gB.t[:, 0:T], ALU.mult, self.psr(bB) + gB.r, tb_.r)
                    self.tt("dve", mT.t[:, ct, 0:T], ta.t[:, 0:T], tb_.t[:, 0:T], ALU.add, ta.r + tb_.r, mT.r)
            for c4 in range(4):
                cs = slice(c4 * 512, (c4 + 1) * 512)
                wo = self.load_wb("w_out", l, 0, KC, c4 * 512, 512)
                gm = self.nxt(pieces)
                a, r = self.mod_piece(l, g, 2, c4)
                f.dma("sp", gm.t[:, :], a, reads=r, writes=gm.r)
                for tb in range(nb):
                    blk = b0 + tb
                    xr = self.nxt(xrs)
                    a, r = self.x_src(l, blk)
                    f.dma("sp", xr.t[:, :], a[:, cs], reads=r, writes=xr.r)
                    bank = self.rot.setdefault("gb", 0) % 4
                    self.rot["gb"] += 1
                    for kc in range(KC):
                        self.mm(self.ps[bank][:, :], mT.t[:, kc, tb * 128:(tb + 1) * 128], wo.t[:, kc, :],
                                kc == 0, kc == KC - 1, mT.r + wo.r, self.psr(bank))
                    t1 = self.nxt(t1s)
                    self.tt("dve", t1.t[:, :], self.ps[bank][:, :], gm.t[:, :], ALU.mult, self.psr(bank) + gm.r, t1.r)
                    self.ts("dve", xr.t[:, :], xr.t[:, :], ALPHA, None, ALU.mult, None, xr.r, xr.r)
                    self.tt("dve", y1[tb].t[:, cs], xr.t[:, :], t1.t[:, :], ALU.add, xr.r + t1.r, y1[tb].r)
            for tb in range(nb):
                blk = b0 + tb
                self.ln_affine_store(y1[tb], self.nxt(sms), self.ln1g[l], self.ln1b[l], pieces,
                                     lambda blk=blk: [(self.x1[l, blk * 128:(blk + 1) * 128, :],
                                                       [("x1", l, blk, c) for c in range(4)])])

    def phase_D(self, l):
        f = self.f
        last = l == DEPTH - 1
        self.phase()
        TM = self.TCD
        x1all = self.atiles("x1t", TM // 128 + 2, [128, D], F32)
        h2T = self.atile("h2T", [128, KC, TM], BF16)
        uT = self.atile("uT", [128, FC, TM], BF16)
        hbs = self.atiles("hbd", 2, [128, D], BF16)
        pieces = self.atiles("pcd", 6, [128, 512], F32)
        tmp = self.atiles("tmpd", 2, [128, 512], F32)
        s1s = self.atiles("s1", 2, [128, TM], F32)
        t1s = self.atiles("t1d", 2, [128, 512], F32)
        sms = [self.small_set(f"smd{i}") for i in range(2)]
        for (b0, nb) in self.supertiles(self.TCD):
            T = nb * 128
            g = self.grp(b0)
            x1t = [self.nxt(x1all) for _ in range(nb)]
            for tb in range(nb):
                blk = b0 + tb
                f.dma("sp", x1t[tb].t[:, :], self.x1[l, blk * 128:(blk + 1) * 128, :],
                      reads=[("x1", l, blk, c) for c in range(4)], writes=x1t[tb].r)
                self.ln_mod_T(l, blk, x1t[tb], self.nxt(sms), pieces, tmp, self.nxt(hbs), 4, 3,
                              lambda c0, m, tb=tb: h2T.t[:, c0:c0 + m, tb * 128:(tb + 1) * 128], h2T.r)
            for f4 in range(DFF // 512):
                w1t = self.load_wb("w1", l, 0, KC, f4 * 512, 512)
                w3t = self.load_wb("w3", l, 0, KC, f4 * 512, 512)
                for sub in range(4):
                    fc = f4 * 4 + sub
                    b1 = self.rot.setdefault("gb", 0) % 4
                    b3 = (b1 + 1) % 4
                    self.rot["gb"] += 2
                    for kc in range(KC):
                        self.mm(self.ps[b1][:, 0:T], w1t.t[:, kc, sub * 128:(sub + 1) * 128], h2T.t[:, kc, 0:T],
                                kc == 0, kc == KC - 1, w1t.r + h2T.r, self.psr(b1))
                    for kc in range(KC):
                        self.mm(self.ps[b3][:, 0:T], w3t.t[:, kc, sub * 128:(sub + 1) * 128], h2T.t[:, kc, 0:T],
                                kc == 0, kc == KC - 1, w3t.r + h2T.r, self.psr(b3))
                    s1 = self.nxt(s1s)
                    self.act(s1.t[:, 0:T], self.ps[b1][:, 0:T], AF.Silu, self.psr(b1), s1.r)
                    self.tt("dve", uT.t[:, fc, 0:T], self.ps[b3][:, 0:T], s1.t[:, 0:T], ALU.mult,
                            self.psr(b3) + s1.r, uT.r)
            kqs = [(0, 16), (16, 16), (32, 12)]
            for c4 in range(4):
                cs = slice(c4 * 512, (c4 + 1) * 512)
                base = 0 if c4 % 2 == 0 else 4
                banks = [base + i for i in range(nb)]
                gf = self.nxt(pieces)
                a, r = self.mod_piece(l, g, 5, c4)
                f.dma("sp", gf.t[:, :], a, reads=r, writes=gf.r)
                for qi, (k0, nk) in enumerate(kqs):
                    wt = self.load_wb("w2", l, k0, nk, c4 * 512, 512)
                    for tb in range(nb):
                        for kc in range(nk):
                            self.mm(self.ps[banks[tb]][:, :], uT.t[:, k0 + kc, tb * 128:(tb + 1) * 128], wt.t[:, kc, :],
                                    qi == 0 and kc == 0, qi == len(kqs) - 1 and kc == nk - 1,
                                    uT.r + wt.r, self.psr(banks[tb]))
                for tb in range(nb):
                    t1 = self.nxt(t1s)
                    self.tt("dve", t1.t[:, :], self.ps[banks[tb]][:, :], gf.t[:, :], ALU.mult,
                            self.psr(banks[tb]) + gf.r, t1.r)
                    self.ts("dve", x1t[tb].t[:, cs], x1t[tb].t[:, cs], ALPHA, None, ALU.mult, None,
                            x1t[tb].r, x1t[tb].r)
                    self.tt("dve", x1t[tb].t[:, cs], x1t[tb].t[:, cs], t1.t[:, :], ALU.add,
                            x1t[tb].r + t1.r, x1t[tb].r)
            for tb in range(nb):
                blk = b0 + tb

                def dsts(blk=blk):
                    if not last:
                        return [(self.x2[l, blk * 128:(blk + 1) * 128, :], [("x2", l, blk, c) for c in range(4)])]
                    if blk < self.NPB:
                        return [(self.y_p[blk * 128:(blk + 1) * 128, :], [])]
                    return [(self.y_s[:, :], [])]
                self.ln_affine_store(x1t[tb], self.nxt(sms), self.ln2g[l], self.ln2b[l], pieces, dsts)

    def build(self, upto=99):
        steps = [self.setup]
        for l in range(DEPTH):
            steps += [lambda l=l: self.phase_A(l),
                      lambda l=l: (self.convert_weights(l), self.phase_attn_prompt(l, "A")),
                      lambda l=l: self.phase_attn_prompt(l, "B"),
                      lambda l=l: self.phase_attn_sample(l, "A"),
                      lambda l=l: self.phase_attn_sample(l, "B"),
                      lambda l=l: self.phase_C(l),
                      lambda l=l: self.phase_D(l)]
        for st in steps[:upto]:
            st()
        info = self.f.emit()
        return self.nc, info


def _t5_bucket_np(rel):
    nb = 16
    max_exact = 8
    bucket = (rel > 0).astype(np.int64) * nb
    n = np.abs(rel)
    nf = np.maximum(n, 1).astype(np.float32)
    large = max_exact + (np.log(nf / max_exact) / math.log(128 / max_exact) * (nb - max_exact)).astype(np.int64)
    large = np.minimum(large, nb - 1)
    return bucket + np.where(n < max_exact, n, large)


def _static_tables():
    k = np.arange(128)[:, None]
    c = np.arange(256)[None, :]
    rel = k - c
    idxA = _t5_bucket_np(rel)
    idxB = np.clip(rel, -128, 128) + 128
    q = np.arange(128)[None, :]
    mdiag = ((k // CHUNK) <= (q // CHUNK)).astype(np.float32)
    mask = np.concatenate([mdiag, np.ones((128, 128), np.float32)], axis=1)
    return idxA, idxB, mask


def _bc(v, n=128):
    return np.ascontiguousarray(np.broadcast_to(v[..., None, :], v.shape[:-1] + (n, v.shape[-1])))


_CACHE = {}


def _get_program(S, upto=99):
    if S not in _CACHE:
        b = Builder(S=S, TA=min(1024, S), TCD=min(512, S))
        nc, info = b.build(upto)
        _CACHE[S] = nc
    return _CACHE[S]


def make_in_maps(inp, ncores, S):
    f32 = np.float32
    idxA, idxB, mask = _static_tables()
    t5 = np.asarray(inp["t5_bias"], f32)
    relb = np.asarray(inp["rel_bias"], f32)
    biasA = np.ascontiguousarray(np.transpose(t5[idxA], (0, 2, 1)))
    biasB = np.ascontiguousarray(np.transpose(relb[:, idxB], (0, 1, 3, 2)))
    c15A = _bc(t5[15])
    c0B = _bc(relb[:, 0, :])
    lamv = np.stack([inp["lambda_q1"], inp["lambda_k1"], inp["lambda_q2"], inp["lambda_k2"]], axis=1)
    shared = {
        "w_mod": np.asarray(inp["w_mod"], f32), "bmod": _bc(np.asarray(inp["b_mod"], f32)),
        "w_in": np.asarray(inp["w_in"], f32), "lamv": _bc(np.asarray(lamv, f32)),
        "subln": _bc(np.asarray(inp["subln_g"], f32)), "biasA": biasA, "c15A": c15A, "biasB": biasB, "c0B": c0B,
        "mask01": mask, "ident": np.eye(128, dtype=f32),
        "w_oa": np.asarray(inp["w_oa"], f32), "w_ob": np.asarray(inp["w_ob"], f32),
        "w_out": np.asarray(inp["w_out"], f32),
        "ln1g": _bc(np.asarray(inp["ln1_g"], f32)), "ln1b": _bc(np.asarray(inp["ln1_b"], f32)),
        "w1": np.asarray(inp["w1"], f32), "w3": np.asarray(inp["w3"], f32), "w2": np.asarray(inp["w2"], f32),
        "ln2g": _bc(np.asarray(inp["ln2_g"], f32)), "ln2b": _bc(np.asarray(inp["ln2_b"], f32)),
    }
    maps = []
    for i in range(ncores):
        sb = slice(4 * i, 4 * i + 4)
        cs = np.asarray(inp["c_sample"][sb], f32)
        cbc = np.stack([np.broadcast_to(np.asarray(inp["c_prompt"][i], f32)[None, :], (128, D)),
                        np.repeat(cs, 32, axis=0)], axis=0)
        m = dict(shared)
        m.update({
            "xp": np.ascontiguousarray(inp["x_prompt"][i], dtype=f32),
            "xs": np.ascontiguousarray(np.asarray(inp["x_sample"][sb], f32).reshape(128, D)),
            "cak": np.ascontiguousarray(np.asarray(inp["cache_a_k"][:, sb], f32).reshape(DEPTH, 4, -1, 1024)),
            "cav": np.ascontiguousarray(np.asarray(inp["cache_a_v"][:, sb], f32).reshape(DEPTH, 4, -1, 1024)),
            "cbk": np.ascontiguousarray(np.asarray(inp["cache_b_k"][:, sb], f32).reshape(DEPTH, 4, -1, 1024)),
            "cbv": np.ascontiguousarray(np.asarray(inp["cache_b_v"][:, sb], f32).reshape(DEPTH, 4, -1, 1024)),
            "cbc": np.ascontiguousarray(cbc),
        })
        maps.append(m)
    return maps


def assemble(results, ncores, S):
    L = DEPTH
    B = ncores
    bo = min(512, S)
    y_p = np.stack([r["y_p"] for r in results], 0)
    y_s = np.concatenate([r["y_s"].reshape(4, 32, D) for r in results], 0)
    akp = np.stack([r["akp"] for r in results], 1).reshape(L, B, S, 8, 2, 64)
    avp = np.stack([r["avp"] for r in results], 1).reshape(L, B, S, 8, 128)
    bkp = np.stack([r["bkp"] for r in results], 1).reshape(L, B, bo, 8, 128)
    bvp = np.stack([r["bvp"] for r in results], 1).reshape(L, B, bo, 8, 128)
    aks = np.concatenate([r["aks"].reshape(L, 4, 32, 1024) for r in results], 1).reshape(L, 4 * B, 32, 8, 2, 64)
    avs = np.concatenate([r["avs"].reshape(L, 4, 32, 1024) for r in results], 1).reshape(L, 4 * B, 32, 8, 128)
    bks = np.concatenate([r["bks"].reshape(L, 4, 32, 1024) for r in results], 1).reshape(L, 4 * B, 32, 8, 128)
    bvs = np.concatenate([r["bvs"].reshape(L, 4, 32, 1024) for r in results], 1).reshape(L, 4 * B, 32, 8, 128)
    return tuple(np.ascontiguousarray(a, dtype=np.float32) for a in (y_p, y_s, akp, avp, bkp, bvp, aks, avs, bks, bvs))


def kernel(**inputs):
    S = inputs["x_prompt"].shape[1]
    ncores = inputs["x_prompt"].shape[0]
    nc = _get_program(S)
    maps = make_in_maps(inputs, ncores, S)
    res = run_bass_kernel_spmd(nc, maps, core_ids=list(range(ncores)))
    return assemble(res.results, ncores, S)
```

```python
import math
import numpy as np
from contextlib import ExitStack
import concourse.bass as bass
import concourse.mybir as mybir
from concourse.bass_utils import run_bass_kernel_spmd

F32 = mybir.dt.float32
BF16 = mybir.dt.bfloat16
AF = mybir.ActivationFunctionType
ALU = mybir.AluOpType
AX = mybir.AxisListType

D = 2048
KC = 16
DFF = 5632
FC = 44
INW = 10240
DEPTH = 2
ALPHA = float((2 * DEPTH) ** 0.25)
LN_EPS = 1e-5
CHUNK = 64
NCORES = 8

ENGS = ("pe", "act", "dve", "pool", "sp")
N_DMA_SEMS = {"sp": 30, "act": 8, "pool": 30}
SEM_WRAP = 30000
SQ = "pool"


class Op:
    __slots__ = ("idx", "eng", "fn", "dma", "deps", "needs_inc", "ticket", "dsem", "dval", "pre")

    def __init__(self, idx, eng, fn, dma):
        self.idx = idx
        self.eng = eng
        self.fn = fn
        self.dma = dma
        self.deps = ()
        self.needs_inc = False
        self.ticket = None
        self.dsem = None
        self.dval = 0
        self.pre = None


class Fw:
    def __init__(self, nc):
        self.nc = nc
        self.ops = []
        self.lastw = {}
        self.readers = {}

    def op(self, eng, fn, reads=(), writes=(), dma=False, barrier=False):
        if not barrier and "ARENA" in writes:
            writes = [w for w in writes if w != "ARENA"]
            reads = list(reads) + ["ARENA"]
        idx = len(self.ops)
        o = Op(idx, eng, fn, dma)
        deps = set()
        for r in reads:
            lw = self.lastw.get(r)
            if lw is not None:
                deps.add(lw)
        for w in writes:
            lw = self.lastw.get(w)
            if lw is not None:
                deps.add(lw)
            rd = self.readers.get(w)
            if rd:
                deps.update(rd[0].values())
                deps.update(rd[1])
        for r in reads:
            rd = self.readers.get(r)
            if rd is None:
                rd = self.readers[r] = ({}, [])
            if dma:
                rd[1].append(idx)
            else:
                rd[0][eng] = idx
        for w in writes:
            self.lastw[w] = idx
            self.readers[w] = ({}, [])
        keep = []
        for d in deps:
            y = self.ops[d]
            if not y.dma:
                if y.eng == eng and not dma and eng == "pe":
                    continue
                y.needs_inc = True
            keep.append(d)
        o.deps = keep
        self.ops.append(o)
        return o

    def pe(self, fn, reads=(), writes=()):
        return self.op("pe", fn, reads, writes)

    def act(self, fn, reads=(), writes=()):
        return self.op("act", fn, reads, writes)

    def dve(self, fn, reads=(), writes=()):
        return self.op("dve", fn, reads, writes)

    def pool(self, fn, reads=(), writes=()):
        return self.op("pool", fn, reads, writes)

    def dma(self, q, out, in_, reads=(), writes=()):
        return self.op(q, lambda e: e.dma_start(out=out, in_=in_), reads, writes, dma=True)

    def emit(self):
        nc = self.nc
        with ExitStack() as es:
            counts = {e: 0 for e in ENGS}
            for o in self.ops:
                if not o.dma and o.needs_inc:
                    counts[o.eng] += 1
                    o.ticket = counts[o.eng]
            esems = {}
            for e in ENGS:
                n = counts[e] // SEM_WRAP + 1
                esems[e] = [es.enter_context(nc.semaphore(f"s_{e}_{i}")) for i in range(n)]
            dsems = {q: [es.enter_context(nc.semaphore(f"d_{q}_{i}")) for i in range(n)]
                     for q, n in N_DMA_SEMS.items()}
            dcount = {q: 0 for q in N_DMA_SEMS}
            dhist = {q: [] for q in N_DMA_SEMS}
            for o in self.ops:
                if o.dma:
                    q = o.eng
                    k = dcount[q]
                    P = N_DMA_SEMS[q]
                    o.dsem = dsems[q][k % P]
                    o.dval = 16 * (k // P + 1)
                    if k >= P:
                        o.pre = dhist[q][k - P]
                    dhist[q].append(o)
                    dcount[q] += 1

            def target(y):
                if y.dma:
                    return y.dsem, y.dval
                t = y.ticket
                ep = (t - 1) // SEM_WRAP
                return esems[y.eng][ep], t - ep * SEM_WRAP

            per_eng = {e: [] for e in ENGS}
            for o in self.ops:
                per_eng[o.eng].append(o)
            all_dma = [o for o in self.ops if o.dma]
            ops = self.ops

            def run_engine(ename, eng):
                waited = {}
                for o in per_eng[ename]:
                    w = {}
                    deps = [ops[d] for d in o.deps]
                    if o.pre is not None:
                        deps.append(o.pre)
                    for y in deps:
                        s, v = target(y)
                        key = id(s)
                        if waited.get(key, 0) >= v:
                            continue
                        if key not in w or w[key][1] < v:
                            w[key] = (s, v)
                    for key, (s, v) in w.items():
                        eng.wait_ge(s, v)
                        waited[key] = v
                    inst = o.fn(eng)
                    if o.dma:
                        inst.then_inc(o.dsem, 16)
                    elif o.needs_inc:
                        s, v = target(o)
                        inst.then_inc(s, 1)
                if ename == "sp":
                    last = {}
                    for y in all_dma:
                        k = id(y.dsem)
                        if k not in last or last[k][1] < y.dval:
                            last[k] = (y.dsem, y.dval)
                    for key, (s, v) in last.items():
                        if waited.get(key, 0) < v:
                            eng.wait_ge(s, v)

            with nc.Block() as block:
                @block.tensor
                def _(e):
                    run_engine("pe", e)

                @block.scalar
                def _(e):
                    run_engine("act", e)

                @block.vector
                def _(e):
                    run_engine("dve", e)

                @block.gpsimd
                def _(e):
                    run_engine("pool", e)

                @block.sync
                def _(e):
                    run_engine("sp", e)
        return counts, dcount


class Tl:
    __slots__ = ("t", "r")

    def __init__(self, t, r):
        self.t = t
        self.r = r


SB_BASE = 16640
CONST_BYTES = 11 * 1024
NWB = 3
WPF = 1
WB_BYTES = 16 * 1024
WB0 = SB_BASE + CONST_BYTES
ARENA0 = WB0 + NWB * WB_BYTES
SBUF_LIMIT = 229376 - 128


class Builder:
    def __init__(self, S=4096, TA=1024, TCD=512, PAST=1024, BPAST=512, NSB=4, TS=32, wplan=None):
        self.wplan = wplan
        self.converted = set()
        self.wlog = []
        self.wissued = 0
        assert NSB * TS == 128
        self.S, self.TA, self.TCD = S, TA, TCD
        self.PAST, self.BPAST, self.NSB, self.TS = PAST, BPAST, NSB, TS
        self.NPB = S // 128
        self.NB = self.NPB + 1
        self.NT = S + 128
        self.BOUT = min(512, S)
        self.nc = bass.Bass("TRN2", target_bir_lowering=False)
        self.f = Fw(self.nc)
        self.uid = 0
        self.wcnt = 0
        self.rot = {}
        self.declare()

    def din(self, name, shape, dt=F32):
        return self.nc.dram_tensor(name, list(shape), dt, kind="ExternalInput").ap()

    def dout(self, name, shape):
        return self.nc.dram_tensor(name, list(shape), F32, kind="ExternalOutput").ap()

    def dscr(self, name, shape, dt):
        return self.nc.dram_tensor(name, list(shape), dt, kind="Internal").ap()

    def declare(self):
        S, NT = self.S, self.NT
        L = DEPTH
        self.xp = self.din("xp", [S, D])
        self.xs = self.din("xs", [128, D])
        self.cak = self.din("cak", [L, self.NSB, self.PAST, 1024])
        self.cav = self.din("cav", [L, self.NSB, self.PAST, 1024])
        self.cbk = self.din("cbk", [L, self.NSB, self.BPAST, 1024])
        self.cbv = self.din("cbv", [L, self.NSB, self.BPAST, 1024])
        self.cbc = self.din("cbc", [2, 128, D])
        self.w_mod = self.din("w_mod", [L, D, 6 * D])
        self.bmod = self.din("bmod", [L, 128, 6 * D])
        self.w_in = self.din("w_in", [L, D, INW])
        self.lamv = self.din("lamv", [L, 4, 128, 64])
        self.subln = self.din("subln", [L, 128, 128])
        self.biasA = self.din("biasA", [128, 8, 256])
        self.c15A = self.din("c15A", [128, 8])
        self.biasB = self.din("biasB", [L, 128, 8, 256])
        self.c0B = self.din("c0B", [L, 128, 8])
        self.mask01 = self.din("mask01", [128, 256])
        self.identd = self.din("ident", [128, 128])
        self.w_oa = self.din("w_oa", [L, 1024, D])
        self.w_ob = self.din("w_ob", [L, 1024, D])
        self.w_out = self.din("w_out", [L, D, D])
        self.ln1g = self.din("ln1g", [L, 128, D])
        self.ln1b = self.din("ln1b", [L, 128, D])
        self.w1 = self.din("w1", [L, D, DFF])
        self.w3 = self.din("w3", [L, D, DFF])
        self.w2 = self.din("w2", [L, DFF, D])
        self.ln2g = self.din("ln2g", [L, 128, D])
        self.ln2b = self.din("ln2b", [L, 128, D])
        self.y_p = self.dout("y_p", [S, D])
        self.y_s = self.dout("y_s", [128, D])
        self.akp = self.dout("akp", [L, S, 1024])
        self.avp = self.dout("avp", [L, S, 1024])
        self.bkp = self.dout("bkp", [L, self.BOUT, 1024])
        self.bvp = self.dout("bvp", [L, self.BOUT, 1024])
        self.aks = self.dout("aks", [L, 128, 1024])
        self.avs = self.dout("avs", [L, 128, 1024])
        self.bks = self.dout("bks", [L, 128, 1024])
        self.bvs = self.dout("bvs", [L, 128, 1024])
        self.modbc = self.dscr("modbc", [L, 2, 128, 6 * D], F32)
        self.qaT = self.dscr("qaT", [L, 1024, NT], BF16)
        self.kaT = self.dscr("kaT", [L, 1024, NT], BF16)
        self.qbT = self.dscr("qbT", [L, 1024, NT], BF16)
        self.kbT = self.dscr("kbT", [L, 1024, NT], BF16)
        self.va = self.dscr("va", [L, NT, 1024], BF16)
        self.vb = self.dscr("vb", [L, NT, 1024], BF16)
        self.gaT = self.dscr("gaT", [L, D, NT], BF16)
        self.gbT = self.dscr("gbT", [L, D, NT], BF16)
        self.oaT = self.dscr("oaT", [L, 1024, NT], BF16)
        self.obT = self.dscr("obT", [L, 1024, NT], BF16)
        self.x1 = self.dscr("x1", [L, NT, D], F32)
        self.x2 = self.dscr("x2", [L, NT, D], F32)
        self.wsc = {"w_oa": (self.w_oa, self.dscr("w_oa_b", [L, 1024, D], BF16)),
                    "w_ob": (self.w_ob, self.dscr("w_ob_b", [L, 1024, D], BF16)),
                    "w_out": (self.w_out, self.dscr("w_out_b", [L, D, D], BF16)),
                    "w1": (self.w1, self.dscr("w1_b", [L, D, DFF], BF16)),
                    "w3": (self.w3, self.dscr("w3_b", [L, D, DFF], BF16)),
                    "w2": (self.w2, self.dscr("w2_b", [L, DFF, D], BF16))}
        self.ps = [self.nc.alloc_psum_tensor(f"psb{i}", [128, 512], F32) for i in range(8)]
        self.coff = SB_BASE
        self.ident = self.ctile("ident", [128, 128], BF16)
        self.c15 = self.ctile("c15", [128, 8], F32)
        self.c0b = self.ctile("c0b", [128, L * 8], F32)
        self.zero = self.ctile("zero", [128, 1], F32)
        self.negl = self.ctile("negl", [128, L], F32)
        self.gsub = self.ctile("gsub", [128, L, 128], F32)
        self.cT = [self.ctile(f"cT{g}", [128, KC, 128], BF16) for g in range(2)]
        self.dummy = self.ctile("dummy", [128, 8], F32)
        assert self.coff <= WB0, self.coff
        self.wb = []
        for i in range(NWB):
            t = self.nc.alloc_sbuf_tensor_at(f"wb{i}", [128, 16, 512], BF16, offset=WB0 + i * WB_BYTES)
            self.wb.append(Tl(t, [f"wb{i}"]))
        self.aoff = ARENA0

    def _alloc(self, name, shape, dt, off):
        self.uid += 1
        return self.nc.alloc_sbuf_tensor_at(f"{name}_{self.uid}", list(shape), dt, offset=off)

    @staticmethod
    def _nbytes(shape, dt):
        n = 1
        for s in shape[1:]:
            n *= s
        return n * (2 if dt == BF16 else 4)

    def ctile(self, name, shape, dt):
        nb = (self._nbytes(shape, dt) + 31) // 32 * 32
        t = self._alloc(name, shape, dt, self.coff)
        self.coff += nb
        return Tl(t, [name])

    def atile(self, name, shape, dt):
        nb = (self._nbytes(shape, dt) + 31) // 32 * 32
        assert self.aoff + nb <= SBUF_LIMIT, (name, self.aoff, nb)
        t = self._alloc(name, shape, dt, self.aoff)
        self.aoff += nb
        self.uid += 1
        return Tl(t, [f"{name}#{self.uid}", "ARENA"])

    def atiles(self, name, n, shape, dt):
        return [self.atile(f"{name}{i}", shape, dt) for i in range(n)]

    def nxt(self, lst):
        k = id(lst)
        i = self.rot.get(k, 0)
        self.rot[k] = i + 1
        return lst[i % len(lst)]

    def phase(self):
        f = self.f
        d = self.dummy
        f.op("dve", lambda e: e.memset(d.t[:, 0:1], 0.0), reads=[], writes=["ARENA"] + d.r, barrier=True)
        self.aoff = ARENA0

    def mm(self, out, lhsT, rhs, start, stop, reads, writes):
        self.f.pe(lambda e: e.matmul(out, lhsT, rhs, start=start, stop=stop), reads, writes)

    def tr(self, out, in_, reads, writes):
        idn = self.ident
        npart = in_.shape[0]
        self.f.pe(lambda e: e.transpose(out, in_, idn.t[0:npart, 0:npart]), list(reads) + idn.r, writes)

    def act(self, out, in_, func, reads, writes, bias=0.0, scale=1.0):
        self.f.act(lambda e: e.activation(out=out, in_=in_, func=func, bias=bias, scale=scale), reads, writes)

    def tt(self, eng, out, in0, in1, op, reads, writes):
        self.f.op(eng, lambda e: e.tensor_tensor(out, in0, in1, op), reads, writes)

    def ts(self, eng, out, in0, s1, s2, op0, op1, reads, writes):
        if op1 is None:
            self.f.op(eng, lambda e: e.tensor_scalar(out, in0, s1, None, op0), reads, writes)
        else:
            self.f.op(eng, lambda e: e.tensor_scalar(out, in0, s1, s2, op0, op1), reads, writes)

    def stt(self, eng, out, in0, scalar, in1, op0, op1, reads, writes):
        self.f.op(eng, lambda e: e.scalar_tensor_tensor(out, in0, scalar, in1, op0, op1), reads, writes)

    def cp(self, eng, out, in_, reads, writes):
        if eng == "act":
            self.f.act(lambda e: e.copy(out, in_), reads, writes)
        else:
            self.f.op(eng, lambda e: e.tensor_copy(out, in_), reads, writes)

    def psr(self, i):
        return [f"ps{i}"]

    def _issue_w(self, k, spec):
        kind, name, l, k0, nk, c0, ncols = spec
        wb = self.wb[k % NWB]
        if kind == "cast":
            W = {"w_mod": self.w_mod, "w_in": self.w_in}[name][l]
            rd = []
        else:
            W = self.wsc[name][1][l]
            rd = [(name + "_b", l, c0 // 512)]
        src = W[k0 * 128:(k0 + nk) * 128, c0:c0 + ncols].rearrange("(kc p) n -> p kc n", p=128)
        self.f.dma("pool", wb.t[:, 0:nk, 0:ncols], src, reads=rd, writes=wb.r)

    def _req_w(self, spec):
        k = self.wcnt
        self.wcnt += 1
        if self.wplan is None:
            self.wlog.append(spec)
            self._issue_w(k, spec)
        else:
            assert self.wplan[k] == spec, (k, spec, self.wplan[k])
            while self.wissued < min(k + 1 + WPF, len(self.wplan)):
                nx = self.wplan[self.wissued]
                if nx[0] == "bf16" and nx[2] not in self.converted and self.wissued > k:
                    break
                self._issue_w(self.wissued, nx)
                self.wissued += 1
        return self.wb[k % NWB]

    def load_w(self, name, l, k0, nk, c0, ncols):
        return self._req_w(("cast", name, l, k0, nk, c0, ncols))

    def convert_weights(self, l):
        self.converted.add(l)
        for name in ("w_oa", "w_ob", "w_out", "w1", "w3", "w2"):
            src, dst = self.wsc[name]
            ncol = src.shape[2]
            for c in range(ncol // 512):
                self.f.dma("pool", dst[l, :, c * 512:(c + 1) * 512], src[l, :, c * 512:(c + 1) * 512],
                           reads=[], writes=[(name + "_b", l, c)])

    def load_wb(self, name, l, k0, nk, c0, ncols):
        return self._req_w(("bf16", name, l, k0, nk, c0, ncols))

    def grp(self, blk):
        return 0 if blk < self.NPB else 1

    def x_src(self, l, blk):
        if l == 0:
            if blk < self.NPB:
                return self.xp[blk * 128:(blk + 1) * 128, :], []
            return self.xs[:, :], []
        return self.x2[l - 1, blk * 128:(blk + 1) * 128, :], [("x2", l - 1, blk, c) for c in range(4)]

    def supertiles(self, T):
        out = []
        nb = T // 128
        b = 0
        while b < self.NPB:
            n = min(nb, self.NPB - b)
            out.append((b, n))
            b += n
        out.append((self.NPB, 1))
        return out

    def ln_stats(self, xt, sm):
        st, mv, rs, nm = sm["st"], sm["mv"], sm["rs"], sm["nm"]
        for c in range(4):
            self.f.dve(lambda e, c=c: e.bn_stats(st.t[:, c, :], xt.t[:, c * 512:(c + 1) * 512]), xt.r, st.r)
        self.f.dve(lambda e: e.bn_aggr(mv.t[:, :], st.t[:, :, :]), st.r, mv.r)
        self.ts("dve", rs.t[:, :], mv.t[:, 1:2], LN_EPS, None, ALU.add, None, mv.r, rs.r)
        self.act(rs.t[:, :], rs.t[:, :], AF.Sqrt, rs.r, rs.r)
        self.f.dve(lambda e: e.reciprocal(rs.t[:, :], rs.t[:, :]), rs.r, rs.r)
        self.stt("dve", nm.t[:, :], mv.t[:, 0:1], -1.0, rs.t[:, :], ALU.mult, ALU.mult, mv.r + rs.r, nm.r)
        return rs, nm

    def small_set(self, name):
        return {"st": self.atile(name + "st", [128, 4, 6], F32), "mv": self.atile(name + "mv", [128, 2], F32),
                "rs": self.atile(name + "rs", [128, 1], F32), "nm": self.atile(name + "nm", [128, 1], F32)}

    def tok2feat(self, src, src_ap_fn, n, dst_fn, dst_r):
        c = 0
        while c < n:
            m = min(8, n - c)
            bank = 6 + (self.rot.setdefault("trb", 0) % 2)
            self.rot["trb"] += 1
            pt = self.ps[bank][:, :].bitcast(BF16).rearrange("p (a b) -> p a b", b=128)
            for j in range(m):
                a = src_ap_fn(c + j)
                npart = a.shape[0]
                self.tr(pt[:, j, 0:npart], a, src.r, self.psr(bank))
            npart = src_ap_fn(c).shape[0]
            eng = "act" if (self.rot["trb"] % 2) else "dve"
            self.cp(eng, dst_fn(c, m), pt[:, 0:m, 0:npart], self.psr(bank), dst_r)
            c += m

    def setup(self):
        f = self.f
        L = DEPTH
        self.phase()
        idf = self.atile("idf", [128, 128], F32)
        f.dma("sp", idf.t[:, :], self.identd[:, :], writes=idf.r)
        self.cp("dve", self.ident.t[:, :], idf.t[:, :], idf.r, self.ident.r)
        f.dma("sp", self.c15.t[:, :], self.c15A[:, :], writes=self.c15.r)
        for l in range(L):
            f.dma("sp", self.c0b.t[:, l * 8:(l + 1) * 8], self.c0B[l], writes=self.c0b.r)
        f.dve(lambda e: e.memset(self.zero.t[:, :], 0.0), [], self.zero.r)
        lv = self.atile("lv", [128, 4, 64], F32)
        pr = self.atile("pr", [128, 2, 64], F32)
        sm2 = self.atile("sm2", [128, 2], F32)
        ex2 = self.atile("ex2", [128, 2], F32)
        sg = self.atile("sg", [128, 128], F32)
        for l in range(L):
            lam_init = 0.8 - 0.6 * math.exp(-0.3 * l)
            for j in range(4):
                f.dma("sp", lv.t[:, j, :], self.lamv[l, j], writes=lv.r)
            self.tt("dve", pr.t[:, 0, :], lv.t[:, 0, :], lv.t[:, 1, :], ALU.mult, lv.r, pr.r)
            self.tt("dve", pr.t[:, 1, :], lv.t[:, 2, :], lv.t[:, 3, :], ALU.mult, lv.r, pr.r)
            f.dve(lambda e: e.reduce_sum(sm2.t[:, :], pr.t[:, :, :], axis=AX.X), pr.r, sm2.r)
            self.act(ex2.t[:, :], sm2.t[:, :], AF.Exp, sm2.r, ex2.r)
            self.tt("dve", sm2.t[:, 0:1], ex2.t[:, 0:1], ex2.t[:, 1:2], ALU.subtract, ex2.r, sm2.r)
            self.ts("dve", self.negl.t[:, l:l + 1], sm2.t[:, 0:1], lam_init, -1.0, ALU.add, ALU.mult,
                    sm2.r, self.negl.r)
            f.dma("sp", sg.t[:, :], self.subln[l], writes=sg.r)
            self.ts("dve", self.gsub.t[:, l, :], sg.t[:, :], 1.0 - lam_init, None, ALU.mult, None,
                    sg.r, self.gsub.r)
        crow = self.atiles("crow", 2, [128, D], F32)
        cbf = self.atiles("cbf", 2, [128, D], BF16)
        for g in range(2):
            f.dma("sp", crow[g].t[:, :], self.cbc[g], writes=crow[g].r)
            self.act(cbf[g].t[:, :], crow[g].t[:, :], AF.Silu, crow[g].r, cbf[g].r)
            cb, ct = cbf[g], self.cT[g]
            self.tok2feat(cb, lambda c, cb=cb: cb.t[:, c * 128:(c + 1) * 128], KC,
                          lambda c0, m, ct=ct: ct.t[:, c0:c0 + m, :], ct.r)
        bt = self.atiles("bt", 2, [128, 512], F32)
        ms = self.atiles("ms", 3, [128, 512], F32)
        for l in range(L):
            for n in range(24):
                wt = self.load_w("w_mod", l, 0, KC, n * 512, 512)
                b = self.nxt(bt)
                f.dma("sp", b.t[:, :], self.bmod[l, :, n * 512:(n + 1) * 512], writes=b.r)
                piece = n // 4
                for g in range(2):
                    bank = self.rot.setdefault("gb", 0) % 4
                    self.rot["gb"] += 1
                    ps = self.ps[bank]
                    for kc in range(KC):
                        self.mm(ps[:, :], self.cT[g].t[:, kc, :], wt.t[:, kc, :], kc == 0, kc == KC - 1,
                                self.cT[g].r + wt.r, self.psr(bank))
                    m = self.nxt(ms)
                    if piece in (1, 4):
                        self.stt("dve", m.t[:, :], ps[:, :], 1.0, b.t[:, :], ALU.add, ALU.add,
                                 self.psr(bank) + b.r, m.r)
                    else:
                        self.tt("dve", m.t[:, :], ps[:, :], b.t[:, :], ALU.add, self.psr(bank) + b.r, m.r)
                    f.dma(SQ, self.modbc[l, g, :, n * 512:(n + 1) * 512], m.t[:, :], reads=m.r,
                          writes=[("modbc", l, g, n)])

    def mod_piece(self, l, g, piece, c4):
        n = piece * 4 + c4
        return self.modbc[l, g, :, n * 512:(n + 1) * 512], [("modbc", l, g, n)]

    def ln_mod_T(self, l, blk, xt, sm, pieces, tmp, hb, piece_sc, piece_sh, dst_fn, dst_r):
        f = self.f
        g = self.grp(blk)
        rs, nm = self.ln_stats(xt, sm)
        for c4 in range(4):
            cs = slice(c4 * 512, (c4 + 1) * 512)
            psc, psh = self.nxt(pieces), self.nxt(pieces)
            a, r = self.mod_piece(l, g, piece_sc, c4)
            f.dma("sp", psc.t[:, :], a, reads=r, writes=psc.r)
            a, r = self.mod_piece(l, g, piece_sh, c4)
            f.dma("sp", psh.t[:, :], a, reads=r, writes=psh.r)
            t = self.nxt(tmp)
            self.act(t.t[:, :], xt.t[:, cs], AF.Identity, xt.r + rs.r + nm.r, t.r,
                     bias=nm.t[:, 0:1], scale=rs.t[:, 0:1])
            self.tt("dve", t.t[:, :], t.t[:, :], psc.t[:, :], ALU.mult, t.r + psc.r, t.r)
            self.tt("dve", hb.t[:, cs], t.t[:, :], psh.t[:, :], ALU.add, t.r + psh.r, hb.r)
        self.tok2feat(hb, lambda c: hb.t[:, c * 128:(c + 1) * 128], KC, dst_fn, dst_r)

    def phase_A(self, l):
        f = self.f
        segs = [("qa", 0, 1024), ("ka", 1024, 2048), ("va", 2048, 3072), ("qb", 3072, 4096),
                ("kb", 4096, 5120), ("vb", 5120, 6144), ("ga", 6144, 8192), ("gb", 8192, 10240)]
        fm_dst = {"qa": self.qaT, "qb": self.qbT, "ga": self.gaT, "gb": self.gbT}
        self.phase()
        hTs = self.atiles("hT", 2, [128, KC, self.TA], BF16)
        xrow = self.atiles("xrow", 2, [128, D], F32)
        hbs = self.atiles("hb", 2, [128, D], BF16)
        pieces = self.atiles("pc", 4, [128, 512], F32)
        tmp = self.atiles("tmp", 2, [128, 512], F32)
        sms = [self.small_set(f"sm{i}") for i in range(2)]
        st32 = self.atiles("st32", 3, [128, 512], F32)
        st16 = self.atiles("st16", 4, [128, 512], BF16)
        kst = self.atiles("kst", 2, [128, 4, 128], BF16)
        for (b0, nb) in self.supertiles(self.TA):
            T = nb * 128
            hT = self.nxt(hTs)
            for tb in range(nb):
                blk = b0 + tb
                xt = self.nxt(xrow)
                a, r = self.x_src(l, blk)
                f.dma("sp", xt.t[:, :], a, reads=r, writes=xt.r)
                self.ln_mod_T(l, blk, xt, self.nxt(sms), pieces, tmp, self.nxt(hbs), 1, 0,
                              lambda c0, m, tb=tb: hT.t[:, c0:c0 + m, tb * 128:(tb + 1) * 128], hT.r)
            for ctile in range(INW // 512):
                c0 = ctile * 512
                name, s0, s1 = [s for s in segs if s[1] <= c0 < s[2]][0]
                wt = self.load_w("w_in", l, 0, KC, c0, 512)
                if name in fm_dst:
                    dst = fm_dst[name]
                    for sub in range(4):
                        row0 = c0 - s0 + sub * 128
                        rc = row0 // 128
                        for t0 in range(0, T, 512):
                            N = min(512, T - t0)
                            bank = self.rot.setdefault("gb", 0) % 4
                            self.rot["gb"] += 1
                            ps = self.ps[bank]
                            for kc in range(KC):
                                self.mm(ps[:, 0:N], wt.t[:, kc, sub * 128:(sub + 1) * 128], hT.t[:, kc, t0:t0 + N],
                                        kc == 0, kc == KC - 1, wt.r + hT.r, self.psr(bank))
                            s = self.nxt(st16)
                            if name in ("ga", "gb"):
                                self.act(s.t[:, 0:N], ps[:, 0:N], AF.Sigmoid, self.psr(bank), s.r)
                            else:
                                self.cp("dve", s.t[:, 0:N], ps[:, 0:N], self.psr(bank), s.r)
                            tok0 = b0 * 128 + t0
                            res = [(name + "T", l, rc, tok0 // 128 + i) for i in range(N // 128)]
                            f.dma(SQ, dst[l, row0:row0 + 128, tok0:tok0 + N], s.t[:, 0:N], reads=s.r, writes=res)
                else:
                    cc = (c0 - s0) // 512
                    for tb in range(nb):
                        blk = b0 + tb
                        bank = self.rot.setdefault("gb", 0) % 4
                        self.rot["gb"] += 1
                        ps = self.ps[bank]
                        for kc in range(KC):
                            self.mm(ps[:, :], hT.t[:, kc, tb * 128:(tb + 1) * 128], wt.t[:, kc, :],
                                    kc == 0, kc == KC - 1, wt.r + hT.r, self.psr(bank))
                        odst = None
                        if blk < self.NPB:
                            if name == "ka":
                                odst = self.akp[l, blk * 128:(blk + 1) * 128, c0 - s0:c0 - s0 + 512]
                            elif name == "va":
                                odst = self.avp[l, blk * 128:(blk + 1) * 128, c0 - s0:c0 - s0 + 512]
                            elif blk * 128 >= self.S - self.BOUT:
                                r0 = blk * 128 - (self.S - self.BOUT)
                                o = self.bkp if name == "kb" else self.bvp
                                odst = o[l, r0:r0 + 128, c0 - s0:c0 - s0 + 512]
                        else:
                            o = {"ka": self.aks, "va": self.avs, "kb": self.bks, "vb": self.bvs}[name]
                            odst = o[l, :, c0 - s0:c0 - s0 + 512]
                        if odst is not None:
                            s32 = self.nxt(st32)
                            self.cp("act", s32.t[:, :], ps[:, :], self.psr(bank), s32.r)
                            f.dma(SQ, odst, s32.t[:, :], reads=s32.r)
                            s = self.nxt(st16)
                            self.cp("dve", s.t[:, :], s32.t[:, :], s32.r, s.r)
                        else:
                            s = self.nxt(st16)
                            self.cp("dve", s.t[:, :], ps[:, :], self.psr(bank), s.r)
                        if name in ("va", "vb"):
                            dst = self.va if name == "va" else self.vb
                            f.dma(SQ, dst[l, blk * 128:(blk + 1) * 128, c0 - s0:c0 - s0 + 512], s.t[:, :],
                                  reads=s.r, writes=[(name, l, blk, cc)])
                        else:
                            k = self.nxt(kst)
                            self.tok2feat(s, lambda c, s=s: s.t[:, c * 128:(c + 1) * 128], 4,
                                          lambda c0_, m, k=k: k.t[:, c0_:c0_ + m, :], k.r)
                            dst = self.kaT if name == "ka" else self.kbT
                            rows = c0 - s0
                            f.dma(SQ, dst[l, rows:rows + 512, blk * 128:(blk + 1) * 128]
                                  .rearrange("(c p) s -> p c s", p=128), k.t[:, :, :], reads=k.r,
                                  writes=[(name + "T", l, rows // 128 + i, blk) for i in range(4)])

    def load_E(self, l, which):
        f = self.f
        nt = 2 if which == "A" else 1
        E = self.atile("E" + which, [128, 8, 2, nt, 128], F32)
        braw = self.atile("braw", [128, 8, 256], F32)
        msk = self.atile("msk", [128, 256], F32)
        src = self.biasA if which == "A" else self.biasB[l]
        f.dma("sp", braw.t[:, :, :], src, writes=braw.r)
        f.dma("sp", msk.t[:, :], self.mask01[:, :], writes=msk.r)
        self.act(braw.t[:, :, :], braw.t[:, :, :], AF.Exp, braw.r, braw.r)
        for h in range(8):
            for t in range(nt):
                self.tt("dve", E.t[:, h, :, t, :], braw.t[:, h, :].rearrange("p (b q) -> p b q", q=128),
                        msk.t[:, :].rearrange("p (b q) -> p b q", q=128), ALU.mult, braw.r + msk.r, E.r)
        return E

    def attn_core(self, which, l, h, qT_ap, q_r, nq, blocks, E, cfar_ap, par, o_dst, o_r, P3, wk, pend=None,
                  after=None):
        f = self.f
        nt = 2 if which == "A" else 1
        scale = 0.125 if which == "A" else 128 ** -0.5
        near = [b for b in blocks if b["kind"] in ("diag", "prev")]
        far = [b for b in blocks if b["kind"] in ("far", "far4")]
        per = 2 if which == "A" else 4
        groups = []
        if near:
            nks = sorted(set(b["nk"] for b in near))
            for nk in nks:
                groups.append(("near", [b for b in near if b["nk"] == nk]))
        n_near = len(groups)
        for i in range(0, len(far), per):
            groups.append(("far", far[i:i + per]))
        accb = [2 + 2 * par + t for t in range(nt)]
        ntot = len(blocks)
        seen = [0]

        def s_view(bank):
            return self.ps[bank][:, :].rearrange("p (b t q) -> p b t q", t=nt, q=128)

        def do_qk(gi):
            kind, grp = groups[gi]
            bank = gi % 2
            S4 = s_view(bank)
            for bi, b in enumerate(grp):
                nk = b["nk"]
                for t in range(nt):
                    lhsT = b["kT"][:, 0:nk]
                    rhs = qT_ap[:, t, 0:nq]
                    self.mm(S4[0:nk, bi, t, 0:nq], lhsT, rhs, True, True, b["r"] + q_r, self.psr(bank))

        def do_rest(gi):
            kind, grp = groups[gi]
            bank = gi % 2
            S4 = s_view(bank)
            P = self.nxt(P3)
            P4 = P.t[:, :].rearrange("p (b t q) -> p b t q", t=nt, q=128)
            nk = grp[0]["nk"]
            nbk = len(grp)
            if kind == "far":
                self.act(P4[0:nk, 0:nbk, :, 0:nq], S4[0:nk, 0:nbk, :, 0:nq], AF.Exp, self.psr(bank), P.r,
                         bias=cfar_ap[0:nk, :], scale=scale)
                for bi, b in enumerate(grp):
                    if b["kind"] == "far4" and nq > 64:
                        f.op("dve", lambda e, bi=bi: e.memset(P4[0:64, bi, :, 64:nq], 0.0), P.r, P.r)
            else:
                self.act(P4[0:nk, 0:nbk, :, 0:nq], S4[0:nk, 0:nbk, :, 0:nq], AF.Exp, self.psr(bank), P.r,
                         bias=self.zero.t[0:nk, :], scale=scale)
                for bi, b in enumerate(grp):
                    eb = 0 if b["kind"] == "diag" else 1
                    self.tt("dve", P4[0:nk, bi, :, 0:nq], P4[0:nk, bi, :, 0:nq], E.t[0:nk, h, eb, :, 0:nq],
                            ALU.mult, P.r + E.r, P.r)
            for bi, b in enumerate(grp):
                nk = b["nk"]
                for t in range(nt):
                    self.mm(self.ps[accb[t]][0:nq, 0:129], P4[0:nk, bi, t, 0:nq], b["v"][0:nk, 0:129],
                            seen[0] == 0, seen[0] == ntot - 1, P.r + b["r"], self.psr(accb[t]))
                seen[0] += 1

        ng = len(groups)
        do_qk(0)
        for gi in range(ng):
            if gi + 1 < ng:
                do_qk(gi + 1)
            do_rest(gi)
            if gi == n_near - 1 and pend is not None:
                pend[0]()
        if pend is not None:
            if n_near == 0:
                pend[0]()
            pend[1]()

        def fin_dve():
            self._attn_norm(which, l, nq, nt, accb, wk)

        def fin_pe():
            on = wk["on"]
            bank = 6 + (self.rot.setdefault("trb", 0) % 2)
            self.rot["trb"] += 1
            pt = self.ps[bank][:, :].bitcast(BF16)
            self.tr(pt[:, 0:nq], on.t[0:nq, :], on.r, self.psr(bank))
            self.cp("act", o_dst, pt[:, 0:nq], self.psr(bank), o_r)
            if after is not None:
                after()
        return (fin_dve, fin_pe)

    def _attn_norm(self, which, l, nq, nt, accb, wk):
        f = self.f
        rz, t1, o, ss, on = wk["rz"], wk["t1"], wk["o"], wk["ss"], wk["on"]
        accr = [r for t in range(nt) for r in self.psr(accb[t])]
        for t in range(nt):
            f.dve(lambda e, t=t: e.reciprocal(rz.t[0:nq, t:t + 1], self.ps[accb[t]][0:nq, 128:129]), accr, rz.r)
        if which == "A":
            self.ts("dve", t1.t[0:nq, :], self.ps[accb[1]][0:nq, 0:128], rz.t[0:nq, 1:2], self.negl.t[0:nq, l:l + 1],
                    ALU.mult, ALU.mult, accr + rz.r + self.negl.r, t1.r)
            self.stt("dve", o.t[0:nq, :], self.ps[accb[0]][0:nq, 0:128], rz.t[0:nq, 0:1], t1.t[0:nq, :],
                     ALU.mult, ALU.add, accr + rz.r + t1.r, o.r)
            self.tt("dve", t1.t[0:nq, :], o.t[0:nq, :], o.t[0:nq, :], ALU.mult, o.r + t1.r, t1.r)
            f.dve(lambda e: e.reduce_sum(ss.t[0:nq, 0:1], t1.t[0:nq, :], axis=AX.X), t1.r + ss.r, ss.r)
            self.ts("dve", ss.t[0:nq, 1:2], ss.t[0:nq, 0:1], 1.0 / 128, LN_EPS, ALU.mult, ALU.add, ss.r, ss.r)
            self.act(ss.t[0:nq, 1:2], ss.t[0:nq, 1:2], AF.Ln, ss.r, ss.r)
            self.act(ss.t[0:nq, 1:2], ss.t[0:nq, 1:2], AF.Exp, ss.r, ss.r, scale=-0.5)
            self.stt("dve", on.t[0:nq, :], o.t[0:nq, :], ss.t[0:nq, 1:2], self.gsub.t[0:nq, l, :],
                     ALU.mult, ALU.mult, o.r + ss.r + self.gsub.r, on.r)
        else:
            self.ts("dve", on.t[0:nq, :], self.ps[accb[0]][0:nq, 0:128], rz.t[0:nq, 0:1], None, ALU.mult, None,
                    accr + rz.r, on.r)

    def attn_work(self):
        wk = []
        for i in range(2):
            wk.append({"rz": self.atile("rz", [128, 2], F32), "t1": self.atile("t1", [128, 128], F32),
                       "o": self.atile("o", [128, 128], F32), "ss": self.atile("ss", [128, 2], F32),
                       "on": self.atile("on", [128, 128], BF16)})
        return wk

    def phase_attn_prompt(self, l, which):
        f = self.f
        NPB = self.NPB
        S = self.S
        A = which == "A"
        kTd, vd, qTd, oTd = (self.kaT, self.va, self.qaT, self.oaT) if A else (self.kbT, self.vb, self.qbT, self.obT)
        kn, vn, qn, on_ = ("kaT", "va", "qaT", "oaT") if A else ("kbT", "vb", "qbT", "obT")
        for hh in range(2):
            self.phase()
            E = self.load_E(l, which)
            kT = self.atile("kT", [128, 4, S], BF16)
            V1 = self.atile("V1", [128, NPB, 4, 130], BF16)
            f.op("dve", lambda e, V1=V1: e.memset(V1.t[:, :, :, 128:129], 1.0), V1.r, V1.r)
            nt = 2 if A else 1
            qTs = self.atiles("qT", 2, [128, 4, nt, 128], BF16)
            if A:
                for q_ in qTs:
                    f.op("dve", lambda e, q_=q_: e.memset(q_.t[64:128, :, 0, :], 0.0), q_.r, q_.r)
                    f.op("dve", lambda e, q_=q_: e.memset(q_.t[0:64, :, 1, :], 0.0), q_.r, q_.r)
            oTs = self.atiles("oT", 2, [128, 4, 128], BF16)
            P3 = self.atiles("P", 3, [128, 512], BF16)
            wk = self.attn_work()
            pend = None
            kres = [[f"kTb{hh}_{i}"] for i in range(NPB)]
            vres = [[f"vTb{hh}_{i}"] for i in range(NPB)]
            cnt = 0
            for i in range(NPB):
                f.dma("sp", kT.t[:, :, i * 128:(i + 1) * 128],
                      kTd[l, hh * 512:(hh + 1) * 512, i * 128:(i + 1) * 128].rearrange("(h p) s -> p h s", p=128),
                      reads=[(kn, l, hh * 4 + c, i) for c in range(4)] + ["ARENA"], writes=kres[i])
                f.dma("sp", V1.t[:, i, :, 0:128],
                      vd[l, i * 128:(i + 1) * 128, hh * 512:(hh + 1) * 512].rearrange("s (h d) -> s h d", d=128),
                      reads=[(vn, l, i, hh), "ARENA"], writes=vres[i])
                qT = self.nxt(qTs)
                qsrc = qTd[l, hh * 512:(hh + 1) * 512, i * 128:(i + 1) * 128].rearrange("(h p) s -> p h s", p=128)
                qrd = [(qn, l, hh * 4 + c, i) for c in range(4)]
                if A:
                    f.dma("sp", qT.t[0:64, :, 0, :], qsrc[0:64], reads=qrd, writes=qT.r)
                    f.dma("sp", qT.t[64:128, :, 1, :], qsrc[64:128], reads=qrd, writes=qT.r)
                else:
                    f.dma("sp", qT.t[:, :, 0, :], qsrc, reads=qrd, writes=qT.r)
                oT = self.nxt(oTs)
                for h4 in range(4):
                    h = hh * 4 + h4
                    blocks = []
                    jlo = 0 if A else max(0, i - 4)
                    for j in range(jlo, i + 1):
                        if j == i:
                            kind = "diag"
                        elif j == i - 1:
                            kind = "prev"
                        elif (not A) and j == i - 4:
                            kind = "far4"
                        else:
                            kind = "far"
                        blocks.append({"kT": kT.t[:, h4, j * 128:(j + 1) * 128], "v": V1.t[:, j, h4, :],
                                       "nk": 128, "kind": kind, "r": kres[j] + vres[j] + V1.r})
                    cfar = self.c15.t[:, h:h + 1] if A else self.c0b.t[:, l * 8 + h:l * 8 + h + 1]
                    after = None
                    if h4 == 3:
                        def after(i=i, oT=oT, hh=hh):
                            f.dma(SQ, oTd[l, hh * 512:(hh + 1) * 512, i * 128:(i + 1) * 128]
                                  .rearrange("(h p) s -> p h s", p=128),
                                  oT.t[:, :, :], reads=oT.r, writes=[(on_, l, hh * 4 + c, i) for c in range(4)])
                    pend = self.attn_core(which, l, h, qT.t[:, h4, :, :], qT.r, 128, blocks, E, cfar, cnt % 2,
                                          oT.t[:, h4, :], oT.r, P3, wk[cnt % 2], pend, after)
                    cnt += 1
            pend[0]()
            pend[1]()

    def phase_attn_sample(self, l, which):
        f = self.f
        A = which == "A"
        NPB, TS = self.NPB, self.TS
        past = self.PAST if A else self.BPAST
        nkb = past // 128
        kTd, vd, qTd, oTd = (self.kaT, self.va, self.qaT, self.oaT) if A else (self.kbT, self.vb, self.qbT, self.obT)
        kn, vn, qn, on_ = ("kaT", "va", "qaT", "oaT") if A else ("kbT", "vb", "qbT", "obT")
        ck, cv = (self.cak, self.cav) if A else (self.cbk, self.cbv)
        self.phase()
        E = self.load_E(l, which)
        kTs = self.atiles("kTs", 2, [128, 8, past + TS], BF16)
        V1s = self.atiles("V1s", 2, [128, nkb + 1, 8, 130], BF16)
        for v in V1s:
            f.op("dve", lambda e, v=v: e.memset(v.t[:, :, :, 128:129], 1.0), v.r, v.r)
        craw = self.atiles("craw", 2, [128, 1024], BF16)
        nt = 2 if A else 1
        qT = self.atile("qTs", [128, 8, nt, 128], BF16)
        if A:
            f.op("dve", lambda e: e.memset(qT.t[64:128, :, 0, :], 0.0), qT.r, qT.r)
            f.op("dve", lambda e: e.memset(qT.t[0:64, :, 1, :], 0.0), qT.r, qT.r)
        oT = self.atile("oTs", [128, 8, 128], BF16)
        P3 = self.atiles("P", 3, [128, 512], BF16)
        wk = self.attn_work()
        tokc = NPB * 128
        qsrc = qTd[l, :, tokc:tokc + 128].rearrange("(h p) s -> p h s", p=128)
        qrd = [(qn, l, c, NPB) for c in range(8)]
        if A:
            f.dma("sp", qT.t[0:64, :, 0, :], qsrc[0:64], reads=qrd, writes=qT.r)
            f.dma("sp", qT.t[64:128, :, 1, :], qsrc[64:128], reads=qrd, writes=qT.r)
        else:
            f.dma("sp", qT.t[:, :, 0, :], qsrc, reads=qrd, writes=qT.r)
        cnt = 0
        pend = None
        for sb in range(self.NSB):
            kT, V1 = self.nxt(kTs), self.nxt(V1s)
            for kb in range(nkb):
                cr = self.nxt(craw)
                f.dma("pool", cr.t[:, :], ck[l, sb, kb * 128:(kb + 1) * 128, :], writes=cr.r)
                self.tok2feat(cr, lambda c, cr=cr: cr.t[:, c * 128:(c + 1) * 128], 8,
                              lambda c0, m, kT=kT, kb=kb: kT.t[:, c0:c0 + m, kb * 128:(kb + 1) * 128], kT.r)
                f.dma("pool", V1.t[:, kb, :, 0:128],
                      cv[l, sb, kb * 128:(kb + 1) * 128, :].rearrange("s (h d) -> s h d", d=128), writes=V1.r)
            f.dma("sp", kT.t[:, :, past:past + TS],
                  kTd[l, :, tokc + sb * TS:tokc + (sb + 1) * TS].rearrange("(h p) s -> p h s", p=128),
                  reads=[(kn, l, c, NPB) for c in range(8)], writes=kT.r)
            f.dma("sp", V1.t[0:TS, nkb, :, 0:128],
                  vd[l, tokc + sb * TS:tokc + (sb + 1) * TS, :].rearrange("s (h d) -> s h d", d=128),
                  reads=[(vn, l, NPB, c) for c in range(2)], writes=V1.r)
            for h in range(8):
                blocks = []
                for j in range(nkb + 1):
                    if j == nkb:
                        kind, nk = "diag", TS
                    elif j == nkb - 1:
                        kind, nk = "prev", 128
                    else:
                        kind, nk = "far", 128
                    blocks.append({"kT": kT.t[:, h, j * 128:j * 128 + nk], "v": V1.t[:, j, h, :], "nk": nk,
                                   "kind": kind, "r": kT.r[:1] + V1.r})
                cfar = self.c15.t[:, h:h + 1] if A else self.c0b.t[:, l * 8 + h:l * 8 + h + 1]
                pend = self.attn_core(which, l, h, qT.t[:, h, :, sb * TS:(sb + 1) * TS], qT.r, TS, blocks, E, cfar,
                                      cnt % 2, oT.t[:, h, sb * TS:(sb + 1) * TS], oT.r, P3, wk[cnt % 2], pend)
                cnt += 1
        pend[0]()
        pend[1]()
        f.dma(SQ, oTd[l, :, tokc:tokc + 128].rearrange("(h p) s -> p h s", p=128), oT.t[:, :, :],
              reads=oT.r, writes=[(on_, l, c, NPB) for c in range(8)])

    def ln_affine_store(self, xt, sm, gsrc, bsrc, pieces, dst_fn):
        f = self.f
        rs, nm = self.ln_stats(xt, sm)
        for c4 in range(4):
            cs = slice(c4 * 512, (c4 + 1) * 512)
            pg, pb = self.nxt(pieces), self.nxt(pieces)
            f.dma("sp", pg.t[:, :], gsrc[:, cs], writes=pg.r)
            f.dma("sp", pb.t[:, :], bsrc[:, cs], writes=pb.r)
            self.act(xt.t[:, cs], xt.t[:, cs], AF.Identity, xt.r + rs.r + nm.r, xt.r,
                     bias=nm.t[:, 0:1], scale=rs.t[:, 0:1])
            self.tt("dve", xt.t[:, cs], xt.t[:, cs], pg.t[:, :], ALU.mult, xt.r + pg.r, xt.r)
            self.tt("dve", xt.t[:, cs], xt.t[:, cs], pb.t[:, :], ALU.add, xt.r + pb.r, xt.r)
        for (dst, wres) in dst_fn():
            f.dma(SQ, dst, xt.t[:, :], reads=xt.r, writes=wres)

    def phase_C(self, l):
        f = self.f
        self.phase()
        TM = self.TCD
        oaTs = self.atiles("oaTt", 2, [128, 8, TM], BF16)
        obTs = self.atiles("obTt", 2, [128, 8, TM], BF16)
        mTs = self.atiles("mT", 2, [128, KC, TM], BF16)
        gts = self.atiles("gt", 4, [128, TM], BF16)
        tmps = self.atiles("tmpc", 4, [128, TM], F32)
        y1all = self.atiles("y1", TM // 128 + 2, [128, D], F32)
        pieces = self.atiles("pcc", 6, [128, 512], F32)
        xrs = self.atiles("xr", 3, [128, 512], F32)
        t1s = self.atiles("t1c", 2, [128, 512], F32)
        sms = [self.small_set(f"smc{i}") for i in range(2)]
        for (b0, nb) in self.supertiles(self.TCD):
            T = nb * 128
            tok0 = b0 * 128
            g = self.grp(b0)
            oaT, obT, mT = self.nxt(oaTs), self.nxt(obTs), self.nxt(mTs)
            y1 = [self.nxt(y1all) for _ in range(nb)]
            blks = list(range(b0, b0 + nb))
            f.dma("sp", oaT.t[:, :, 0:T], self.oaT[l, :, tok0:tok0 + T].rearrange("(kc p) s -> p kc s", p=128),
                  reads=[("oaT", l, c, b) for c in range(8) for b in blks], writes=oaT.r)
            f.dma("sp", obT.t[:, :, 0:T], self.obT[l, :, tok0:tok0 + T].rearrange("(kc p) s -> p kc s", p=128),
                  reads=[("obT", l, c, b) for c in range(8) for b in blks], writes=obT.r)
            for ct4 in range(4):
                wa = self.load_wb("w_oa", l, 0, 8, ct4 * 512, 512)
                wb_ = self.load_wb("w_ob", l, 0, 8, ct4 * 512, 512)
                for sub in range(4):
                    ct = ct4 * 4 + sub
                    gA, gB = self.nxt(gts), self.nxt(gts)
                    f.dma("sp", gA.t[:, 0:T], self.gaT[l, ct * 128:(ct + 1) * 128, tok0:tok0 + T],
                          reads=[("gaT", l, ct, b) for b in blks], writes=gA.r)
                    f.dma("sp", gB.t[:, 0:T], self.gbT[l, ct * 128:(ct + 1) * 128, tok0:tok0 + T],
                          reads=[("gbT", l, ct, b) for b in blks], writes=gB.r)
                    bA = self.rot.setdefault("gb", 0) % 4
                    bB = (bA + 1) % 4
                    self.rot["gb"] += 2
                    for kc in range(8):
                        self.mm(self.ps[bA][:, 0:T], wa.t[:, kc, sub * 128:(sub + 1) * 128], oaT.t[:, kc, 0:T],
                                kc == 0, kc == 7, wa.r + oaT.r, self.psr(bA))
                    for kc in range(8):
                        self.mm(self.ps[bB][:, 0:T], wb_.t[:, kc, sub * 128:(sub + 1) * 128], obT.t[:, kc, 0:T],
                                kc == 0, kc == 7, wb_.r + obT.r, self.psr(bB))
                    ta, tb_ = self.nxt(tmps), self.nxt(tmps)
                    self.tt("dve", ta.t[:, 0:T], self.ps[bA][:, 0:T], gA.t[:, 0:T], ALU.mult, self.psr(bA) + gA.r, ta.r)
                    self.tt("dve", tb_.t[:, 0:T], self.ps[bB][:, 0:T], gB.t[:, 0:T], ALU.mult, self.psr(bB) + gB.r, tb_.r)
                    self.tt("dve", mT.t[:, ct, 0:T], ta.t[:, 0:T], tb_.t[:, 0:T], ALU.add, ta.r + tb_.r, mT.r)
            for c4 in range(4):
                cs = slice(c4 * 512, (c4 + 1) * 512)
                wo = self.load_wb("w_out", l, 0, KC, c4 * 512, 512)
                gm = self.nxt(pieces)
                a, r = self.mod_piece(l, g, 2, c4)
                f.dma("sp", gm.t[:, :], a, reads=r, writes=gm.r)
                for tb in range(nb):
                    blk = b0 + tb
                    xr = self.nxt(xrs)
                    a, r = self.x_src(l, blk)
                    f.dma("sp", xr.t[:, :], a[:, cs], reads=r, writes=xr.r)
                    bank = self.rot.setdefault("gb", 0) % 4
                    self.rot["gb"] += 1
                    for kc in range(KC):
                        self.mm(self.ps[bank][:, :], mT.t[:, kc, tb * 128:(tb + 1) * 128], wo.t[:, kc, :],
                                kc == 0, kc == KC - 1, mT.r + wo.r, self.psr(bank))
                    t1 = self.nxt(t1s)
                    self.tt("dve", t1.t[:, :], self.ps[bank][:, :], gm.t[:, :], ALU.mult, self.psr(bank) + gm.r, t1.r)
                    self.ts("dve", xr.t[:, :], xr.t[:, :], ALPHA, None, ALU.mult, None, xr.r, xr.r)
                    self.tt("dve", y1[tb].t[:, cs], xr.t[:, :], t1.t[:, :], ALU.add, xr.r + t1.r, y1[tb].r)
            for tb in range(nb):
                blk = b0 + tb
                self.ln_affine_store(y1[tb], self.nxt(sms), self.ln1g[l], self.ln1b[l], pieces,
                                     lambda blk=blk: [(self.x1[l, blk * 128:(blk + 1) * 128, :],
                                                       [("x1", l, blk, c) for c in range(4)])])

    def phase_D(self, l):
        f = self.f
        last = l == DEPTH - 1
        self.phase()
        TM = self.TCD
        x1all = self.atiles("x1t", TM // 128 + 2, [128, D], F32)
        h2T = self.atile("h2T", [128, KC, TM], BF16)
        uT = self.atile("uT", [128, FC, TM], BF16)
        hbs = self.atiles("hbd", 2, [128, D], BF16)
        pieces = self.atiles("pcd", 6, [128, 512], F32)
        tmp = self.atiles("tmpd", 2, [128, 512], F32)
        s1s = self.atiles("s1", 2, [128, TM], F32)
        t1s = self.atiles("t1d", 2, [128, 512], F32)
        sms = [self.small_set(f"smd{i}") for i in range(2)]
        for (b0, nb) in self.supertiles(self.TCD):
            T = nb * 128
            g = self.grp(b0)
            x1t = [self.nxt(x1all) for _ in range(nb)]
            for tb in range(nb):
                blk = b0 + tb
                f.dma("sp", x1t[tb].t[:, :], self.x1[l, blk * 128:(blk + 1) * 128, :],
                      reads=[("x1", l, blk, c) for c in range(4)], writes=x1t[tb].r)
                self.ln_mod_T(l, blk, x1t[tb], self.nxt(sms), pieces, tmp, self.nxt(hbs), 4, 3,
                              lambda c0, m, tb=tb: h2T.t[:, c0:c0 + m, tb * 128:(tb + 1) * 128], h2T.r)
            for f4 in range(DFF // 512):
                w1t = self.load_wb("w1", l, 0, KC, f4 * 512, 512)
                w3t = self.load_wb("w3", l, 0, KC, f4 * 512, 512)
                for sub in range(4):
                    fc = f4 * 4 + sub
                    b1 = self.rot.setdefault("gb", 0) % 4
                    b3 = (b1 + 1) % 4
                    self.rot["gb"] += 2
                    for kc in range(KC):
                        self.mm(self.ps[b1][:, 0:T], w1t.t[:, kc, sub * 128:(sub + 1) * 128], h2T.t[:, kc, 0:T],
                                kc == 0, kc == KC - 1, w1t.r + h2T.r, self.psr(b1))
                    for kc in range(KC):
                        self.mm(self.ps[b3][:, 0:T], w3t.t[:, kc, sub * 128:(sub + 1) * 128], h2T.t[:, kc, 0:T],
                                kc == 0, kc == KC - 1, w3t.r + h2T.r, self.psr(b3))
                    s1 = self.nxt(s1s)
                    self.act(s1.t[:, 0:T], self.ps[b1][:, 0:T], AF.Silu, self.psr(b1), s1.r)
                    self.tt("dve", uT.t[:, fc, 0:T], self.ps[b3][:, 0:T], s1.t[:, 0:T], ALU.mult,
                            self.psr(b3) + s1.r, uT.r)
            kqs = [(0, 16), (16, 16), (32, 12)]
            for c4 in range(4):
                cs = slice(c4 * 512, (c4 + 1) * 512)
                base = 0 if c4 % 2 == 0 else 4
                banks = [base + i for i in range(nb)]
                gf = self.nxt(pieces)
                a, r = self.mod_piece(l, g, 5, c4)
                f.dma("sp", gf.t[:, :], a, reads=r, writes=gf.r)
                for qi, (k0, nk) in enumerate(kqs):
                    wt = self.load_wb("w2", l, k0, nk, c4 * 512, 512)
                    for tb in range(nb):
                        for kc in range(nk):
                            self.mm(self.ps[banks[tb]][:, :], uT.t[:, k0 + kc, tb * 128:(tb + 1) * 128], wt.t[:, kc, :],
                                    qi == 0 and kc == 0, qi == len(kqs) - 1 and kc == nk - 1,
                                    uT.r + wt.r, self.psr(banks[tb]))
                for tb in range(nb):
                    t1 = self.nxt(t1s)
                    self.tt("dve", t1.t[:, :], self.ps[banks[tb]][:, :], gf.t[:, :], ALU.mult,
                            self.psr(banks[tb]) + gf.r, t1.r)
                    self.ts("dve", x1t[tb].t[:, cs], x1t[tb].t[:, cs], ALPHA, None, ALU.mult, None,
                            x1t[tb].r, x1t[tb].r)
                    self.tt("dve", x1t[tb].t[:, cs], x1t[tb].t[:, cs], t1.t[:, :], ALU.add,
                            x1t[tb].r + t1.r, x1t[tb].r)
            for tb in range(nb):
                blk = b0 + tb

                def dsts(blk=blk):
                    if not last:
                        return [(self.x2[l, blk * 128:(blk + 1) * 128, :], [("x2", l, blk, c) for c in range(4)])]
                    if blk < self.NPB:
                        return [(self.y_p[blk * 128:(blk + 1) * 128, :], [])]
                    return [(self.y_s[:, :], [])]
                self.ln_affine_store(x1t[tb], self.nxt(sms), self.ln2g[l], self.ln2b[l], pieces, dsts)

    def build(self, upto=99):
        steps = [self.setup]
        for l in range(DEPTH):
            steps += [lambda l=l: self.phase_A(l),
                      lambda l=l: ([self.convert_weights(j) for j in range(DEPTH)] if l == 0 else None,
                                   self.phase_attn_prompt(l, "A")),
                      lambda l=l: self.phase_attn_prompt(l, "B"),
                      lambda l=l: self.phase_attn_sample(l, "A"),
                      lambda l=l: self.phase_attn_sample(l, "B"),
                      lambda l=l: self.phase_C(l),
                      lambda l=l: self.phase_D(l)]
        for st in steps[:upto]:
            st()
        if self.wplan is None:
            return None, self.wlog
        info = self.f.emit()
        return self.nc, info


def _t5_bucket_np(rel):
    nb = 16
    max_exact = 8
    bucket = (rel > 0).astype(np.int64) * nb
    n = np.abs(rel)
    nf = np.maximum(n, 1).astype(np.float32)
    large = max_exact + (np.log(nf / max_exact) / math.log(128 / max_exact) * (nb - max_exact)).astype(np.int64)
    large = np.minimum(large, nb - 1)
    return bucket + np.where(n < max_exact, n, large)


def _static_tables():
    k = np.arange(128)[:, None]
    c = np.arange(256)[None, :]
    rel = k - c
    idxA = _t5_bucket_np(rel)
    idxB = np.clip(rel, -128, 128) + 128
    q = np.arange(128)[None, :]
    mdiag = ((k // CHUNK) <= (q // CHUNK)).astype(np.float32)
    mask = np.concatenate([mdiag, np.ones((128, 128), np.float32)], axis=1)
    return idxA, idxB, mask


def _bc(v, n=128):
    return np.ascontiguousarray(np.broadcast_to(v[..., None, :], v.shape[:-1] + (n, v.shape[-1])))


_CACHE = {}


def _get_program(S, upto=99):
    if S not in _CACHE:
        _, wplan = Builder(S=S, TA=min(1024, S), TCD=min(512, S)).build(upto)
        b = Builder(S=S, TA=min(1024, S), TCD=min(512, S), wplan=wplan)
        nc, info = b.build(upto)
        _CACHE[S] = nc
    return _CACHE[S]


def make_in_maps(inp, ncores, S):
    f32 = np.float32
    idxA, idxB, mask = _static_tables()
    t5 = np.asarray(inp["t5_bias"], f32)
    relb = np.asarray(inp["rel_bias"], f32)
    biasA = np.ascontiguousarray(np.transpose(t5[idxA], (0, 2, 1)))
    biasB = np.ascontiguousarray(np.transpose(relb[:, idxB], (0, 1, 3, 2)))
    c15A = _bc(t5[15])
    c0B = _bc(relb[:, 0, :])
    lamv = np.stack([inp["lambda_q1"], inp["lambda_k1"], inp["lambda_q2"], inp["lambda_k2"]], axis=1)
    shared = {
        "w_mod": np.asarray(inp["w_mod"], f32), "bmod": _bc(np.asarray(inp["b_mod"], f32)),
        "w_in": np.asarray(inp["w_in"], f32), "lamv": _bc(np.asarray(lamv, f32)),
        "subln": _bc(np.asarray(inp["subln_g"], f32)), "biasA": biasA, "c15A": c15A, "biasB": biasB, "c0B": c0B,
        "mask01": mask, "ident": np.eye(128, dtype=f32),
        "w_oa": np.asarray(inp["w_oa"], f32), "w_ob": np.asarray(inp["w_ob"], f32),
        "w_out": np.asarray(inp["w_out"], f32),
        "ln1g": _bc(np.asarray(inp["ln1_g"], f32)), "ln1b": _bc(np.asarray(inp["ln1_b"], f32)),
        "w1": np.asarray(inp["w1"], f32), "w3": np.asarray(inp["w3"], f32), "w2": np.asarray(inp["w2"], f32),
        "ln2g": _bc(np.asarray(inp["ln2_g"], f32)), "ln2b": _bc(np.asarray(inp["ln2_b"], f32)),
    }
    maps = []
    for i in range(ncores):
        sb = slice(4 * i, 4 * i + 4)
        cs = np.asarray(inp["c_sample"][sb], f32)
        cbc = np.stack([np.broadcast_to(np.asarray(inp["c_prompt"][i], f32)[None, :], (128, D)),
                        np.repeat(cs, 32, axis=0)], axis=0)
        m = dict(shared)
        m.update({
            "xp": np.ascontiguousarray(inp["x_prompt"][i], dtype=f32),
            "xs": np.ascontiguousarray(np.asarray(inp["x_sample"][sb], f32).reshape(128, D)),
            "cak": np.ascontiguousarray(np.asarray(inp["cache_a_k"][:, sb], f32).reshape(DEPTH, 4, -1, 1024)),
            "cav": np.ascontiguousarray(np.asarray(inp["cache_a_v"][:, sb], f32).reshape(DEPTH, 4, -1, 1024)),
            "cbk": np.ascontiguousarray(np.asarray(inp["cache_b_k"][:, sb], f32).reshape(DEPTH, 4, -1, 1024)),
            "cbv": np.ascontiguousarray(np.asarray(inp["cache_b_v"][:, sb], f32).reshape(DEPTH, 4, -1, 1024)),
            "cbc": np.ascontiguousarray(cbc),
        })
        maps.append(m)
    return maps


def assemble(results, ncores, S):
    L = DEPTH
    B = ncores
    bo = min(512, S)
    y_p = np.stack([r["y_p"] for r in results], 0)
    y_s = np.concatenate([r["y_s"].reshape(4, 32, D) for r in results], 0)
    akp = np.stack([r["akp"] for r in results], 1).reshape(L, B, S, 8, 2, 64)
    avp = np.stack([r["avp"] for r in results], 1).reshape(L, B, S, 8, 128)
    bkp = np.stack([r["bkp"] for r in results], 1).reshape(L, B, bo, 8, 128)
    bvp = np.stack([r["bvp"] for r in results], 1).reshape(L, B, bo, 8, 128)
    aks = np.concatenate([r["aks"].reshape(L, 4, 32, 1024) for r in results], 1).reshape(L, 4 * B, 32, 8, 2, 64)
    avs = np.concatenate([r["avs"].reshape(L, 4, 32, 1024) for r in results], 1).reshape(L, 4 * B, 32, 8, 128)
    bks = np.concatenate([r["bks"].reshape(L, 4, 32, 1024) for r in results], 1).reshape(L, 4 * B, 32, 8, 128)
    bvs = np.concatenate([r["bvs"].reshape(L, 4, 32, 1024) for r in results], 1).reshape(L, 4 * B, 32, 8, 128)
    return tuple(np.ascontiguousarray(a, dtype=np.float32) for a in (y_p, y_s, akp, avp, bkp, bvp, aks, avs, bks, bvs))


def kernel(**inputs):
    S = inputs["x_prompt"].shape[1]
    ncores = inputs["x_prompt"].shape[0]
    nc = _get_program(S)
    maps = make_in_maps(inputs, ncores, S)
    res = run_bass_kernel_spmd(nc, maps, core_ids=list(range(ncores)))
    return assemble(res.results, ncores, S)
```

```python
import math
import numpy as np
from contextlib import ExitStack
import concourse.bass as bass
import concourse.mybir as mybir
from concourse.bass_utils import run_bass_kernel_spmd

F32 = mybir.dt.float32
BF16 = mybir.dt.bfloat16
AF = mybir.ActivationFunctionType
ALU = mybir.AluOpType
AX = mybir.AxisListType

D = 2048
KC = 16
DFF = 5632
FC = 44
INW = 10240
DEPTH = 2
ALPHA = float((2 * DEPTH) ** 0.25)
LN_EPS = 1e-5
CHUNK = 64
NCORES = 8

ENGS = ("pe", "act", "dve", "pool", "sp")
N_DMA_SEMS = {"sp": 30, "act": 8, "pool": 30}
SEM_WRAP = 30000
SQ = "pool"


class Op:
    __slots__ = ("idx", "eng", "fn", "dma", "deps", "needs_inc", "ticket", "dsem", "dval", "pre")

    def __init__(self, idx, eng, fn, dma):
        self.idx = idx
        self.eng = eng
        self.fn = fn
        self.dma = dma
        self.deps = ()
        self.needs_inc = False
        self.ticket = None
        self.dsem = None
        self.dval = 0
        self.pre = None


class Fw:
    def __init__(self, nc):
        self.nc = nc
        self.ops = []
        self.lastw = {}
        self.readers = {}

    def op(self, eng, fn, reads=(), writes=(), dma=False, barrier=False):
        if not barrier and "ARENA" in writes:
            writes = [w for w in writes if w != "ARENA"]
            reads = list(reads) + ["ARENA"]
        idx = len(self.ops)
        o = Op(idx, eng, fn, dma)
        deps = set()
        for r in reads:
            lw = self.lastw.get(r)
            if lw is not None:
                deps.add(lw)
        for w in writes:
            lw = self.lastw.get(w)
            if lw is not None:
                deps.add(lw)
            rd = self.readers.get(w)
            if rd:
                deps.update(rd[0].values())
                deps.update(rd[1])
        for r in reads:
            rd = self.readers.get(r)
            if rd is None:
                rd = self.readers[r] = ({}, [])
            if dma:
                rd[1].append(idx)
            else:
                rd[0][eng] = idx
        for w in writes:
            self.lastw[w] = idx
            self.readers[w] = ({}, [])
        keep = []
        for d in deps:
            y = self.ops[d]
            if not y.dma:
                if y.eng == eng and not dma and eng == "pe":
                    continue
                y.needs_inc = True
            keep.append(d)
        o.deps = keep
        self.ops.append(o)
        return o

    def pe(self, fn, reads=(), writes=()):
        return self.op("pe", fn, reads, writes)

    def act(self, fn, reads=(), writes=()):
        return self.op("act", fn, reads, writes)

    def dve(self, fn, reads=(), writes=()):
        return self.op("dve", fn, reads, writes)

    def pool(self, fn, reads=(), writes=()):
        return self.op("pool", fn, reads, writes)

    def dma(self, q, out, in_, reads=(), writes=()):
        return self.op(q, lambda e: e.dma_start(out=out, in_=in_), reads, writes, dma=True)

    def emit(self):
        nc = self.nc
        with ExitStack() as es:
            counts = {e: 0 for e in ENGS}
            for o in self.ops:
                if not o.dma and o.needs_inc:
                    counts[o.eng] += 1
                    o.ticket = counts[o.eng]
            esems = {}
            for e in ENGS:
                n = counts[e] // SEM_WRAP + 1
                esems[e] = [es.enter_context(nc.semaphore(f"s_{e}_{i}")) for i in range(n)]
            dsems = {q: [es.enter_context(nc.semaphore(f"d_{q}_{i}")) for i in range(n)]
                     for q, n in N_DMA_SEMS.items()}
            dcount = {q: 0 for q in N_DMA_SEMS}
            dhist = {q: [] for q in N_DMA_SEMS}
            for o in self.ops:
                if o.dma:
                    q = o.eng
                    k = dcount[q]
                    P = N_DMA_SEMS[q]
                    o.dsem = dsems[q][k % P]
                    o.dval = 16 * (k // P + 1)
                    if k >= P:
                        o.pre = dhist[q][k - P]
                    dhist[q].append(o)
                    dcount[q] += 1

            def target(y):
                if y.dma:
                    return y.dsem, y.dval
                t = y.ticket
                ep = (t - 1) // SEM_WRAP
                return esems[y.eng][ep], t - ep * SEM_WRAP

            per_eng = {e: [] for e in ENGS}
            for o in self.ops:
                per_eng[o.eng].append(o)
            all_dma = [o for o in self.ops if o.dma]
            ops = self.ops

            def run_engine(ename, eng):
                waited = {}
                for o in per_eng[ename]:
                    w = {}
                    deps = [ops[d] for d in o.deps]
                    if o.pre is not None:
                        deps.append(o.pre)
                    for y in deps:
                        s, v = target(y)
                        key = id(s)
                        if waited.get(key, 0) >= v:
                            continue
                        if key not in w or w[key][1] < v:
                            w[key] = (s, v)
                    for key, (s, v) in w.items():
                        eng.wait_ge(s, v)
                        waited[key] = v
                    inst = o.fn(eng)
                    if o.dma:
                        inst.then_inc(o.dsem, 16)
                    elif o.needs_inc:
                        s, v = target(o)
                        inst.then_inc(s, 1)
                if ename == "sp":
                    last = {}
                    for y in all_dma:
                        k = id(y.dsem)
                        if k not in last or last[k][1] < y.dval:
                            last[k] = (y.dsem, y.dval)
                    for key, (s, v) in last.items():
                        if waited.get(key, 0) < v:
                            eng.wait_ge(s, v)

            with nc.Block() as block:
                @block.tensor
                def _(e):
                    run_engine("pe", e)

                @block.scalar
                def _(e):
                    run_engine("act", e)

                @block.vector
                def _(e):
                    run_engine("dve", e)

                @block.gpsimd
                def _(e):
                    run_engine("pool", e)

                @block.sync
                def _(e):
                    run_engine("sp", e)
        return counts, dcount


class Tl:
    __slots__ = ("t", "r")

    def __init__(self, t, r):
        self.t = t
        self.r = r


SB_BASE = 16640
CONST_BYTES = 11 * 1024
NWB = 4
WPF = 2
WB_BYTES = 16 * 1024
WB0 = SB_BASE + CONST_BYTES
ARENA0 = WB0 + NWB * WB_BYTES
SBUF_LIMIT = 229376 - 128


class Builder:
    def __init__(self, S=4096, TA=1024, TCD=512, PAST=1024, BPAST=512, NSB=4, TS=32, wplan=None):
        self.wplan = wplan
        self.converted = set()
        self.wlog = []
        self.wissued = 0
        assert NSB * TS == 128
        self.S, self.TA, self.TCD = S, TA, TCD
        self.PAST, self.BPAST, self.NSB, self.TS = PAST, BPAST, NSB, TS
        self.NPB = S // 128
        self.NB = self.NPB + 1
        self.NT = S + 128
        self.BOUT = min(512, S)
        self.nc = bass.Bass("TRN2", target_bir_lowering=False)
        self.f = Fw(self.nc)
        self.uid = 0
        self.wcnt = 0
        self.rot = {}
        self.declare()

    def din(self, name, shape, dt=F32):
        return self.nc.dram_tensor(name, list(shape), dt, kind="ExternalInput").ap()

    def dout(self, name, shape):
        return self.nc.dram_tensor(name, list(shape), F32, kind="ExternalOutput").ap()

    def dscr(self, name, shape, dt):
        return self.nc.dram_tensor(name, list(shape), dt, kind="Internal").ap()

    def declare(self):
        S, NT = self.S, self.NT
        L = DEPTH
        self.xp = self.din("xp", [S, D])
        self.xs = self.din("xs", [128, D])
        self.cak = self.din("cak", [L, self.NSB, self.PAST, 1024])
        self.cav = self.din("cav", [L, self.NSB, self.PAST, 1024])
        self.cbk = self.din("cbk", [L, self.NSB, self.BPAST, 1024])
        self.cbv = self.din("cbv", [L, self.NSB, self.BPAST, 1024])
        self.cbc = self.din("cbc", [2, 128, D])
        self.w_mod = self.din("w_mod", [L, D, 6 * D])
        self.bmod = self.din("bmod", [L, 128, 6 * D])
        self.w_in = self.din("w_in", [L, D, INW])
        self.lamv = self.din("lamv", [L, 4, 128, 64])
        self.subln = self.din("subln", [L, 128, 128])
        self.biasA = self.din("biasA", [128, 8, 256])
        self.c15A = self.din("c15A", [128, 8])
        self.biasB = self.din("biasB", [L, 128, 8, 256])
        self.c0B = self.din("c0B", [L, 128, 8])
        self.mask01 = self.din("mask01", [128, 256])
        self.identd = self.din("ident", [128, 128])
        self.w_oa = self.din("w_oa", [L, 1024, D])
        self.w_ob = self.din("w_ob", [L, 1024, D])
        self.w_out = self.din("w_out", [L, D, D])
        self.ln1g = self.din("ln1g", [L, 128, D])
        self.ln1b = self.din("ln1b", [L, 128, D])
        self.w1 = self.din("w1", [L, D, DFF])
        self.w3 = self.din("w3", [L, D, DFF])
        self.w2 = self.din("w2", [L, DFF, D])
        self.ln2g = self.din("ln2g", [L, 128, D])
        self.ln2b = self.din("ln2b", [L, 128, D])
        self.y_p = self.dout("y_p", [S, D])
        self.y_s = self.dout("y_s", [128, D])
        self.akp = self.dout("akp", [L, S, 1024])
        self.avp = self.dout("avp", [L, S, 1024])
        self.bkp = self.dout("bkp", [L, self.BOUT, 1024])
        self.bvp = self.dout("bvp", [L, self.BOUT, 1024])
        self.aks = self.dout("aks", [L, 128, 1024])
        self.avs = self.dout("avs", [L, 128, 1024])
        self.bks = self.dout("bks", [L, 128, 1024])
        self.bvs = self.dout("bvs", [L, 128, 1024])
        self.modbc = self.dscr("modbc", [L, 2, 128, 6 * D], F32)
        self.qaT = self.dscr("qaT", [L, 1024, NT], BF16)
        self.kaT = self.dscr("kaT", [L, 1024, NT], BF16)
        self.qbT = self.dscr("qbT", [L, 1024, NT], BF16)
        self.kbT = self.dscr("kbT", [L, 1024, NT], BF16)
        self.va = self.dscr("va", [L, NT, 1024], BF16)
        self.vb = self.dscr("vb", [L, NT, 1024], BF16)
        self.gaT = self.dscr("gaT", [L, D, NT], BF16)
        self.gbT = self.dscr("gbT", [L, D, NT], BF16)
        self.oaT = self.dscr("oaT", [L, 1024, NT], BF16)
        self.obT = self.dscr("obT", [L, 1024, NT], BF16)
        self.x1 = self.dscr("x1", [L, NT, D], F32)
        self.x2 = self.dscr("x2", [L, NT, D], F32)
        def wscr(name, src):
            Kd, Nd = src.shape[1], src.shape[2]
            return (src, self.dscr(name + "_b", [L, Nd // 512, 128, Kd // 128, 512], BF16))
        self.wsc = {"w_in": wscr("w_in", self.w_in), "w_oa": wscr("w_oa", self.w_oa),
                    "w_ob": wscr("w_ob", self.w_ob), "w_out": wscr("w_out", self.w_out),
                    "w1": wscr("w1", self.w1), "w3": wscr("w3", self.w3), "w2": wscr("w2", self.w2)}
        self.ps = [self.nc.alloc_psum_tensor(f"psb{i}", [128, 512], F32) for i in range(8)]
        self.coff = SB_BASE
        self.ident = self.ctile("ident", [128, 128], BF16)
        self.c15 = self.ctile("c15", [128, 8], F32)
        self.c0b = self.ctile("c0b", [128, L * 8], F32)
        self.zero = self.ctile("zero", [128, 1], F32)
        self.negl = self.ctile("negl", [128, L], F32)
        self.gsub = self.ctile("gsub", [128, L, 128], F32)
        self.cT = [self.ctile(f"cT{g}", [128, KC, 128], BF16) for g in range(2)]
        self.dummy = self.ctile("dummy", [128, 8], F32)
        assert self.coff <= WB0, self.coff
        self.wb = []
        for i in range(NWB):
            t = self.nc.alloc_sbuf_tensor_at(f"wb{i}", [128, 16, 512], BF16, offset=WB0 + i * WB_BYTES)
            self.wb.append(Tl(t, [f"wb{i}"]))
        self.aoff = ARENA0

    def _alloc(self, name, shape, dt, off):
        self.uid += 1
        return self.nc.alloc_sbuf_tensor_at(f"{name}_{self.uid}", list(shape), dt, offset=off)

    @staticmethod
    def _nbytes(shape, dt):
        n = 1
        for s in shape[1:]:
            n *= s
        return n * (2 if dt == BF16 else 4)

    def ctile(self, name, shape, dt):
        nb = (self._nbytes(shape, dt) + 31) // 32 * 32
        t = self._alloc(name, shape, dt, self.coff)
        self.coff += nb
        return Tl(t, [name])

    def atile(self, name, shape, dt):
        nb = (self._nbytes(shape, dt) + 31) // 32 * 32
        assert self.aoff + nb <= SBUF_LIMIT, (name, self.aoff, nb)
        t = self._alloc(name, shape, dt, self.aoff)
        self.aoff += nb
        self.uid += 1
        return Tl(t, [f"{name}#{self.uid}", "ARENA"])

    def atiles(self, name, n, shape, dt):
        return [self.atile(f"{name}{i}", shape, dt) for i in range(n)]

    def nxt(self, lst):
        k = id(lst)
        i = self.rot.get(k, 0)
        self.rot[k] = i + 1
        return lst[i % len(lst)]

    def phase(self):
        f = self.f
        d = self.dummy
        f.op("dve", lambda e: e.memset(d.t[:, 0:1], 0.0), reads=[], writes=["ARENA"] + d.r, barrier=True)
        self.aoff = ARENA0

    def mm(self, out, lhsT, rhs, start, stop, reads, writes):
        self.f.pe(lambda e: e.matmul(out, lhsT, rhs, start=start, stop=stop), reads, writes)

    def tr(self, out, in_, reads, writes):
        idn = self.ident
        npart = in_.shape[0]
        self.f.pe(lambda e: e.transpose(out, in_, idn.t[0:npart, 0:npart]), list(reads) + idn.r, writes)

    def act(self, out, in_, func, reads, writes, bias=0.0, scale=1.0):
        self.f.act(lambda e: e.activation(out=out, in_=in_, func=func, bias=bias, scale=scale), reads, writes)

    def tt(self, eng, out, in0, in1, op, reads, writes):
        self.f.op(eng, lambda e: e.tensor_tensor(out, in0, in1, op), reads, writes)

    def ts(self, eng, out, in0, s1, s2, op0, op1, reads, writes):
        if op1 is None:
            self.f.op(eng, lambda e: e.tensor_scalar(out, in0, s1, None, op0), reads, writes)
        else:
            self.f.op(eng, lambda e: e.tensor_scalar(out, in0, s1, s2, op0, op1), reads, writes)

    def stt(self, eng, out, in0, scalar, in1, op0, op1, reads, writes):
        self.f.op(eng, lambda e: e.scalar_tensor_tensor(out, in0, scalar, in1, op0, op1), reads, writes)

    def cp(self, eng, out, in_, reads, writes):
        if eng == "act":
            self.f.act(lambda e: e.copy(out, in_), reads, writes)
        else:
            self.f.op(eng, lambda e: e.tensor_copy(out, in_), reads, writes)

    def psr(self, i):
        return [f"ps{i}"]

    def _issue_w(self, k, spec):
        kind, name, l, k0, nk, c0, ncols = spec
        wb = self.wb[k % NWB]
        if kind == "cast":
            W = {"w_mod": self.w_mod, "w_in": self.w_in}[name][l]
            rd = []
        else:
            assert ncols == 512 and c0 % 512 == 0
            src = self.wsc[name][1][l, c0 // 512, :, k0:k0 + nk, :]
            self.f.dma("pool", wb.t[:, 0:nk, 0:ncols], src, reads=[(name + "_b", l, c0 // 512)], writes=wb.r)
            return
        src = W[k0 * 128:(k0 + nk) * 128, c0:c0 + ncols].rearrange("(kc p) n -> p kc n", p=128)
        self.f.dma("pool", wb.t[:, 0:nk, 0:ncols], src, reads=rd, writes=wb.r)

    def _req_w(self, spec):
        k = self.wcnt
        self.wcnt += 1
        if self.wplan is None:
            self.wlog.append(spec)
            self._issue_w(k, spec)
        else:
            assert self.wplan[k] == spec, (k, spec, self.wplan[k])
            while self.wissued < min(k + 1 + WPF, len(self.wplan)):
                nx = self.wplan[self.wissued]
                if nx[0] == "bf16" and (nx[1], nx[2]) not in self.converted and self.wissued > k:
                    break
                self._issue_w(self.wissued, nx)
                self.wissued += 1
        return self.wb[k % NWB]

    def load_w(self, name, l, k0, nk, c0, ncols):
        return self._req_w(("cast", name, l, k0, nk, c0, ncols))

    def convert_weights(self, l, names):
        for name in names:
            self.converted.add((name, l))
            src, dst = self.wsc[name]
            ncol = src.shape[2]
            for c in range(ncol // 512):
                self.f.dma("pool", dst[l, c], src[l, :, c * 512:(c + 1) * 512].rearrange("(kc p) n -> p kc n", p=128),
                           reads=[], writes=[(name + "_b", l, c)])

    def load_wb(self, name, l, k0, nk, c0, ncols):
        return self._req_w(("bf16", name, l, k0, nk, c0, ncols))

    def grp(self, blk):
        return 0 if blk < self.NPB else 1

    def x_src(self, l, blk):
        if l == 0:
            if blk < self.NPB:
                return self.xp[blk * 128:(blk + 1) * 128, :], []
            return self.xs[:, :], []
        return self.x2[l - 1, blk * 128:(blk + 1) * 128, :], [("x2", l - 1, blk, c) for c in range(4)]

    def supertiles(self, T):
        out = []
        nb = T // 128
        b = 0
        while b < self.NPB:
            n = min(nb, self.NPB - b)
            out.append((b, n))
            b += n
        out.append((self.NPB, 1))
        return out

    def ln_stats(self, xt, sm):
        st, mv, rs, nm = sm["st"], sm["mv"], sm["rs"], sm["nm"]
        for c in range(4):
            self.f.dve(lambda e, c=c: e.bn_stats(st.t[:, c, :], xt.t[:, c * 512:(c + 1) * 512]), xt.r, st.r)
        self.f.dve(lambda e: e.bn_aggr(mv.t[:, :], st.t[:, :, :]), st.r, mv.r)
        self.ts("dve", rs.t[:, :], mv.t[:, 1:2], LN_EPS, None, ALU.add, None, mv.r, rs.r)
        self.act(rs.t[:, :], rs.t[:, :], AF.Sqrt, rs.r, rs.r)
        self.f.dve(lambda e: e.reciprocal(rs.t[:, :], rs.t[:, :]), rs.r, rs.r)
        self.stt("dve", nm.t[:, :], mv.t[:, 0:1], -1.0, rs.t[:, :], ALU.mult, ALU.mult, mv.r + rs.r, nm.r)
        return rs, nm

    def small_set(self, name):
        return {"st": self.atile(name + "st", [128, 4, 6], F32), "mv": self.atile(name + "mv", [128, 2], F32),
                "rs": self.atile(name + "rs", [128, 1], F32), "nm": self.atile(name + "nm", [128, 1], F32)}

    def tok2feat(self, src, src_ap_fn, n, dst_fn, dst_r):
        c = 0
        while c < n:
            m = min(8, n - c)
            bank = 6 + (self.rot.setdefault("trb", 0) % 2)
            self.rot["trb"] += 1
            pt = self.ps[bank][:, :].bitcast(BF16).rearrange("p (a b) -> p a b", b=128)
            for j in range(m):
                a = src_ap_fn(c + j)
                npart = a.shape[0]
                self.tr(pt[:, j, 0:npart], a, src.r, self.psr(bank))
            npart = src_ap_fn(c).shape[0]
            eng = "act" if (self.rot["trb"] % 2) else "dve"
            self.cp(eng, dst_fn(c, m), pt[:, 0:m, 0:npart], self.psr(bank), dst_r)
            c += m

    def setup(self):
        f = self.f
        L = DEPTH
        self.phase()
        idf = self.atile("idf", [128, 128], F32)
        f.dma("sp", idf.t[:, :], self.identd[:, :], writes=idf.r)
        self.cp("dve", self.ident.t[:, :], idf.t[:, :], idf.r, self.ident.r)
        f.dma("sp", self.c15.t[:, :], self.c15A[:, :], writes=self.c15.r)
        for l in range(L):
            f.dma("sp", self.c0b.t[:, l * 8:(l + 1) * 8], self.c0B[l], writes=self.c0b.r)
        f.dve(lambda e: e.memset(self.zero.t[:, :], 0.0), [], self.zero.r)
        lv = self.atile("lv", [128, 4, 64], F32)
        pr = self.atile("pr", [128, 2, 64], F32)
        sm2 = self.atile("sm2", [128, 2], F32)
        ex2 = self.atile("ex2", [128, 2], F32)
        sg = self.atile("sg", [128, 128], F32)
        for l in range(L):
            lam_init = 0.8 - 0.6 * math.exp(-0.3 * l)
            for j in range(4):
                f.dma("sp", lv.t[:, j, :], self.lamv[l, j], writes=lv.r)
            self.tt("dve", pr.t[:, 0, :], lv.t[:, 0, :], lv.t[:, 1, :], ALU.mult, lv.r, pr.r)
            self.tt("dve", pr.t[:, 1, :], lv.t[:, 2, :], lv.t[:, 3, :], ALU.mult, lv.r, pr.r)
            f.dve(lambda e: e.reduce_sum(sm2.t[:, :], pr.t[:, :, :], axis=AX.X), pr.r, sm2.r)
            self.act(ex2.t[:, :], sm2.t[:, :], AF.Exp, sm2.r, ex2.r)
            self.tt("dve", sm2.t[:, 0:1], ex2.t[:, 0:1], ex2.t[:, 1:2], ALU.subtract, ex2.r, sm2.r)
            self.ts("dve", self.negl.t[:, l:l + 1], sm2.t[:, 0:1], lam_init, -1.0, ALU.add, ALU.mult,
                    sm2.r, self.negl.r)
            f.dma("sp", sg.t[:, :], self.subln[l], writes=sg.r)
            self.ts("dve", self.gsub.t[:, l, :], sg.t[:, :], 1.0 - lam_init, None, ALU.mult, None,
                    sg.r, self.gsub.r)
        crow = self.atiles("crow", 2, [128, D], F32)
        cbf = self.atiles("cbf", 2, [128, D], BF16)
        for g in range(2):
            f.dma("sp", crow[g].t[:, :], self.cbc[g], writes=crow[g].r)
            self.act(cbf[g].t[:, :], crow[g].t[:, :], AF.Silu, crow[g].r, cbf[g].r)
            cb, ct = cbf[g], self.cT[g]
            self.tok2feat(cb, lambda c, cb=cb: cb.t[:, c * 128:(c + 1) * 128], KC,
                          lambda c0, m, ct=ct: ct.t[:, c0:c0 + m, :], ct.r)
        bt = self.atiles("bt", 2, [128, 512], F32)
        ms = self.atiles("ms", 3, [128, 512], F32)
        for l in range(L):
            for n in range(24):
                wt = self.load_w("w_mod", l, 0, KC, n * 512, 512)
                b = self.nxt(bt)
                f.dma("sp", b.t[:, :], self.bmod[l, :, n * 512:(n + 1) * 512], writes=b.r)
                piece = n // 4
                for g in range(2):
                    bank = self.rot.setdefault("gb", 0) % 4
                    self.rot["gb"] += 1
                    ps = self.ps[bank]
                    for kc in range(KC):
                        self.mm(ps[:, :], self.cT[g].t[:, kc, :], wt.t[:, kc, :], kc == 0, kc == KC - 1,
                                self.cT[g].r + wt.r, self.psr(bank))
                    m = self.nxt(ms)
                    if piece in (1, 4):
                        self.stt("dve", m.t[:, :], ps[:, :], 1.0, b.t[:, :], ALU.add, ALU.add,
                                 self.psr(bank) + b.r, m.r)
                    else:
                        self.tt("dve", m.t[:, :], ps[:, :], b.t[:, :], ALU.add, self.psr(bank) + b.r, m.r)
                    f.dma(SQ, self.modbc[l, g, :, n * 512:(n + 1) * 512], m.t[:, :], reads=m.r,
                          writes=[("modbc", l, g, n)])

    def mod_piece(self, l, g, piece, c4):
        n = piece * 4 + c4
        return self.modbc[l, g, :, n * 512:(n + 1) * 512], [("modbc", l, g, n)]

    def ln_mod_T(self, l, blk, xt, sm, pieces, tmp, hb, piece_sc, piece_sh, dst_fn, dst_r, defer=False):
        f = self.f
        g = self.grp(blk)
        rs, nm = self.ln_stats(xt, sm)
        for c4 in range(4):
            cs = slice(c4 * 512, (c4 + 1) * 512)
            psc, psh = self.nxt(pieces), self.nxt(pieces)
            a, r = self.mod_piece(l, g, piece_sc, c4)
            f.dma("sp", psc.t[:, :], a, reads=r, writes=psc.r)
            a, r = self.mod_piece(l, g, piece_sh, c4)
            f.dma("sp", psh.t[:, :], a, reads=r, writes=psh.r)
            t = self.nxt(tmp)
            self.act(t.t[:, :], xt.t[:, cs], AF.Identity, xt.r + rs.r + nm.r, t.r,
                     bias=nm.t[:, 0:1], scale=rs.t[:, 0:1])
            self.tt("dve", t.t[:, :], t.t[:, :], psc.t[:, :], ALU.mult, t.r + psc.r, t.r)
            self.tt("dve", hb.t[:, cs], t.t[:, :], psh.t[:, :], ALU.add, t.r + psh.r, hb.r)
        def tpart():
            self.tok2feat(hb, lambda c: hb.t[:, c * 128:(c + 1) * 128], KC, dst_fn, dst_r)
        if defer:
            return tpart
        tpart()

    def phase_A(self, l):
        f = self.f
        segs = [("qa", 0, 1024), ("ka", 1024, 2048), ("va", 2048, 3072), ("qb", 3072, 4096),
                ("kb", 4096, 5120), ("vb", 5120, 6144), ("ga", 6144, 8192), ("gb", 8192, 10240)]
        fm_dst = {"qa": self.qaT, "qb": self.qbT, "ga": self.gaT, "gb": self.gbT}
        self.phase()
        hTs = self.atiles("hT", 2, [128, KC, self.TA], BF16)
        xrow = self.atiles("xrow", 2, [128, D], F32)
        hbs = self.atiles("hb", 2, [128, D], BF16)
        pieces = self.atiles("pc", 4, [128, 512], F32)
        tmp = self.atiles("tmp", 2, [128, 512], F32)
        sms = [self.small_set(f"sm{i}") for i in range(2)]
        st32 = self.atiles("st32", 3, [128, 512], F32)
        st16 = self.atiles("st16", 4, [128, 512], BF16)
        kst = self.atiles("kst", 2, [128, 4, 128], BF16)
        sts = self.supertiles(self.TA)

        def ln_block(si, tb, defer):
            b0_, nb_ = sts[si]
            hT_ = hTs[si % 2]
            blk = b0_ + tb
            xt = self.nxt(xrow)
            a, r = self.x_src(l, blk)
            f.dma("sp", xt.t[:, :], a, reads=r, writes=xt.r)
            return self.ln_mod_T(l, blk, xt, self.nxt(sms), pieces, tmp, self.nxt(hbs), 1, 0,
                                 lambda c0, m: hT_.t[:, c0:c0 + m, tb * 128:(tb + 1) * 128], hT_.r, defer=defer)

        for tb in range(sts[0][1]):
            ln_block(0, tb, False)
        pendk = [None]
        for si, (b0, nb) in enumerate(sts):
            T = nb * 128
            hT = hTs[si % 2]
            nb_next = sts[si + 1][1] if si + 1 < len(sts) else 0
            pend_tr = None
            for ctile in range(INW // 512):
                c0 = ctile * 512
                name, s0, s1 = [s for s in segs if s[1] <= c0 < s[2]][0]
                new_tr = ln_block(si + 1, ctile, True) if ctile < nb_next else None
                wt = self.load_wb("w_in", l, 0, KC, c0, 512)
                if name in fm_dst:
                    dst = fm_dst[name]
                    for sub in range(4):
                        row0 = c0 - s0 + sub * 128
                        rc = row0 // 128
                        for t0 in range(0, T, 512):
                            N = min(512, T - t0)
                            bank = self.rot.setdefault("gb", 0) % 4
                            self.rot["gb"] += 1
                            ps = self.ps[bank]
                            for kc in range(KC):
                                self.mm(ps[:, 0:N], wt.t[:, kc, sub * 128:(sub + 1) * 128], hT.t[:, kc, t0:t0 + N],
                                        kc == 0, kc == KC - 1, wt.r + hT.r, self.psr(bank))
                            s = self.nxt(st16)
                            if name in ("ga", "gb"):
                                self.act(s.t[:, 0:N], ps[:, 0:N], AF.Sigmoid, self.psr(bank), s.r)
                            else:
                                self.cp("dve", s.t[:, 0:N], ps[:, 0:N], self.psr(bank), s.r)
                            tok0 = b0 * 128 + t0
                            res = [(name + "T", l, rc, tok0 // 128 + i) for i in range(N // 128)]
                            f.dma(SQ, dst[l, row0:row0 + 128, tok0:tok0 + N], s.t[:, 0:N], reads=s.r, writes=res)
                else:
                    cc = (c0 - s0) // 512
                    for tb in range(nb):
                        blk = b0 + tb
                        bank = self.rot.setdefault("gb", 0) % 4
                        self.rot["gb"] += 1
                        ps = self.ps[bank]
                        for kc in range(KC):
                            self.mm(ps[:, :], hT.t[:, kc, tb * 128:(tb + 1) * 128], wt.t[:, kc, :],
                                    kc == 0, kc == KC - 1, wt.r + hT.r, self.psr(bank))
                        odst = None
                        if blk < self.NPB:
                            if name == "ka":
                                odst = self.akp[l, blk * 128:(blk + 1) * 128, c0 - s0:c0 - s0 + 512]
                            elif name == "va":
                                odst = self.avp[l, blk * 128:(blk + 1) * 128, c0 - s0:c0 - s0 + 512]
                            elif blk * 128 >= self.S - self.BOUT:
                                r0 = blk * 128 - (self.S - self.BOUT)
                                o = self.bkp if name == "kb" else self.bvp
                                odst = o[l, r0:r0 + 128, c0 - s0:c0 - s0 + 512]
                        else:
                            o = {"ka": self.aks, "va": self.avs, "kb": self.bks, "vb": self.bvs}[name]
                            odst = o[l, :, c0 - s0:c0 - s0 + 512]
                        if odst is not None:
                            s32 = self.nxt(st32)
                            self.cp("act", s32.t[:, :], ps[:, :], self.psr(bank), s32.r)
                            f.dma(SQ, odst, s32.t[:, :], reads=s32.r)
                            s = self.nxt(st16)
                            self.cp("dve", s.t[:, :], s32.t[:, :], s32.r, s.r)
                        else:
                            s = self.nxt(st16)
                            self.cp("dve", s.t[:, :], ps[:, :], self.psr(bank), s.r)
                        if name in ("va", "vb"):
                            dst = self.va if name == "va" else self.vb
                            f.dma(SQ, dst[l, blk * 128:(blk + 1) * 128, c0 - s0:c0 - s0 + 512], s.t[:, :],
                                  reads=s.r, writes=[(name, l, blk, cc)])
                        else:
                            def ktr(s=s, name=name, rows=c0 - s0, blk=blk):
                                k = self.nxt(kst)
                                self.tok2feat(s, lambda c: s.t[:, c * 128:(c + 1) * 128], 4,
                                              lambda c0_, m: k.t[:, c0_:c0_ + m, :], k.r)
                                dst = self.kaT if name == "ka" else self.kbT
                                f.dma(SQ, dst[l, rows:rows + 512, blk * 128:(blk + 1) * 128]
                                      .rearrange("(c p) s -> p c s", p=128), k.t[:, :, :], reads=k.r,
                                      writes=[(name + "T", l, rows // 128 + i, blk) for i in range(4)])
                            if pendk[0] is not None:
                                pendk[0]()
                            pendk[0] = ktr
                if pendk[0] is not None:
                    pendk[0]()
                    pendk[0] = None
                if pend_tr is not None:
                    pend_tr()
                pend_tr = new_tr
            if pend_tr is not None:
                pend_tr()
        if pendk[0] is not None:
            pendk[0]()

    def load_E(self, l, which):
        f = self.f
        nt = 2 if which == "A" else 1
        E = self.atile("E" + which, [128, 8, 2, nt, 128], F32)
        braw = self.atile("braw", [128, 8, 256], F32)
        msk = self.atile("msk", [128, 256], F32)
        src = self.biasA if which == "A" else self.biasB[l]
        f.dma("sp", braw.t[:, :, :], src, writes=braw.r)
        f.dma("sp", msk.t[:, :], self.mask01[:, :], writes=msk.r)
        self.act(braw.t[:, :, :], braw.t[:, :, :], AF.Exp, braw.r, braw.r)
        for h in range(8):
            for t in range(nt):
                self.tt("dve", E.t[:, h, :, t, :], braw.t[:, h, :].rearrange("p (b q) -> p b q", q=128),
                        msk.t[:, :].rearrange("p (b q) -> p b q", q=128), ALU.mult, braw.r + msk.r, E.r)
        return E

    def attn_core(self, which, l, h, qT_ap, q_r, nq, blocks, E, cfar_ap, par, o_dst, o_r, P3, wk, pend=None,
                  after=None, sbanks=(0, 1)):
        f = self.f
        nt = 2 if which == "A" else 1
        scale = 0.125 if which == "A" else 128 ** -0.5
        near = [b for b in blocks if b["kind"] in ("diag", "prev")]
        far = [b for b in blocks if b["kind"] in ("far", "far4")]
        per = 2 if which == "A" else 4
        groups = []
        if near:
            nks = sorted(set(b["nk"] for b in near))
            for nk in nks:
                groups.append(("near", [b for b in near if b["nk"] == nk]))
        n_near = len(groups)
        for i in range(0, len(far), per):
            groups.append(("far", far[i:i + per]))
        accb = [2 + 2 * par + t for t in range(nt)]
        ntot = len(blocks)
        seen = [0]

        def s_view(bank):
            return self.ps[bank][:, :].rearrange("p (b t q) -> p b t q", t=nt, q=128)

        nsb = len(sbanks)

        def do_qk(gi):
            kind, grp = groups[gi]
            bank = sbanks[gi % nsb]
            S4 = s_view(bank)
            for bi, b in enumerate(grp):
                nk = b["nk"]
                for t in range(nt):
                    lhsT = b["kT"][:, 0:nk]
                    rhs = qT_ap[:, t, 0:nq]
                    self.mm(S4[0:nk, bi, t, 0:nq], lhsT, rhs, True, True, b["r"] + q_r, self.psr(bank))

        def do_rest(gi):
            kind, grp = groups[gi]
            bank = sbanks[gi % nsb]
            S4 = s_view(bank)
            P = self.nxt(P3)
            P4 = P.t[:, :].rearrange("p (b t q) -> p b t q", t=nt, q=128)
            nk = grp[0]["nk"]
            nbk = len(grp)
            if kind == "far":
                self.act(P4[0:nk, 0:nbk, :, 0:nq], S4[0:nk, 0:nbk, :, 0:nq], AF.Exp, self.psr(bank), P.r,
                         bias=cfar_ap[0:nk, :], scale=scale)
                for bi, b in enumerate(grp):
                    if b["kind"] == "far4" and nq > 64:
                        f.op("dve", lambda e, bi=bi: e.memset(P4[0:64, bi, :, 64:nq], 0.0), P.r, P.r)
            else:
                self.act(P4[0:nk, 0:nbk, :, 0:nq], S4[0:nk, 0:nbk, :, 0:nq], AF.Exp, self.psr(bank), P.r,
                         bias=self.zero.t[0:nk, :], scale=scale)
                for bi, b in enumerate(grp):
                    eb = 0 if b["kind"] == "diag" else 1
                    self.tt("dve", P4[0:nk, bi, :, 0:nq], P4[0:nk, bi, :, 0:nq], E.t[0:nk, h, eb, :, 0:nq],
                            ALU.mult, P.r + E.r, P.r)
            for bi, b in enumerate(grp):
                nk = b["nk"]
                for t in range(nt):
                    self.mm(self.ps[accb[t]][0:nq, 0:129], P4[0:nk, bi, t, 0:nq], b["v"][0:nk, 0:129],
                            seen[0] == 0, seen[0] == ntot - 1, P.r + b["r"], self.psr(accb[t]))
                seen[0] += 1

        ng = len(groups)
        la = nsb - 1
        for g0 in range(min(la, ng)):
            do_qk(g0)
        for gi in range(ng):
            if gi + la < ng:
                do_qk(gi + la)
            do_rest(gi)
            if gi == n_near - 1 and pend is not None:
                pend[0]()
        if pend is not None:
            if n_near == 0:
                pend[0]()
            pend[1]()

        def fin_dve():
            self._attn_norm(which, l, nq, nt, accb, wk)

        def fin_pe():
            on = wk["on"]
            if 7 in sbanks:
                bank = 6
            else:
                bank = 6 + (self.rot.setdefault("trb", 0) % 2)
                self.rot["trb"] += 1
            pt = self.ps[bank][:, :].bitcast(BF16)
            self.tr(pt[:, 0:nq], on.t[0:nq, :], on.r, self.psr(bank))
            self.cp("act", o_dst, pt[:, 0:nq], self.psr(bank), o_r)
            if after is not None:
                after()
        return (fin_dve, fin_pe)

    def _attn_norm(self, which, l, nq, nt, accb, wk):
        f = self.f
        rz, t1, o, ss, on = wk["rz"], wk["t1"], wk["o"], wk["ss"], wk["on"]
        accr = [r for t in range(nt) for r in self.psr(accb[t])]
        for t in range(nt):
            f.dve(lambda e, t=t: e.reciprocal(rz.t[0:nq, t:t + 1], self.ps[accb[t]][0:nq, 128:129]), accr, rz.r)
        if which == "A":
            self.ts("dve", t1.t[0:nq, :], self.ps[accb[1]][0:nq, 0:128], rz.t[0:nq, 1:2], self.negl.t[0:nq, l:l + 1],
                    ALU.mult, ALU.mult, accr + rz.r + self.negl.r, t1.r)
            self.stt("dve", o.t[0:nq, :], self.ps[accb[0]][0:nq, 0:128], rz.t[0:nq, 0:1], t1.t[0:nq, :],
                     ALU.mult, ALU.add, accr + rz.r + t1.r, o.r)
            self.tt("dve", t1.t[0:nq, :], o.t[0:nq, :], o.t[0:nq, :], ALU.mult, o.r + t1.r, t1.r)
            f.dve(lambda e: e.reduce_sum(ss.t[0:nq, 0:1], t1.t[0:nq, :], axis=AX.X), t1.r + ss.r, ss.r)
            self.ts("dve", ss.t[0:nq, 1:2], ss.t[0:nq, 0:1], 1.0 / 128, LN_EPS, ALU.mult, ALU.add, ss.r, ss.r)
            self.act(ss.t[0:nq, 1:2], ss.t[0:nq, 1:2], AF.Ln, ss.r, ss.r)
            self.act(ss.t[0:nq, 1:2], ss.t[0:nq, 1:2], AF.Exp, ss.r, ss.r, scale=-0.5)
            self.stt("dve", on.t[0:nq, :], o.t[0:nq, :], ss.t[0:nq, 1:2], self.gsub.t[0:nq, l, :],
                     ALU.mult, ALU.mult, o.r + ss.r + self.gsub.r, on.r)
        else:
            self.ts("dve", on.t[0:nq, :], self.ps[accb[0]][0:nq, 0:128], rz.t[0:nq, 0:1], None, ALU.mult, None,
                    accr + rz.r, on.r)

    def attn_work(self):
        wk = []
        for i in range(2):
            wk.append({"rz": self.atile("rz", [128, 2], F32), "t1": self.atile("t1", [128, 128], F32),
                       "o": self.atile("o", [128, 128], F32), "ss": self.atile("ss", [128, 2], F32),
                       "on": self.atile("on", [128, 128], BF16)})
        return wk

    def phase_attn_prompt(self, l, which):
        f = self.f
        NPB = self.NPB
        S = self.S
        A = which == "A"
        kTd, vd, qTd, oTd = (self.kaT, self.va, self.qaT, self.oaT) if A else (self.kbT, self.vb, self.qbT, self.obT)
        kn, vn, qn, on_ = ("kaT", "va", "qaT", "oaT") if A else ("kbT", "vb", "qbT", "obT")
        for hh in range(2):
            self.phase()
            E = self.load_E(l, which)
            kT = self.atile("kT", [128, 4, S], BF16)
            V1 = self.atile("V1", [128, NPB, 4, 130], BF16)
            f.op("dve", lambda e, V1=V1: e.memset(V1.t[:, :, :, 128:129], 1.0), V1.r, V1.r)
            nt = 2 if A else 1
            qTs = self.atiles("qT", 2, [128, 4, nt, 128], BF16)
            if A:
                for q_ in qTs:
                    f.op("dve", lambda e, q_=q_: e.memset(q_.t[64:128, :, 0, :], 0.0), q_.r, q_.r)
                    f.op("dve", lambda e, q_=q_: e.memset(q_.t[0:64, :, 1, :], 0.0), q_.r, q_.r)
            oTs = self.atiles("oT", 2, [128, 4, 128], BF16)
            P3 = self.atiles("P", 4, [128, 512], BF16)
            wk = self.attn_work()
            pend = None
            kres = [[f"kTb{hh}_{i}"] for i in range(NPB)]
            vres = [[f"vTb{hh}_{i}"] for i in range(NPB)]
            cnt = 0
            for i in range(NPB):
                f.dma("sp", kT.t[:, :, i * 128:(i + 1) * 128],
                      kTd[l, hh * 512:(hh + 1) * 512, i * 128:(i + 1) * 128].rearrange("(h p) s -> p h s", p=128),
                      reads=[(kn, l, hh * 4 + c, i) for c in range(4)] + ["ARENA"], writes=kres[i])
                f.dma("sp", V1.t[:, i, :, 0:128],
                      vd[l, i * 128:(i + 1) * 128, hh * 512:(hh + 1) * 512].rearrange("s (h d) -> s h d", d=128),
                      reads=[(vn, l, i, hh), "ARENA"], writes=vres[i])
                qT = self.nxt(qTs)
                qsrc = qTd[l, hh * 512:(hh + 1) * 512, i * 128:(i + 1) * 128].rearrange("(h p) s -> p h s", p=128)
                qrd = [(qn, l, hh * 4 + c, i) for c in range(4)]
                if A:
                    f.dma("sp", qT.t[0:64, :, 0, :], qsrc[0:64], reads=qrd, writes=qT.r)
                    f.dma("sp", qT.t[64:128, :, 1, :], qsrc[64:128], reads=qrd, writes=qT.r)
                else:
                    f.dma("sp", qT.t[:, :, 0, :], qsrc, reads=qrd, writes=qT.r)
                oT = self.nxt(oTs)
                for h4 in range(4):
                    h = hh * 4 + h4
                    blocks = []
                    jlo = 0 if A else max(0, i - 4)
                    for j in range(jlo, i + 1):
                        if j == i:
                            kind = "diag"
                        elif j == i - 1:
                            kind = "prev"
                        elif (not A) and j == i - 4:
                            kind = "far4"
                        else:
                            kind = "far"
                        blocks.append({"kT": kT.t[:, h4, j * 128:(j + 1) * 128], "v": V1.t[:, j, h4, :],
                                       "nk": 128, "kind": kind, "r": kres[j] + vres[j] + V1.r})
                    cfar = self.c15.t[:, h:h + 1] if A else self.c0b.t[:, l * 8 + h:l * 8 + h + 1]
                    after = None
                    if h4 == 3:
                        def after(i=i, oT=oT, hh=hh):
                            f.dma(SQ, oTd[l, hh * 512:(hh + 1) * 512, i * 128:(i + 1) * 128]
                                  .rearrange("(h p) s -> p h s", p=128),
                                  oT.t[:, :, :], reads=oT.r, writes=[(on_, l, hh * 4 + c, i) for c in range(4)])
                    pend = self.attn_core(which, l, h, qT.t[:, h4, :, :], qT.r, 128, blocks, E, cfar, cnt % 2,
                                          oT.t[:, h4, :], oT.r, P3, wk[cnt % 2], pend, after, sbanks=(0, 1, 7))
                    cnt += 1
            pend[0]()
            pend[1]()

    def phase_attn_sample(self, l, which):
        f = self.f
        A = which == "A"
        NPB, TS = self.NPB, self.TS
        past = self.PAST if A else self.BPAST
        nkb = past // 128
        kTd, vd, qTd, oTd = (self.kaT, self.va, self.qaT, self.oaT) if A else (self.kbT, self.vb, self.qbT, self.obT)
        kn, vn, qn, on_ = ("kaT", "va", "qaT", "oaT") if A else ("kbT", "vb", "qbT", "obT")
        ck, cv = (self.cak, self.cav) if A else (self.cbk, self.cbv)
        self.phase()
        E = self.load_E(l, which)
        kTs = self.atiles("kTs", 2, [128, 8, past + TS], BF16)
        V1s = self.atiles("V1s", 2, [128, nkb + 1, 8, 130], BF16)
        for v in V1s:
            f.op("dve", lambda e, v=v: e.memset(v.t[:, :, :, 128:129], 1.0), v.r, v.r)
        craw = self.atiles("craw", 2, [128, 1024], BF16)
        nt = 2 if A else 1
        qT = self.atile("qTs", [128, 8, nt, 128], BF16)
        if A:
            f.op("dve", lambda e: e.memset(qT.t[64:128, :, 0, :], 0.0), qT.r, qT.r)
            f.op("dve", lambda e: e.memset(qT.t[0:64, :, 1, :], 0.0), qT.r, qT.r)
        oT = self.atile("oTs", [128, 8, 128], BF16)
        P3 = self.atiles("P", 3, [128, 512], BF16)
        wk = self.attn_work()
        tokc = NPB * 128
        qsrc = qTd[l, :, tokc:tokc + 128].rearrange("(h p) s -> p h s", p=128)
        qrd = [(qn, l, c, NPB) for c in range(8)]
        if A:
            f.dma("sp", qT.t[0:64, :, 0, :], qsrc[0:64], reads=qrd, writes=qT.r)
            f.dma("sp", qT.t[64:128, :, 1, :], qsrc[64:128], reads=qrd, writes=qT.r)
        else:
            f.dma("sp", qT.t[:, :, 0, :], qsrc, reads=qrd, writes=qT.r)
        cnt = 0
        pend = None
        for sb in range(self.NSB):
            kT, V1 = self.nxt(kTs), self.nxt(V1s)
            for kb in range(nkb):
                cr = self.nxt(craw)
                f.dma("pool", cr.t[:, :], ck[l, sb, kb * 128:(kb + 1) * 128, :], writes=cr.r)
                self.tok2feat(cr, lambda c, cr=cr: cr.t[:, c * 128:(c + 1) * 128], 8,
                              lambda c0, m, kT=kT, kb=kb: kT.t[:, c0:c0 + m, kb * 128:(kb + 1) * 128], kT.r)
                f.dma("pool", V1.t[:, kb, :, 0:128],
                      cv[l, sb, kb * 128:(kb + 1) * 128, :].rearrange("s (h d) -> s h d", d=128), writes=V1.r)
            f.dma("sp", kT.t[:, :, past:past + TS],
                  kTd[l, :, tokc + sb * TS:tokc + (sb + 1) * TS].rearrange("(h p) s -> p h s", p=128),
                  reads=[(kn, l, c, NPB) for c in range(8)], writes=kT.r)
            f.dma("sp", V1.t[0:TS, nkb, :, 0:128],
                  vd[l, tokc + sb * TS:tokc + (sb + 1) * TS, :].rearrange("s (h d) -> s h d", d=128),
                  reads=[(vn, l, NPB, c) for c in range(2)], writes=V1.r)
            for h in range(8):
                blocks = []
                for j in range(nkb + 1):
                    if j == nkb:
                        kind, nk = "diag", TS
                    elif j == nkb - 1:
                        kind, nk = "prev", 128
                    else:
                        kind, nk = "far", 128
                    blocks.append({"kT": kT.t[:, h, j * 128:j * 128 + nk], "v": V1.t[:, j, h, :], "nk": nk,
                                   "kind": kind, "r": kT.r[:1] + V1.r})
                cfar = self.c15.t[:, h:h + 1] if A else self.c0b.t[:, l * 8 + h:l * 8 + h + 1]
                pend = self.attn_core(which, l, h, qT.t[:, h, :, sb * TS:(sb + 1) * TS], qT.r, TS, blocks, E, cfar,
                                      cnt % 2, oT.t[:, h, sb * TS:(sb + 1) * TS], oT.r, P3, wk[cnt % 2], pend)
                cnt += 1
        pend[0]()
        pend[1]()
        f.dma(SQ, oTd[l, :, tokc:tokc + 128].rearrange("(h p) s -> p h s", p=128), oT.t[:, :, :],
              reads=oT.r, writes=[(on_, l, c, NPB) for c in range(8)])

    def ln_affine_store(self, xt, sm, gsrc, bsrc, pieces, dst_fn):
        f = self.f
        rs, nm = self.ln_stats(xt, sm)
        for c4 in range(4):
            cs = slice(c4 * 512, (c4 + 1) * 512)
            pg, pb = self.nxt(pieces), self.nxt(pieces)
            f.dma("sp", pg.t[:, :], gsrc[:, cs], writes=pg.r)
            f.dma("sp", pb.t[:, :], bsrc[:, cs], writes=pb.r)
            self.act(xt.t[:, cs], xt.t[:, cs], AF.Identity, xt.r + rs.r + nm.r, xt.r,
                     bias=nm.t[:, 0:1], scale=rs.t[:, 0:1])
            self.tt("dve", xt.t[:, cs], xt.t[:, cs], pg.t[:, :], ALU.mult, xt.r + pg.r, xt.r)
            self.tt("dve", xt.t[:, cs], xt.t[:, cs], pb.t[:, :], ALU.add, xt.r + pb.r, xt.r)
        for (dst, wres) in dst_fn():
            f.dma(SQ, dst, xt.t[:, :], reads=xt.r, writes=wres)

    def phase_C(self, l):
        f = self.f
        self.phase()
        TM = self.TCD
        oaTs = self.atiles("oaTt", 2, [128, 8, TM], BF16)
        obTs = self.atiles("obTt", 2, [128, 8, TM], BF16)
        mTs = self.atiles("mT", 1, [128, KC, TM], BF16)
        gts = self.atiles("gt", 4, [128, TM], BF16)
        tmps = self.atiles("tmpc", 4, [128, TM], F32)
        y1all = self.atiles("y1", TM // 128 + 2, [128, D], F32)
        pieces = self.atiles("pcc", 6, [128, 512], F32)
        xrs = self.atiles("xr", 3, [128, 512], F32)
        t1s = self.atiles("t1c", 2, [128, 512], F32)
        sms = [self.small_set(f"smc{i}") for i in range(2)]
        for (b0, nb) in self.supertiles(self.TCD):
            T = nb * 128
            tok0 = b0 * 128
            g = self.grp(b0)
            oaT, obT, mT = self.nxt(oaTs), self.nxt(obTs), self.nxt(mTs)
            y1 = [self.nxt(y1all) for _ in range(nb)]
            blks = list(range(b0, b0 + nb))
            f.dma("sp", oaT.t[:, :, 0:T], self.oaT[l, :, tok0:tok0 + T].rearrange("(kc p) s -> p kc s", p=128),
                  reads=[("oaT", l, c, b) for c in range(8) for b in blks], writes=oaT.r)
            f.dma("sp", obT.t[:, :, 0:T], self.obT[l, :, tok0:tok0 + T].rearrange("(kc p) s -> p kc s", p=128),
                  reads=[("obT", l, c, b) for c in range(8) for b in blks], writes=obT.r)
            for ct4 in range(4):
                wa = self.load_wb("w_oa", l, 0, 8, ct4 * 512, 512)
                wb_ = self.load_wb("w_ob", l, 0, 8, ct4 * 512, 512)
                for sub in range(4):
                    ct = ct4 * 4 + sub
                    gA, gB = self.nxt(gts), self.nxt(gts)
                    f.dma("sp", gA.t[:, 0:T], self.gaT[l, ct * 128:(ct + 1) * 128, tok0:tok0 + T],
                          reads=[("gaT", l, ct, b) for b in blks], writes=gA.r)
                    f.dma("sp", gB.t[:, 0:T], self.gbT[l, ct * 128:(ct + 1) * 128, tok0:tok0 + T],
                          reads=[("gbT", l, ct, b) for b in blks], writes=gB.r)
                    bA = self.rot.setdefault("gb", 0) % 4
                    bB = (bA + 1) % 4
                    self.rot["gb"] += 2
                    for kc in range(8):
                        self.mm(self.ps[bA][:, 0:T], wa.t[:, kc, sub * 128:(sub + 1) * 128], oaT.t[:, kc, 0:T],
                                kc == 0, kc == 7, wa.r + oaT.r, self.psr(bA))
                    for kc in range(8):
                        self.mm(self.ps[bB][:, 0:T], wb_.t[:, kc, sub * 128:(sub + 1) * 128], obT.t[:, kc, 0:T],
                                kc == 0, kc == 7, wb_.r + obT.r, self.psr(bB))
                    ta, tb_ = self.nxt(tmps), self.nxt(tmps)
                    self.tt("dve", ta.t[:, 0:T], self.ps[bA][:, 0:T], gA.t[:, 0:T], ALU.mult, self.psr(bA) + gA.r, ta.r)
                    self.tt("dve", tb_.t[:, 0:T], self.ps[bB][:, 0:T], gB.t[:, 0:T], ALU.mult, self.psr(bB) + gB.r, tb_.r)
                    self.tt("dve", mT.t[:, ct, 0:T], ta.t[:, 0:T], tb_.t[:, 0:T], ALU.add, ta.r + tb_.r, mT.r)
            for c4 in range(4):
                cs = slice(c4 * 512, (c4 + 1) * 512)
                wo = self.load_wb("w_out", l, 0, KC, c4 * 512, 512)
                gm = self.nxt(pieces)
                a, r = self.mod_piece(l, g, 2, c4)
                f.dma("sp", gm.t[:, :], a, reads=r, writes=gm.r)
                for tb in range(nb):
                    blk = b0 + tb
                    xr = self.nxt(xrs)
                    a, r = self.x_src(l, blk)
                    f.dma("sp", xr.t[:, :], a[:, cs], reads=r, writes=xr.r)
                    bank = self.rot.setdefault("gb", 0) % 4
                    self.rot["gb"] += 1
                    for kc in range(KC):
                        self.mm(self.ps[bank][:, :], mT.t[:, kc, tb * 128:(tb + 1) * 128], wo.t[:, kc, :],
                                kc == 0, kc == KC - 1, mT.r + wo.r, self.psr(bank))
                    t1 = self.nxt(t1s)
                    self.tt("dve", t1.t[:, :], self.ps[bank][:, :], gm.t[:, :], ALU.mult, self.psr(bank) + gm.r, t1.r)
                    self.ts("dve", xr.t[:, :], xr.t[:, :], ALPHA, None, ALU.mult, None, xr.r, xr.r)
                    self.tt("dve", y1[tb].t[:, cs], xr.t[:, :], t1.t[:, :], ALU.add, xr.r + t1.r, y1[tb].r)
            for tb in range(nb):
                blk = b0 + tb
                self.ln_affine_store(y1[tb], self.nxt(sms), self.ln1g[l], self.ln1b[l], pieces,
                                     lambda blk=blk: [(self.x1[l, blk * 128:(blk + 1) * 128, :],
                                                       [("x1", l, blk, c) for c in range(4)])])

    def phase_D(self, l):
        f = self.f
        last = l == DEPTH - 1
        self.phase()
        TM = self.TCD
        x1all = self.atiles("x1t", TM // 128 + 1, [128, D], F32)
        h2T = self.atile("h2T", [128, KC, TM], BF16)
        uT = self.atile("uT", [128, FC, TM], BF16)
        hbs = self.atiles("hbd", 2, [128, D], BF16)
        pieces = self.atiles("pcd", 6, [128, 512], F32)
        tmp = self.atiles("tmpd", 2, [128, 512], F32)
        s1s = self.atiles("s1", 2, [128, TM], F32)
        t1s = self.atiles("t1d", 2, [128, 512], F32)
        sms = [self.small_set(f"smd{i}") for i in range(2)]
        for (b0, nb) in self.supertiles(self.TCD):
            T = nb * 128
            g = self.grp(b0)
            x1t = [self.nxt(x1all) for _ in range(nb)]
            for tb in range(nb):
                blk = b0 + tb
                f.dma("sp", x1t[tb].t[:, :], self.x1[l, blk * 128:(blk + 1) * 128, :],
                      reads=[("x1", l, blk, c) for c in range(4)], writes=x1t[tb].r)
                self.ln_mod_T(l, blk, x1t[tb], self.nxt(sms), pieces, tmp, self.nxt(hbs), 4, 3,
                              lambda c0, m, tb=tb: h2T.t[:, c0:c0 + m, tb * 128:(tb + 1) * 128], h2T.r)
            for f4 in range(DFF // 512):
                w1t = self.load_wb("w1", l, 0, KC, f4 * 512, 512)
                w3t = self.load_wb("w3", l, 0, KC, f4 * 512, 512)
                for sub in range(4):
                    fc = f4 * 4 + sub
                    b1 = self.rot.setdefault("gb", 0) % 4
                    b3 = (b1 + 1) % 4
                    self.rot["gb"] += 2
                    for kc in range(KC):
                        self.mm(self.ps[b1][:, 0:T], w1t.t[:, kc, sub * 128:(sub + 1) * 128], h2T.t[:, kc, 0:T],
                                kc == 0, kc == KC - 1, w1t.r + h2T.r, self.psr(b1))
                    for kc in range(KC):
                        self.mm(self.ps[b3][:, 0:T], w3t.t[:, kc, sub * 128:(sub + 1) * 128], h2T.t[:, kc, 0:T],
                                kc == 0, kc == KC - 1, w3t.r + h2T.r, self.psr(b3))
                    s1 = self.nxt(s1s)
                    self.act(s1.t[:, 0:T], self.ps[b1][:, 0:T], AF.Silu, self.psr(b1), s1.r)
                    self.tt("dve", uT.t[:, fc, 0:T], self.ps[b3][:, 0:T], s1.t[:, 0:T], ALU.mult,
                            self.psr(b3) + s1.r, uT.r)
            kqs = [(0, 16), (16, 16), (32, 12)]
            for c4 in range(4):
                cs = slice(c4 * 512, (c4 + 1) * 512)
                base = 0 if c4 % 2 == 0 else 4
                banks = [base + i for i in range(nb)]
                gf = self.nxt(pieces)
                a, r = self.mod_piece(l, g, 5, c4)
                f.dma("sp", gf.t[:, :], a, reads=r, writes=gf.r)
                for qi, (k0, nk) in enumerate(kqs):
                    wt = self.load_wb("w2", l, k0, nk, c4 * 512, 512)
                    for tb in range(nb):
                        for kc in range(nk):
                            self.mm(self.ps[banks[tb]][:, :], uT.t[:, k0 + kc, tb * 128:(tb + 1) * 128], wt.t[:, kc, :],
                                    qi == 0 and kc == 0, qi == len(kqs) - 1 and kc == nk - 1,
                                    uT.r + wt.r, self.psr(banks[tb]))
                for tb in range(nb):
                    t1 = self.nxt(t1s)
                    self.tt("dve", t1.t[:, :], self.ps[banks[tb]][:, :], gf.t[:, :], ALU.mult,
                            self.psr(banks[tb]) + gf.r, t1.r)
                    self.ts("dve", x1t[tb].t[:, cs], x1t[tb].t[:, cs], ALPHA, None, ALU.mult, None,
                            x1t[tb].r, x1t[tb].r)
                    self.tt("dve", x1t[tb].t[:, cs], x1t[tb].t[:, cs], t1.t[:, :], ALU.add,
                            x1t[tb].r + t1.r, x1t[tb].r)
            for tb in range(nb):
                blk = b0 + tb

                def dsts(blk=blk):
                    if not last:
                        return [(self.x2[l, blk * 128:(blk + 1) * 128, :], [("x2", l, blk, c) for c in range(4)])]
                    if blk < self.NPB:
                        return [(self.y_p[blk * 128:(blk + 1) * 128, :], [])]
                    return [(self.y_s[:, :], [])]
                self.ln_affine_store(x1t[tb], self.nxt(sms), self.ln2g[l], self.ln2b[l], pieces, dsts)

    def build(self, upto=99):
        rest = ("w_oa", "w_ob", "w_out", "w1", "w3", "w2")

        def conv_all():
            self.convert_weights(0, rest)
            for j in range(1, DEPTH):
                self.convert_weights(j, ("w_in",) + rest)
        steps = [lambda: (self.convert_weights(0, ("w_in",)), self.setup())]
        for l in range(DEPTH):
            steps += [lambda l=l: self.phase_A(l),
                      lambda l=l: (conv_all() if l == 0 else None, self.phase_attn_prompt(l, "A")),
                      lambda l=l: self.phase_attn_prompt(l, "B"),
                      lambda l=l: self.phase_attn_sample(l, "A"),
                      lambda l=l: self.phase_attn_sample(l, "B"),
                      lambda l=l: self.phase_C(l),
                      lambda l=l: self.phase_D(l)]
        for st in steps[:upto]:
            st()
        if self.wplan is None:
            return None, self.wlog
        info = self.f.emit()
        return self.nc, info


def _t5_bucket_np(rel):
    nb = 16
    max_exact = 8
    bucket = (rel > 0).astype(np.int64) * nb
    n = np.abs(rel)
    nf = np.maximum(n, 1).astype(np.float32)
    large = max_exact + (np.log(nf / max_exact) / math.log(128 / max_exact) * (nb - max_exact)).astype(np.int64)
    large = np.minimum(large, nb - 1)
    return bucket + np.where(n < max_exact, n, large)


def _static_tables():
    k = np.arange(128)[:, None]
    c = np.arange(256)[None, :]
    rel = k - c
    idxA = _t5_bucket_np(rel)
    idxB = np.clip(rel, -128, 128) + 128
    q = np.arange(128)[None, :]
    mdiag = ((k // CHUNK) <= (q // CHUNK)).astype(np.float32)
    mask = np.concatenate([mdiag, np.ones((128, 128), np.float32)], axis=1)
    return idxA, idxB, mask


def _bc(v, n=128):
    return np.ascontiguousarray(np.broadcast_to(v[..., None, :], v.shape[:-1] + (n, v.shape[-1])))


_CACHE = {}


def _get_program(S, upto=99):
    if S not in _CACHE:
        _, wplan = Builder(S=S, TA=min(1024, S), TCD=min(512, S)).build(upto)
        b = Builder(S=S, TA=min(1024, S), TCD=min(512, S), wplan=wplan)
        nc, info = b.build(upto)
        _CACHE[S] = nc
    return _CACHE[S]


def make_in_maps(inp, ncores, S):
    f32 = np.float32
    idxA, idxB, mask = _static_tables()
    t5 = np.asarray(inp["t5_bias"], f32)
    relb = np.asarray(inp["rel_bias"], f32)
    biasA = np.ascontiguousarray(np.transpose(t5[idxA], (0, 2, 1)))
    biasB = np.ascontiguousarray(np.transpose(relb[:, idxB], (0, 1, 3, 2)))
    c15A = _bc(t5[15])
    c0B = _bc(relb[:, 0, :])
    lamv = np.stack([inp["lambda_q1"], inp["lambda_k1"], inp["lambda_q2"], inp["lambda_k2"]], axis=1)
    shared = {
        "w_mod": np.asarray(inp["w_mod"], f32), "bmod": _bc(np.asarray(inp["b_mod"], f32)),
        "w_in": np.asarray(inp["w_in"], f32), "lamv": _bc(np.asarray(lamv, f32)),
        "subln": _bc(np.asarray(inp["subln_g"], f32)), "biasA": biasA, "c15A": c15A, "biasB": biasB, "c0B": c0B,
        "mask01": mask, "ident": np.eye(128, dtype=f32),
        "w_oa": np.asarray(inp["w_oa"], f32), "w_ob": np.asarray(inp["w_ob"], f32),
        "w_out": np.asarray(inp["w_out"], f32),
        "ln1g": _bc(np.asarray(inp["ln1_g"], f32)), "ln1b": _bc(np.asarray(inp["ln1_b"], f32)),
        "w1": np.asarray(inp["w1"], f32), "w3": np.asarray(inp["w3"], f32), "w2": np.asarray(inp["w2"], f32),
        "ln2g": _bc(np.asarray(inp["ln2_g"], f32)), "ln2b": _bc(np.asarray(inp["ln2_b"], f32)),
    }
    maps = []
    for i in range(ncores):
        sb = slice(4 * i, 4 * i + 4)
        cs = np.asarray(inp["c_sample"][sb], f32)
        cbc = np.stack([np.broadcast_to(np.asarray(inp["c_prompt"][i], f32)[None, :], (128, D)),
                        np.repeat(cs, 32, axis=0)], axis=0)
        m = dict(shared)
        m.update({
            "xp": np.ascontiguousarray(inp["x_prompt"][i], dtype=f32),
            "xs": np.ascontiguousarray(np.asarray(inp["x_sample"][sb], f32).reshape(128, D)),
            "cak": np.ascontiguousarray(np.asarray(inp["cache_a_k"][:, sb], f32).reshape(DEPTH, 4, -1, 1024)),
            "cav": np.ascontiguousarray(np.asarray(inp["cache_a_v"][:, sb], f32).reshape(DEPTH, 4, -1, 1024)),
            "cbk": np.ascontiguousarray(np.asarray(inp["cache_b_k"][:, sb], f32).reshape(DEPTH, 4, -1, 1024)),
            "cbv": np.ascontiguousarray(np.asarray(inp["cache_b_v"][:, sb], f32).reshape(DEPTH, 4, -1, 1024)),
            "cbc": np.ascontiguousarray(cbc),
        })
        maps.append(m)
    return maps


def assemble(results, ncores, S):
    L = DEPTH
    B = ncores
    bo = min(512, S)
    y_p = np.stack([r["y_p"] for r in results], 0)
    y_s = np.concatenate([r["y_s"].reshape(4, 32, D) for r in results], 0)
    akp = np.stack([r["akp"] for r in results], 1).reshape(L, B, S, 8, 2, 64)
    avp = np.stack([r["avp"] for r in results], 1).reshape(L, B, S, 8, 128)
    bkp = np.stack([r["bkp"] for r in results], 1).reshape(L, B, bo, 8, 128)
    bvp = np.stack([r["bvp"] for r in results], 1).reshape(L, B, bo, 8, 128)
    aks = np.concatenate([r["aks"].reshape(L, 4, 32, 1024) for r in results], 1).reshape(L, 4 * B, 32, 8, 2, 64)
    avs = np.concatenate([r["avs"].reshape(L, 4, 32, 1024) for r in results], 1).reshape(L, 4 * B, 32, 8, 128)
    bks = np.concatenate([r["bks"].reshape(L, 4, 32, 1024) for r in results], 1).reshape(L, 4 * B, 32, 8, 128)
    bvs = np.concatenate([r["bvs"].reshape(L, 4, 32, 1024) for r in results], 1).reshape(L, 4 * B, 32, 8, 128)
    return tuple(np.ascontiguousarray(a, dtype=np.float32) for a in (y_p, y_s, akp, avp, bkp, bvp, aks, avs, bks, bvs))


def kernel(**inputs):
    S = inputs["x_prompt"].shape[1]
    ncores = inputs["x_prompt"].shape[0]
    nc = _get_program(S)
    maps = make_in_maps(inputs, ncores, S)
    res = run_bass_kernel_spmd(nc, maps, core_ids=list(range(ncores)))
    return assemble(res.results, ncores, S)
```

```python
import math
import numpy as np
from contextlib import ExitStack
import concourse.bass as bass
import concourse.mybir as mybir
from concourse.bass_utils import run_bass_kernel_spmd

F32 = mybir.dt.float32
BF16 = mybir.dt.bfloat16
AF = mybir.ActivationFunctionType
ALU = mybir.AluOpType
AX = mybir.AxisListType

D = 2048
KC = 16
DFF = 5632
FC = 44
INW = 10240
DEPTH = 2
ALPHA = float((2 * DEPTH) ** 0.25)
LN_EPS = 1e-5
CHUNK = 64
NCORES = 8

ENGS = ("pe", "act", "dve", "pool", "sp")
N_DMA_SEMS = {"sp": 30, "act": 8, "pool": 30}
SEM_WRAP = 30000
SQ = "pool"


class Op:
    __slots__ = ("idx", "eng", "fn", "dma", "deps", "needs_inc", "ticket", "dsem", "dval", "pre")

    def __init__(self, idx, eng, fn, dma):
        self.idx = idx
        self.eng = eng
        self.fn = fn
        self.dma = dma
        self.deps = ()
        self.needs_inc = False
        self.ticket = None
        self.dsem = None
        self.dval = 0
        self.pre = None


class Fw:
    def __init__(self, nc):
        self.nc = nc
        self.ops = []
        self.lastw = {}
        self.readers = {}

    def op(self, eng, fn, reads=(), writes=(), dma=False, barrier=False):
        if not barrier and "ARENA" in writes:
            writes = [w for w in writes if w != "ARENA"]
            reads = list(reads) + ["ARENA"]
        idx = len(self.ops)
        o = Op(idx, eng, fn, dma)
        deps = set()
        for r in reads:
            lw = self.lastw.get(r)
            if lw is not None:
                deps.add(lw)
        for w in writes:
            lw = self.lastw.get(w)
            if lw is not None:
                deps.add(lw)
            rd = self.readers.get(w)
            if rd:
                deps.update(rd[0].values())
                deps.update(rd[1])
        for r in reads:
            rd = self.readers.get(r)
            if rd is None:
                rd = self.readers[r] = ({}, [])
            if dma:
                rd[1].append(idx)
            else:
                rd[0][eng] = idx
        for w in writes:
            self.lastw[w] = idx
            self.readers[w] = ({}, [])
        keep = []
        for d in deps:
            y = self.ops[d]
            if not y.dma:
                if y.eng == eng and not dma and eng == "pe":
                    continue
                y.needs_inc = True
            keep.append(d)
        o.deps = keep
        self.ops.append(o)
        return o

    def pe(self, fn, reads=(), writes=()):
        return self.op("pe", fn, reads, writes)

    def act(self, fn, reads=(), writes=()):
        return self.op("act", fn, reads, writes)

    def dve(self, fn, reads=(), writes=()):
        return self.op("dve", fn, reads, writes)

    def pool(self, fn, reads=(), writes=()):
        return self.op("pool", fn, reads, writes)

    def dma(self, q, out, in_, reads=(), writes=()):
        return self.op(q, lambda e: e.dma_start(out=out, in_=in_), reads, writes, dma=True)

    def emit(self):
        nc = self.nc
        with ExitStack() as es:
            counts = {e: 0 for e in ENGS}
            for o in self.ops:
                if not o.dma and o.needs_inc:
                    counts[o.eng] += 1
                    o.ticket = counts[o.eng]
            esems = {}
            for e in ENGS:
                n = counts[e] // SEM_WRAP + 1
                esems[e] = [es.enter_context(nc.semaphore(f"s_{e}_{i}")) for i in range(n)]
            dsems = {q: [es.enter_context(nc.semaphore(f"d_{q}_{i}")) for i in range(n)]
                     for q, n in N_DMA_SEMS.items()}
            dcount = {q: 0 for q in N_DMA_SEMS}
            dhist = {q: [] for q in N_DMA_SEMS}
            for o in self.ops:
                if o.dma:
                    q = o.eng
                    k = dcount[q]
                    P = N_DMA_SEMS[q]
                    o.dsem = dsems[q][k % P]
                    o.dval = 16 * (k // P + 1)
                    if k >= P:
                        o.pre = dhist[q][k - P]
                    dhist[q].append(o)
                    dcount[q] += 1

            def target(y):
                if y.dma:
                    return y.dsem, y.dval
                t = y.ticket
                ep = (t - 1) // SEM_WRAP
                return esems[y.eng][ep], t - ep * SEM_WRAP

            per_eng = {e: [] for e in ENGS}
            for o in self.ops:
                per_eng[o.eng].append(o)
            all_dma = [o for o in self.ops if o.dma]
            ops = self.ops

            def run_engine(ename, eng):
                waited = {}
                for o in per_eng[ename]:
                    w = {}
                    deps = [ops[d] for d in o.deps]
                    if o.pre is not None:
                        deps.append(o.pre)
                    for y in deps:
                        s, v = target(y)
                        key = id(s)
                        if waited.get(key, 0) >= v:
                            continue
                        if key not in w or w[key][1] < v:
                            w[key] = (s, v)
                    for key, (s, v) in w.items():
                        eng.wait_ge(s, v)
                        waited[key] = v
                    inst = o.fn(eng)
                    if o.dma:
                        inst.then_inc(o.dsem, 16)
                    elif o.needs_inc:
                        s, v = target(o)
                        inst.then_inc(s, 1)
                if ename == "sp":
                    last = {}
                    for y in all_dma:
                        k = id(y.dsem)
                        if k not in last or last[k][1] < y.dval:
                            last[k] = (y.dsem, y.dval)
                    for key, (s, v) in last.items():
                        if waited.get(key, 0) < v:
                            eng.wait_ge(s, v)

            with nc.Block() as block:
                @block.tensor
                def _(e):
                    run_engine("pe", e)

                @block.scalar
                def _(e):
                    run_engine("act", e)

                @block.vector
                def _(e):
                    run_engine("dve", e)

                @block.gpsimd
                def _(e):
                    run_engine("pool", e)

                @block.sync
                def _(e):
                    run_engine("sp", e)
        return counts, dcount


class Tl:
    __slots__ = ("t", "r")

    def __init__(self, t, r):
        self.t = t
        self.r = r


SB_BASE = 16640
CONST_BYTES = 11 * 1024
NWB = 4
WPF = 2
WB_BYTES = 16 * 1024
WB0 = SB_BASE + CONST_BYTES
ARENA0 = WB0 + NWB * WB_BYTES
SBUF_LIMIT = 229376 - 128


class Builder:
    def __init__(self, S=4096, TA=1024, TCD=512, PAST=1024, BPAST=512, NSB=4, TS=32, wplan=None):
        self.wplan = wplan
        self.attn_hooks = {}
        self.converted = set()
        self.wlog = []
        self.wissued = 0
        assert NSB * TS == 128
        self.S, self.TA, self.TCD = S, TA, TCD
        self.PAST, self.BPAST, self.NSB, self.TS = PAST, BPAST, NSB, TS
        self.NPB = S // 128
        self.NB = self.NPB + 1
        self.NT = S + 128
        self.BOUT = min(512, S)
        self.nc = bass.Bass("TRN2", target_bir_lowering=False)
        self.f = Fw(self.nc)
        self.uid = 0
        self.wcnt = 0
        self.rot = {}
        self.declare()

    def din(self, name, shape, dt=F32):
        return self.nc.dram_tensor(name, list(shape), dt, kind="ExternalInput").ap()

    def dout(self, name, shape):
        return self.nc.dram_tensor(name, list(shape), F32, kind="ExternalOutput").ap()

    def dscr(self, name, shape, dt):
        return self.nc.dram_tensor(name, list(shape), dt, kind="Internal").ap()

    def declare(self):
        S, NT = self.S, self.NT
        L = DEPTH
        self.xp = self.din("xp", [S, D])
        self.xs = self.din("xs", [128, D])
        self.cak = self.din("cak", [L, self.NSB, self.PAST, 1024])
        self.cav = self.din("cav", [L, self.NSB, self.PAST, 1024])
        self.cbk = self.din("cbk", [L, self.NSB, self.BPAST, 1024])
        self.cbv = self.din("cbv", [L, self.NSB, self.BPAST, 1024])
        self.cbc = self.din("cbc", [2, 128, D])
        self.w_mod = self.din("w_mod", [L, D, 6 * D])
        self.bmod = self.din("bmod", [L, 128, 6 * D])
        self.w_in = self.din("w_in", [L, D, INW])
        self.lamv = self.din("lamv", [L, 4, 128, 64])
        self.subln = self.din("subln", [L, 128, 128])
        self.biasA = self.din("biasA", [128, 8, 256])
        self.c15A = self.din("c15A", [128, 8])
        self.biasB = self.din("biasB", [L, 128, 8, 256])
        self.c0B = self.din("c0B", [L, 128, 8])
        self.mask01 = self.din("mask01", [128, 256])
        self.identd = self.din("ident", [128, 128])
        self.w_oa = self.din("w_oa", [L, 1024, D])
        self.w_ob = self.din("w_ob", [L, 1024, D])
        self.w_out = self.din("w_out", [L, D, D])
        self.ln1g = self.din("ln1g", [L, 128, D])
        self.ln1b = self.din("ln1b", [L, 128, D])
        self.w1 = self.din("w1", [L, D, DFF])
        self.w3 = self.din("w3", [L, D, DFF])
        self.w2 = self.din("w2", [L, DFF, D])
        self.ln2g = self.din("ln2g", [L, 128, D])
        self.ln2b = self.din("ln2b", [L, 128, D])
        self.y_p = self.dout("y_p", [S, D])
        self.y_s = self.dout("y_s", [128, D])
        self.akp = self.dout("akp", [L, S, 1024])
        self.avp = self.dout("avp", [L, S, 1024])
        self.bkp = self.dout("bkp", [L, self.BOUT, 1024])
        self.bvp = self.dout("bvp", [L, self.BOUT, 1024])
        self.aks = self.dout("aks", [L, 128, 1024])
        self.avs = self.dout("avs", [L, 128, 1024])
        self.bks = self.dout("bks", [L, 128, 1024])
        self.bvs = self.dout("bvs", [L, 128, 1024])
        self.modbc = self.dscr("modbc", [L, 2, 128, 6 * D], F32)
        self.qaT = self.dscr("qaT", [L, 1024, NT], BF16)
        self.kaT = self.dscr("kaT", [L, 1024, NT], BF16)
        self.qbT = self.dscr("qbT", [L, 1024, NT], BF16)
        self.kbT = self.dscr("kbT", [L, 1024, NT], BF16)
        self.va = self.dscr("va", [L, NT, 1024], BF16)
        self.vb = self.dscr("vb", [L, NT, 1024], BF16)
        self.gaT = self.dscr("gaT", [L, D, NT], BF16)
        self.gbT = self.dscr("gbT", [L, D, NT], BF16)
        self.oaT = self.dscr("oaT", [L, 1024, NT], BF16)
        self.obT = self.dscr("obT", [L, 1024, NT], BF16)
        self.x1 = self.dscr("x1", [L, NT, D], F32)
        self.x2 = self.dscr("x2", [L, NT, D], F32)
        def wscr(name, src):
            Kd, Nd = src.shape[1], src.shape[2]
            return (src, self.dscr(name + "_b", [L, Nd // 512, 128, Kd // 128, 512], BF16))
        self.wsc = {"w_in": wscr("w_in", self.w_in), "w_oa": wscr("w_oa", self.w_oa),
                    "w_ob": wscr("w_ob", self.w_ob), "w_out": wscr("w_out", self.w_out),
                    "w1": wscr("w1", self.w1), "w3": wscr("w3", self.w3), "w2": wscr("w2", self.w2)}
        self.ps = [self.nc.alloc_psum_tensor(f"psb{i}", [128, 512], F32) for i in range(8)]
        self.coff = SB_BASE
        self.ident = self.ctile("ident", [128, 128], BF16)
        self.c15 = self.ctile("c15", [128, 8], F32)
        self.c0b = self.ctile("c0b", [128, L * 8], F32)
        self.zero = self.ctile("zero", [128, 1], F32)
        self.negl = self.ctile("negl", [128, L], F32)
        self.gsub = self.ctile("gsub", [128, L, 128], F32)
        self.cT = [self.ctile(f"cT{g}", [128, KC, 128], BF16) for g in range(2)]
        self.dummy = self.ctile("dummy", [128, 8], F32)
        assert self.coff <= WB0, self.coff
        self.wb = []
        for i in range(NWB):
            t = self.nc.alloc_sbuf_tensor_at(f"wb{i}", [128, 16, 512], BF16, offset=WB0 + i * WB_BYTES)
            self.wb.append(Tl(t, [f"wb{i}"]))
        self.aoff = ARENA0

    def _alloc(self, name, shape, dt, off):
        self.uid += 1
        return self.nc.alloc_sbuf_tensor_at(f"{name}_{self.uid}", list(shape), dt, offset=off)

    @staticmethod
    def _nbytes(shape, dt):
        n = 1
        for s in shape[1:]:
            n *= s
        return n * (2 if dt == BF16 else 4)

    def ctile(self, name, shape, dt):
        nb = (self._nbytes(shape, dt) + 31) // 32 * 32
        t = self._alloc(name, shape, dt, self.coff)
        self.coff += nb
        return Tl(t, [name])

    def atile(self, name, shape, dt):
        nb = (self._nbytes(shape, dt) + 31) // 32 * 32
        assert self.aoff + nb <= SBUF_LIMIT, (name, self.aoff, nb)
        t = self._alloc(name, shape, dt, self.aoff)
        self.aoff += nb
        self.uid += 1
        return Tl(t, [f"{name}#{self.uid}", "ARENA"])

    def atiles(self, name, n, shape, dt):
        return [self.atile(f"{name}{i}", shape, dt) for i in range(n)]

    def nxt(self, lst):
        k = id(lst)
        i = self.rot.get(k, 0)
        self.rot[k] = i + 1
        return lst[i % len(lst)]

    def phase(self):
        f = self.f
        d = self.dummy
        f.op("dve", lambda e: e.memset(d.t[:, 0:1], 0.0), reads=[], writes=["ARENA"] + d.r, barrier=True)
        self.aoff = ARENA0

    def mm(self, out, lhsT, rhs, start, stop, reads, writes):
        self.f.pe(lambda e: e.matmul(out, lhsT, rhs, start=start, stop=stop), reads, writes)

    def tr(self, out, in_, reads, writes):
        idn = self.ident
        npart = in_.shape[0]
        self.f.pe(lambda e: e.transpose(out, in_, idn.t[0:npart, 0:npart]), list(reads) + idn.r, writes)

    def act(self, out, in_, func, reads, writes, bias=0.0, scale=1.0):
        self.f.act(lambda e: e.activation(out=out, in_=in_, func=func, bias=bias, scale=scale), reads, writes)

    def tt(self, eng, out, in0, in1, op, reads, writes):
        self.f.op(eng, lambda e: e.tensor_tensor(out, in0, in1, op), reads, writes)

    def ts(self, eng, out, in0, s1, s2, op0, op1, reads, writes):
        if op1 is None:
            self.f.op(eng, lambda e: e.tensor_scalar(out, in0, s1, None, op0), reads, writes)
        else:
            self.f.op(eng, lambda e: e.tensor_scalar(out, in0, s1, s2, op0, op1), reads, writes)

    def stt(self, eng, out, in0, scalar, in1, op0, op1, reads, writes):
        self.f.op(eng, lambda e: e.scalar_tensor_tensor(out, in0, scalar, in1, op0, op1), reads, writes)

    def cp(self, eng, out, in_, reads, writes):
        if eng == "act":
            self.f.act(lambda e: e.copy(out, in_), reads, writes)
        else:
            self.f.op(eng, lambda e: e.tensor_copy(out, in_), reads, writes)

    def psr(self, i):
        return [f"ps{i}"]

    def _issue_w(self, k, spec):
        kind, name, l, k0, nk, c0, ncols = spec
        wb = self.wb[k % NWB]
        if kind == "cast":
            W = {"w_mod": self.w_mod, "w_in": self.w_in}[name][l]
            rd = []
        else:
            assert ncols == 512 and c0 % 512 == 0
            src = self.wsc[name][1][l, c0 // 512, :, k0:k0 + nk, :]
            self.f.dma("pool", wb.t[:, 0:nk, 0:ncols], src, reads=[(name + "_b", l, c0 // 512)], writes=wb.r)
            return
        src = W[k0 * 128:(k0 + nk) * 128, c0:c0 + ncols].rearrange("(kc p) n -> p kc n", p=128)
        self.f.dma("pool", wb.t[:, 0:nk, 0:ncols], src, reads=rd, writes=wb.r)

    def _req_w(self, spec):
        k = self.wcnt
        self.wcnt += 1
        if self.wplan is None:
            self.wlog.append(spec)
            self._issue_w(k, spec)
        else:
            assert self.wplan[k] == spec, (k, spec, self.wplan[k])
            while self.wissued < min(k + 1 + WPF, len(self.wplan)):
                nx = self.wplan[self.wissued]
                if nx[0] == "bf16" and (nx[1], nx[2]) not in self.converted and self.wissued > k:
                    break
                self._issue_w(self.wissued, nx)
                self.wissued += 1
        return self.wb[k % NWB]

    def load_w(self, name, l, k0, nk, c0, ncols):
        return self._req_w(("cast", name, l, k0, nk, c0, ncols))

    def convert_weights(self, l, names):
        for name in names:
            self.converted.add((name, l))
            src, dst = self.wsc[name]
            ncol = src.shape[2]
            for c in range(ncol // 512):
                self.f.dma("pool", dst[l, c], src[l, :, c * 512:(c + 1) * 512].rearrange("(kc p) n -> p kc n", p=128),
                           reads=[], writes=[(name + "_b", l, c)])

    def load_wb(self, name, l, k0, nk, c0, ncols):
        return self._req_w(("bf16", name, l, k0, nk, c0, ncols))

    def grp(self, blk):
        return 0 if blk < self.NPB else 1

    def x_src(self, l, blk):
        if l == 0:
            if blk < self.NPB:
                return self.xp[blk * 128:(blk + 1) * 128, :], []
            return self.xs[:, :], []
        return self.x2[l - 1, blk * 128:(blk + 1) * 128, :], [("x2", l - 1, blk, c) for c in range(4)]

    def supertiles(self, T):
        out = []
        nb = T // 128
        b = 0
        while b < self.NPB:
            n = min(nb, self.NPB - b)
            out.append((b, n))
            b += n
        out.append((self.NPB, 1))
        return out

    def ln_stats(self, xt, sm):
        st, mv, rs, nm = sm["st"], sm["mv"], sm["rs"], sm["nm"]
        for c in range(4):
            self.f.dve(lambda e, c=c: e.bn_stats(st.t[:, c, :], xt.t[:, c * 512:(c + 1) * 512]), xt.r, st.r)
        self.f.dve(lambda e: e.bn_aggr(mv.t[:, :], st.t[:, :, :]), st.r, mv.r)
        self.ts("dve", rs.t[:, :], mv.t[:, 1:2], LN_EPS, None, ALU.add, None, mv.r, rs.r)
        self.act(rs.t[:, :], rs.t[:, :], AF.Sqrt, rs.r, rs.r)
        self.f.dve(lambda e: e.reciprocal(rs.t[:, :], rs.t[:, :]), rs.r, rs.r)
        self.stt("dve", nm.t[:, :], mv.t[:, 0:1], -1.0, rs.t[:, :], ALU.mult, ALU.mult, mv.r + rs.r, nm.r)
        return rs, nm

    def small_set(self, name):
        return {"st": self.atile(name + "st", [128, 4, 6], F32), "mv": self.atile(name + "mv", [128, 2], F32),
                "rs": self.atile(name + "rs", [128, 1], F32), "nm": self.atile(name + "nm", [128, 1], F32)}

    def tok2feat(self, src, src_ap_fn, n, dst_fn, dst_r):
        c = 0
        while c < n:
            m = min(8, n - c)
            bank = 6 + (self.rot.setdefault("trb", 0) % 2)
            self.rot["trb"] += 1
            pt = self.ps[bank][:, :].bitcast(BF16).rearrange("p (a b) -> p a b", b=128)
            for j in range(m):
                a = src_ap_fn(c + j)
                npart = a.shape[0]
                self.tr(pt[:, j, 0:npart], a, src.r, self.psr(bank))
            npart = src_ap_fn(c).shape[0]
            eng = "act" if (self.rot["trb"] % 2) else "dve"
            self.cp(eng, dst_fn(c, m), pt[:, 0:m, 0:npart], self.psr(bank), dst_r)
            c += m

    def setup(self):
        f = self.f
        L = DEPTH
        self.phase()
        idf = self.atile("idf", [128, 128], F32)
        f.dma("sp", idf.t[:, :], self.identd[:, :], writes=idf.r)
        self.cp("dve", self.ident.t[:, :], idf.t[:, :], idf.r, self.ident.r)
        f.dma("sp", self.c15.t[:, :], self.c15A[:, :], writes=self.c15.r)
        for l in range(L):
            f.dma("sp", self.c0b.t[:, l * 8:(l + 1) * 8], self.c0B[l], writes=self.c0b.r)
        f.dve(lambda e: e.memset(self.zero.t[:, :], 0.0), [], self.zero.r)
        lv = self.atile("lv", [128, 4, 64], F32)
        pr = self.atile("pr", [128, 2, 64], F32)
        sm2 = self.atile("sm2", [128, 2], F32)
        ex2 = self.atile("ex2", [128, 2], F32)
        sg = self.atile("sg", [128, 128], F32)
        for l in range(L):
            lam_init = 0.8 - 0.6 * math.exp(-0.3 * l)
            for j in range(4):
                f.dma("sp", lv.t[:, j, :], self.lamv[l, j], writes=lv.r)
            self.tt("dve", pr.t[:, 0, :], lv.t[:, 0, :], lv.t[:, 1, :], ALU.mult, lv.r, pr.r)
            self.tt("dve", pr.t[:, 1, :], lv.t[:, 2, :], lv.t[:, 3, :], ALU.mult, lv.r, pr.r)
            f.dve(lambda e: e.reduce_sum(sm2.t[:, :], pr.t[:, :, :], axis=AX.X), pr.r, sm2.r)
            self.act(ex2.t[:, :], sm2.t[:, :], AF.Exp, sm2.r, ex2.r)
            self.tt("dve", sm2.t[:, 0:1], ex2.t[:, 0:1], ex2.t[:, 1:2], ALU.subtract, ex2.r, sm2.r)
            self.ts("dve", self.negl.t[:, l:l + 1], sm2.t[:, 0:1], lam_init, -1.0, ALU.add, ALU.mult,
                    sm2.r, self.negl.r)
            f.dma("sp", sg.t[:, :], self.subln[l], writes=sg.r)
            self.ts("dve", self.gsub.t[:, l, :], sg.t[:, :], 1.0 - lam_init, None, ALU.mult, None,
                    sg.r, self.gsub.r)
        crow = self.atiles("crow", 2, [128, D], F32)
        cbf = self.atiles("cbf", 2, [128, D], BF16)
        for g in range(2):
            f.dma("sp", crow[g].t[:, :], self.cbc[g], writes=crow[g].r)
            self.act(cbf[g].t[:, :], crow[g].t[:, :], AF.Silu, crow[g].r, cbf[g].r)
            cb, ct = cbf[g], self.cT[g]
            self.tok2feat(cb, lambda c, cb=cb: cb.t[:, c * 128:(c + 1) * 128], KC,
                          lambda c0, m, ct=ct: ct.t[:, c0:c0 + m, :], ct.r)
        bt = self.atiles("bt", 2, [128, 512], F32)
        ms = self.atiles("ms", 3, [128, 512], F32)
        for l in range(L):
            for n in range(24):
                wt = self.load_w("w_mod", l, 0, KC, n * 512, 512)
                b = self.nxt(bt)
                f.dma("sp", b.t[:, :], self.bmod[l, :, n * 512:(n + 1) * 512], writes=b.r)
                piece = n // 4
                for g in range(2):
                    bank = self.rot.setdefault("gb", 0) % 4
                    self.rot["gb"] += 1
                    ps = self.ps[bank]
                    for kc in range(KC):
                        self.mm(ps[:, :], self.cT[g].t[:, kc, :], wt.t[:, kc, :], kc == 0, kc == KC - 1,
                                self.cT[g].r + wt.r, self.psr(bank))
                    m = self.nxt(ms)
                    if piece in (1, 4):
                        self.stt("dve", m.t[:, :], ps[:, :], 1.0, b.t[:, :], ALU.add, ALU.add,
                                 self.psr(bank) + b.r, m.r)
                    else:
                        self.tt("dve", m.t[:, :], ps[:, :], b.t[:, :], ALU.add, self.psr(bank) + b.r, m.r)
                    f.dma(SQ, self.modbc[l, g, :, n * 512:(n + 1) * 512], m.t[:, :], reads=m.r,
                          writes=[("modbc", l, g, n)])

    def mod_piece(self, l, g, piece, c4):
        n = piece * 4 + c4
        return self.modbc[l, g, :, n * 512:(n + 1) * 512], [("modbc", l, g, n)]

    def ln_mod_T(self, l, blk, xt, sm, pieces, tmp, hb, piece_sc, piece_sh, dst_fn, dst_r, defer=False):
        f = self.f
        g = self.grp(blk)
        rs, nm = self.ln_stats(xt, sm)
        for c4 in range(4):
            cs = slice(c4 * 512, (c4 + 1) * 512)
            psc, psh = self.nxt(pieces), self.nxt(pieces)
            a, r = self.mod_piece(l, g, piece_sc, c4)
            f.dma("sp", psc.t[:, :], a, reads=r, writes=psc.r)
            a, r = self.mod_piece(l, g, piece_sh, c4)
            f.dma("sp", psh.t[:, :], a, reads=r, writes=psh.r)
            t = self.nxt(tmp)
            self.act(t.t[:, :], xt.t[:, cs], AF.Identity, xt.r + rs.r + nm.r, t.r,
                     bias=nm.t[:, 0:1], scale=rs.t[:, 0:1])
            self.tt("dve", t.t[:, :], t.t[:, :], psc.t[:, :], ALU.mult, t.r + psc.r, t.r)
            self.tt("dve", hb.t[:, cs], t.t[:, :], psh.t[:, :], ALU.add, t.r + psh.r, hb.r)
        def tpart():
            self.tok2feat(hb, lambda c: hb.t[:, c * 128:(c + 1) * 128], KC, dst_fn, dst_r)
        if defer:
            return tpart
        tpart()

    def phase_A(self, l):
        f = self.f
        segs = [("qa", 0, 1024), ("ka", 1024, 2048), ("va", 2048, 3072), ("qb", 3072, 4096),
                ("kb", 4096, 5120), ("vb", 5120, 6144), ("ga", 6144, 8192), ("gb", 8192, 10240)]
        fm_dst = {"qa": self.qaT, "qb": self.qbT, "ga": self.gaT, "gb": self.gbT}
        self.phase()
        hTs = self.atiles("hT", 2, [128, KC, self.TA], BF16)
        xrow = self.atiles("xrow", 2, [128, D], F32)
        hbs = self.atiles("hb", 2, [128, D], BF16)
        pieces = self.atiles("pc", 4, [128, 512], F32)
        tmp = self.atiles("tmp", 2, [128, 512], F32)
        sms = [self.small_set(f"sm{i}") for i in range(2)]
        st32 = self.atiles("st32", 3, [128, 512], F32)
        st16 = self.atiles("st16", 4, [128, 512], BF16)
        kst = self.atiles("kst", 2, [128, 4, 128], BF16)
        sts = self.supertiles(self.TA)

        def ln_block(si, tb, defer):
            b0_, nb_ = sts[si]
            hT_ = hTs[si % 2]
            blk = b0_ + tb
            xt = self.nxt(xrow)
            a, r = self.x_src(l, blk)
            f.dma("sp", xt.t[:, :], a, reads=r, writes=xt.r)
            return self.ln_mod_T(l, blk, xt, self.nxt(sms), pieces, tmp, self.nxt(hbs), 1, 0,
                                 lambda c0, m: hT_.t[:, c0:c0 + m, tb * 128:(tb + 1) * 128], hT_.r, defer=defer)

        for tb in range(sts[0][1]):
            ln_block(0, tb, False)
        pendk = [None]
        for si, (b0, nb) in enumerate(sts):
            T = nb * 128
            hT = hTs[si % 2]
            nb_next = sts[si + 1][1] if si + 1 < len(sts) else 0
            pend_tr = None
            for ctile in range(INW // 512):
                c0 = ctile * 512
                name, s0, s1 = [s for s in segs if s[1] <= c0 < s[2]][0]
                new_tr = ln_block(si + 1, ctile, True) if ctile < nb_next else None
                wt = self.load_wb("w_in", l, 0, KC, c0, 512)
                if name in fm_dst:
                    dst = fm_dst[name]
                    for sub in range(4):
                        row0 = c0 - s0 + sub * 128
                        rc = row0 // 128
                        for t0 in range(0, T, 512):
                            N = min(512, T - t0)
                            bank = self.rot.setdefault("gb", 0) % 4
                            self.rot["gb"] += 1
                            ps = self.ps[bank]
                            for kc in range(KC):
                                self.mm(ps[:, 0:N], wt.t[:, kc, sub * 128:(sub + 1) * 128], hT.t[:, kc, t0:t0 + N],
                                        kc == 0, kc == KC - 1, wt.r + hT.r, self.psr(bank))
                            s = self.nxt(st16)
                            if name in ("ga", "gb"):
                                self.act(s.t[:, 0:N], ps[:, 0:N], AF.Sigmoid, self.psr(bank), s.r)
                            else:
                                self.cp("dve", s.t[:, 0:N], ps[:, 0:N], self.psr(bank), s.r)
                            tok0 = b0 * 128 + t0
                            res = [(name + "T", l, rc, tok0 // 128 + i) for i in range(N // 128)]
                            f.dma(SQ, dst[l, row0:row0 + 128, tok0:tok0 + N], s.t[:, 0:N], reads=s.r, writes=res)
                else:
                    cc = (c0 - s0) // 512
                    for tb in range(nb):
                        blk = b0 + tb
                        bank = self.rot.setdefault("gb", 0) % 4
                        self.rot["gb"] += 1
                        ps = self.ps[bank]
                        for kc in range(KC):
                            self.mm(ps[:, :], hT.t[:, kc, tb * 128:(tb + 1) * 128], wt.t[:, kc, :],
                                    kc == 0, kc == KC - 1, wt.r + hT.r, self.psr(bank))
                        odst = None
                        if blk < self.NPB:
                            if name == "ka":
                                odst = self.akp[l, blk * 128:(blk + 1) * 128, c0 - s0:c0 - s0 + 512]
                            elif name == "va":
                                odst = self.avp[l, blk * 128:(blk + 1) * 128, c0 - s0:c0 - s0 + 512]
                            elif blk * 128 >= self.S - self.BOUT:
                                r0 = blk * 128 - (self.S - self.BOUT)
                                o = self.bkp if name == "kb" else self.bvp
                                odst = o[l, r0:r0 + 128, c0 - s0:c0 - s0 + 512]
                        else:
                            o = {"ka": self.aks, "va": self.avs, "kb": self.bks, "vb": self.bvs}[name]
                            odst = o[l, :, c0 - s0:c0 - s0 + 512]
                        if odst is not None:
                            s32 = self.nxt(st32)
                            self.cp("act", s32.t[:, :], ps[:, :], self.psr(bank), s32.r)
                            f.dma(SQ, odst, s32.t[:, :], reads=s32.r)
                            s = self.nxt(st16)
                            self.cp("dve", s.t[:, :], s32.t[:, :], s32.r, s.r)
                        else:
                            s = self.nxt(st16)
                            self.cp("dve", s.t[:, :], ps[:, :], self.psr(bank), s.r)
                        if name in ("va", "vb"):
                            dst = self.va if name == "va" else self.vb
                            f.dma(SQ, dst[l, blk * 128:(blk + 1) * 128, c0 - s0:c0 - s0 + 512], s.t[:, :],
                                  reads=s.r, writes=[(name, l, blk, cc)])
                        else:
                            def ktr(s=s, name=name, rows=c0 - s0, blk=blk):
                                k = self.nxt(kst)
                                self.tok2feat(s, lambda c: s.t[:, c * 128:(c + 1) * 128], 4,
                                              lambda c0_, m: k.t[:, c0_:c0_ + m, :], k.r)
                                dst = self.kaT if name == "ka" else self.kbT
                                f.dma(SQ, dst[l, rows:rows + 512, blk * 128:(blk + 1) * 128]
                                      .rearrange("(c p) s -> p c s", p=128), k.t[:, :, :], reads=k.r,
                                      writes=[(name + "T", l, rows // 128 + i, blk) for i in range(4)])
                            if pendk[0] is not None:
                                pendk[0]()
                            pendk[0] = ktr
                if pendk[0] is not None:
                    pendk[0]()
                    pendk[0] = None
                if pend_tr is not None:
                    pend_tr()
                pend_tr = new_tr
            if pend_tr is not None:
                pend_tr()
        if pendk[0] is not None:
            pendk[0]()

    def load_E(self, l, which):
        f = self.f
        nt = 2 if which == "A" else 1
        E = self.atile("E" + which, [128, 8, 2, nt, 128], F32)
        braw = self.atile("braw", [128, 8, 256], F32)
        msk = self.atile("msk", [128, 256], F32)
        src = self.biasA if which == "A" else self.biasB[l]
        f.dma("sp", braw.t[:, :, :], src, writes=braw.r)
        f.dma("sp", msk.t[:, :], self.mask01[:, :], writes=msk.r)
        self.act(braw.t[:, :, :], braw.t[:, :, :], AF.Exp, braw.r, braw.r)
        for h in range(8):
            for t in range(nt):
                self.tt("dve", E.t[:, h, :, t, :], braw.t[:, h, :].rearrange("p (b q) -> p b q", q=128),
                        msk.t[:, :].rearrange("p (b q) -> p b q", q=128), ALU.mult, braw.r + msk.r, E.r)
        return E

    def attn_core(self, which, l, h, qT_ap, q_r, nq, blocks, E, cfar_ap, par, o_dst, o_r, P3, wk, pend=None,
                  after=None, sbanks=(0, 1)):
        f = self.f
        nt = 2 if which == "A" else 1
        scale = 0.125 if which == "A" else 128 ** -0.5
        near = [b for b in blocks if b["kind"] in ("diag", "prev")]
        far = [b for b in blocks if b["kind"] in ("far", "far4")]
        per = 2 if which == "A" else 4
        groups = []
        if near:
            nks = sorted(set(b["nk"] for b in near))
            for nk in nks:
                groups.append(("near", [b for b in near if b["nk"] == nk]))
        n_near = len(groups)
        for i in range(0, len(far), per):
            groups.append(("far", far[i:i + per]))
        accb = [2 + 2 * par + t for t in range(nt)]
        ntot = len(blocks)
        seen = [0]

        def s_view(bank):
            return self.ps[bank][:, :].rearrange("p (b t q) -> p b t q", t=nt, q=128)

        nsb = len(sbanks)

        def do_qk(gi):
            kind, grp = groups[gi]
            bank = sbanks[gi % nsb]
            S4 = s_view(bank)
            for bi, b in enumerate(grp):
                nk = b["nk"]
                for t in range(nt):
                    lhsT = b["kT"][:, 0:nk]
                    rhs = qT_ap[:, t, 0:nq]
                    self.mm(S4[0:nk, bi, t, 0:nq], lhsT, rhs, True, True, b["r"] + q_r, self.psr(bank))

        def do_rest(gi):
            kind, grp = groups[gi]
            bank = sbanks[gi % nsb]
            S4 = s_view(bank)
            P = self.nxt(P3)
            P4 = P.t[:, :].rearrange("p (b t q) -> p b t q", t=nt, q=128)
            nk = grp[0]["nk"]
            nbk = len(grp)
            if kind == "far":
                self.act(P4[0:nk, 0:nbk, :, 0:nq], S4[0:nk, 0:nbk, :, 0:nq], AF.Exp, self.psr(bank), P.r,
                         bias=cfar_ap[0:nk, :], scale=scale)
                for bi, b in enumerate(grp):
                    if b["kind"] == "far4" and nq > 64:
                        f.op("dve", lambda e, bi=bi: e.memset(P4[0:64, bi, :, 64:nq], 0.0), P.r, P.r)
            else:
                self.act(P4[0:nk, 0:nbk, :, 0:nq], S4[0:nk, 0:nbk, :, 0:nq], AF.Exp, self.psr(bank), P.r,
                         bias=self.zero.t[0:nk, :], scale=scale)
                for bi, b in enumerate(grp):
                    eb = 0 if b["kind"] == "diag" else 1
                    self.tt("dve", P4[0:nk, bi, :, 0:nq], P4[0:nk, bi, :, 0:nq], E.t[0:nk, h, eb, :, 0:nq],
                            ALU.mult, P.r + E.r, P.r)
            for bi, b in enumerate(grp):
                nk = b["nk"]
                for t in range(nt):
                    self.mm(self.ps[accb[t]][0:nq, 0:129], P4[0:nk, bi, t, 0:nq], b["v"][0:nk, 0:129],
                            seen[0] == 0, seen[0] == ntot - 1, P.r + b["r"], self.psr(accb[t]))
                seen[0] += 1

        ng = len(groups)
        la = nsb - 1
        for g0 in range(min(la, ng)):
            do_qk(g0)
        for gi in range(ng):
            if gi + la < ng:
                do_qk(gi + la)
            do_rest(gi)
            if gi == n_near - 1 and pend is not None:
                pend[0]()
        if pend is not None:
            if n_near == 0:
                pend[0]()
            pend[1]()

        def fin_dve():
            self._attn_norm(which, l, nq, nt, accb, wk)

        def fin_pe():
            on = wk["on"]
            if 7 in sbanks:
                bank = 6
            else:
                bank = 6 + (self.rot.setdefault("trb", 0) % 2)
                self.rot["trb"] += 1
            pt = self.ps[bank][:, :].bitcast(BF16)
            self.tr(pt[:, 0:nq], on.t[0:nq, :], on.r, self.psr(bank))
            self.cp("act", o_dst, pt[:, 0:nq], self.psr(bank), o_r)
            if after is not None:
                after()
        return (fin_dve, fin_pe)

    def _attn_norm(self, which, l, nq, nt, accb, wk):
        f = self.f
        rz, t1, o, ss, on = wk["rz"], wk["t1"], wk["o"], wk["ss"], wk["on"]
        accr = [r for t in range(nt) for r in self.psr(accb[t])]
        for t in range(nt):
            f.dve(lambda e, t=t: e.reciprocal(rz.t[0:nq, t:t + 1], self.ps[accb[t]][0:nq, 128:129]), accr, rz.r)
        if which == "A":
            self.ts("dve", t1.t[0:nq, :], self.ps[accb[1]][0:nq, 0:128], rz.t[0:nq, 1:2], self.negl.t[0:nq, l:l + 1],
                    ALU.mult, ALU.mult, accr + rz.r + self.negl.r, t1.r)
            self.stt("dve", o.t[0:nq, :], self.ps[accb[0]][0:nq, 0:128], rz.t[0:nq, 0:1], t1.t[0:nq, :],
                     ALU.mult, ALU.add, accr + rz.r + t1.r, o.r)
            self.tt("dve", t1.t[0:nq, :], o.t[0:nq, :], o.t[0:nq, :], ALU.mult, o.r + t1.r, t1.r)
            f.dve(lambda e: e.reduce_sum(ss.t[0:nq, 0:1], t1.t[0:nq, :], axis=AX.X), t1.r + ss.r, ss.r)
            self.ts("dve", ss.t[0:nq, 1:2], ss.t[0:nq, 0:1], 1.0 / 128, LN_EPS, ALU.mult, ALU.add, ss.r, ss.r)
            self.act(ss.t[0:nq, 1:2], ss.t[0:nq, 1:2], AF.Ln, ss.r, ss.r)
            self.act(ss.t[0:nq, 1:2], ss.t[0:nq, 1:2], AF.Exp, ss.r, ss.r, scale=-0.5)
            self.stt("dve", on.t[0:nq, :], o.t[0:nq, :], ss.t[0:nq, 1:2], self.gsub.t[0:nq, l, :],
                     ALU.mult, ALU.mult, o.r + ss.r + self.gsub.r, on.r)
        else:
            self.ts("dve", on.t[0:nq, :], self.ps[accb[0]][0:nq, 0:128], rz.t[0:nq, 0:1], None, ALU.mult, None,
                    accr + rz.r, on.r)

    def attn_work(self):
        wk = []
        for i in range(2):
            wk.append({"rz": self.atile("rz", [128, 2], F32), "t1": self.atile("t1", [128, 128], F32),
                       "o": self.atile("o", [128, 128], F32), "ss": self.atile("ss", [128, 2], F32),
                       "on": self.atile("on", [128, 128], BF16)})
        return wk

    def phase_attn_prompt(self, l, which):
        f = self.f
        NPB = self.NPB
        S = self.S
        A = which == "A"
        kTd, vd, qTd, oTd = (self.kaT, self.va, self.qaT, self.oaT) if A else (self.kbT, self.vb, self.qbT, self.obT)
        kn, vn, qn, on_ = ("kaT", "va", "qaT", "oaT") if A else ("kbT", "vb", "qbT", "obT")
        for hh in range(2):
            self.phase()
            E = self.load_E(l, which)
            kT = self.atile("kT", [128, 4, S], BF16)
            V1 = self.atile("V1", [128, NPB, 4, 130], BF16)
            f.op("dve", lambda e, V1=V1: e.memset(V1.t[:, :, :, 128:129], 1.0), V1.r, V1.r)
            nt = 2 if A else 1
            qTs = self.atiles("qT", 2, [128, 4, nt, 128], BF16)
            if A:
                for q_ in qTs:
                    f.op("dve", lambda e, q_=q_: e.memset(q_.t[64:128, :, 0, :], 0.0), q_.r, q_.r)
                    f.op("dve", lambda e, q_=q_: e.memset(q_.t[0:64, :, 1, :], 0.0), q_.r, q_.r)
            oTs = self.atiles("oT", 2, [128, 4, 128], BF16)
            P3 = self.atiles("P", 4, [128, 512], BF16)
            wk = self.attn_work()
            pend = None
            kres = [[f"kTb{hh}_{i}"] for i in range(NPB)]
            vres = [[f"vTb{hh}_{i}"] for i in range(NPB)]
            cnt = 0
            hook = self.attn_hooks.get((l, which, hh))
            if hook is not None:
                hook()
            for i in range(NPB):
                f.dma("sp", kT.t[:, :, i * 128:(i + 1) * 128],
                      kTd[l, hh * 512:(hh + 1) * 512, i * 128:(i + 1) * 128].rearrange("(h p) s -> p h s", p=128),
                      reads=[(kn, l, hh * 4 + c, i) for c in range(4)] + ["ARENA"], writes=kres[i])
                f.dma("sp", V1.t[:, i, :, 0:128],
                      vd[l, i * 128:(i + 1) * 128, hh * 512:(hh + 1) * 512].rearrange("s (h d) -> s h d", d=128),
                      reads=[(vn, l, i, hh), "ARENA"], writes=vres[i])

            def load_q(i):
                qT = self.nxt(qTs)
                qsrc = qTd[l, hh * 512:(hh + 1) * 512, i * 128:(i + 1) * 128].rearrange("(h p) s -> p h s", p=128)
                qrd = [(qn, l, hh * 4 + c, i) for c in range(4)]
                if A:
                    f.dma("sp", qT.t[0:64, :, 0, :], qsrc[0:64], reads=qrd, writes=qT.r)
                    f.dma("sp", qT.t[64:128, :, 1, :], qsrc[64:128], reads=qrd, writes=qT.r)
                else:
                    f.dma("sp", qT.t[:, :, 0, :], qsrc, reads=qrd, writes=qT.r)
                return qT
            qnext = load_q(0)
            for i in range(NPB):
                qT = qnext
                if i + 1 < NPB:
                    qnext = load_q(i + 1)
                oT = self.nxt(oTs)
                for h4 in range(4):
                    h = hh * 4 + h4
                    blocks = []
                    jlo = 0 if A else max(0, i - 4)
                    for j in range(jlo, i + 1):
                        if j == i:
                            kind = "diag"
                        elif j == i - 1:
                            kind = "prev"
                        elif (not A) and j == i - 4:
                            kind = "far4"
                        else:
                            kind = "far"
                        blocks.append({"kT": kT.t[:, h4, j * 128:(j + 1) * 128], "v": V1.t[:, j, h4, :],
                                       "nk": 128, "kind": kind, "r": kres[j] + vres[j] + V1.r})
                    cfar = self.c15.t[:, h:h + 1] if A else self.c0b.t[:, l * 8 + h:l * 8 + h + 1]
                    after = None
                    if h4 == 3:
                        def after(i=i, oT=oT, hh=hh):
                            f.dma(SQ, oTd[l, hh * 512:(hh + 1) * 512, i * 128:(i + 1) * 128]
                                  .rearrange("(h p) s -> p h s", p=128),
                                  oT.t[:, :, :], reads=oT.r, writes=[(on_, l, hh * 4 + c, i) for c in range(4)])
                    pend = self.attn_core(which, l, h, qT.t[:, h4, :, :], qT.r, 128, blocks, E, cfar, cnt % 2,
                                          oT.t[:, h4, :], oT.r, P3, wk[cnt % 2], pend, after, sbanks=(0, 1, 7))
                    cnt += 1
            pend[0]()
            pend[1]()

    def phase_attn_sample(self, l, which):
        f = self.f
        A = which == "A"
        NPB, TS = self.NPB, self.TS
        past = self.PAST if A else self.BPAST
        nkb = past // 128
        kTd, vd, qTd, oTd = (self.kaT, self.va, self.qaT, self.oaT) if A else (self.kbT, self.vb, self.qbT, self.obT)
        kn, vn, qn, on_ = ("kaT", "va", "qaT", "oaT") if A else ("kbT", "vb", "qbT", "obT")
        ck, cv = (self.cak, self.cav) if A else (self.cbk, self.cbv)
        self.phase()
        E = self.load_E(l, which)
        kTs = self.atiles("kTs", 2, [128, 8, past + TS], BF16)
        V1s = self.atiles("V1s", 2, [128, nkb + 1, 8, 130], BF16)
        for v in V1s:
            f.op("dve", lambda e, v=v: e.memset(v.t[:, :, :, 128:129], 1.0), v.r, v.r)
        craw = self.atiles("craw", 2, [128, 1024], BF16)
        nt = 2 if A else 1
        qT = self.atile("qTs", [128, 8, nt, 128], BF16)
        if A:
            f.op("dve", lambda e: e.memset(qT.t[64:128, :, 0, :], 0.0), qT.r, qT.r)
            f.op("dve", lambda e: e.memset(qT.t[0:64, :, 1, :], 0.0), qT.r, qT.r)
        oT = self.atile("oTs", [128, 8, 128], BF16)
        P3 = self.atiles("P", 3, [128, 512], BF16)
        wk = self.attn_work()
        tokc = NPB * 128
        qsrc = qTd[l, :, tokc:tokc + 128].rearrange("(h p) s -> p h s", p=128)
        qrd = [(qn, l, c, NPB) for c in range(8)]
        if A:
            f.dma("sp", qT.t[0:64, :, 0, :], qsrc[0:64], reads=qrd, writes=qT.r)
            f.dma("sp", qT.t[64:128, :, 1, :], qsrc[64:128], reads=qrd, writes=qT.r)
        else:
            f.dma("sp", qT.t[:, :, 0, :], qsrc, reads=qrd, writes=qT.r)
        cnt = 0
        pend = None
        for sb in range(self.NSB):
            kT, V1 = self.nxt(kTs), self.nxt(V1s)
            for kb in range(nkb):
                cr = self.nxt(craw)
                f.dma("pool", cr.t[:, :], ck[l, sb, kb * 128:(kb + 1) * 128, :], writes=cr.r)
                self.tok2feat(cr, lambda c, cr=cr: cr.t[:, c * 128:(c + 1) * 128], 8,
                              lambda c0, m, kT=kT, kb=kb: kT.t[:, c0:c0 + m, kb * 128:(kb + 1) * 128], kT.r)
                f.dma("pool", V1.t[:, kb, :, 0:128],
                      cv[l, sb, kb * 128:(kb + 1) * 128, :].rearrange("s (h d) -> s h d", d=128), writes=V1.r)
            f.dma("sp", kT.t[:, :, past:past + TS],
                  kTd[l, :, tokc + sb * TS:tokc + (sb + 1) * TS].rearrange("(h p) s -> p h s", p=128),
                  reads=[(kn, l, c, NPB) for c in range(8)], writes=kT.r)
            f.dma("sp", V1.t[0:TS, nkb, :, 0:128],
                  vd[l, tokc + sb * TS:tokc + (sb + 1) * TS, :].rearrange("s (h d) -> s h d", d=128),
                  reads=[(vn, l, NPB, c) for c in range(2)], writes=V1.r)
            for h in range(8):
                blocks = []
                for j in range(nkb + 1):
                    if j == nkb:
                        kind, nk = "diag", TS
                    elif j == nkb - 1:
                        kind, nk = "prev", 128
                    else:
                        kind, nk = "far", 128
                    blocks.append({"kT": kT.t[:, h, j * 128:j * 128 + nk], "v": V1.t[:, j, h, :], "nk": nk,
                                   "kind": kind, "r": kT.r[:1] + V1.r})
                cfar = self.c15.t[:, h:h + 1] if A else self.c0b.t[:, l * 8 + h:l * 8 + h + 1]
                pend = self.attn_core(which, l, h, qT.t[:, h, :, sb * TS:(sb + 1) * TS], qT.r, TS, blocks, E, cfar,
                                      cnt % 2, oT.t[:, h, sb * TS:(sb + 1) * TS], oT.r, P3, wk[cnt % 2], pend)
                cnt += 1
        pend[0]()
        pend[1]()
        f.dma(SQ, oTd[l, :, tokc:tokc + 128].rearrange("(h p) s -> p h s", p=128), oT.t[:, :, :],
              reads=oT.r, writes=[(on_, l, c, NPB) for c in range(8)])

    def ln_affine_store(self, xt, sm, gsrc, bsrc, pieces, dst_fn):
        f = self.f
        rs, nm = self.ln_stats(xt, sm)
        for c4 in range(4):
            cs = slice(c4 * 512, (c4 + 1) * 512)
            pg, pb = self.nxt(pieces), self.nxt(pieces)
            f.dma("sp", pg.t[:, :], gsrc[:, cs], writes=pg.r)
            f.dma("sp", pb.t[:, :], bsrc[:, cs], writes=pb.r)
            self.act(xt.t[:, cs], xt.t[:, cs], AF.Identity, xt.r + rs.r + nm.r, xt.r,
                     bias=nm.t[:, 0:1], scale=rs.t[:, 0:1])
            self.tt("dve", xt.t[:, cs], xt.t[:, cs], pg.t[:, :], ALU.mult, xt.r + pg.r, xt.r)
            self.tt("dve", xt.t[:, cs], xt.t[:, cs], pb.t[:, :], ALU.add, xt.r + pb.r, xt.r)
        for (dst, wres) in dst_fn():
            f.dma(SQ, dst, xt.t[:, :], reads=xt.r, writes=wres)

    def phase_C(self, l):
        f = self.f
        self.phase()
        TM = self.TCD
        oaTs = self.atiles("oaTt", 2, [128, 8, TM], BF16)
        obTs = self.atiles("obTt", 2, [128, 8, TM], BF16)
        mTs = self.atiles("mT", 1, [128, KC, TM], BF16)
        gts = self.atiles("gt", 4, [128, TM], BF16)
        tmps = self.atiles("tmpc", 4, [128, TM], F32)
        y1all = self.atiles("y1", TM // 128 + 2, [128, D], F32)
        pieces = self.atiles("pcc", 6, [128, 512], F32)
        xrs = self.atiles("xr", 3, [128, 512], F32)
        t1s = self.atiles("t1c", 2, [128, 512], F32)
        sms = [self.small_set(f"smc{i}") for i in range(2)]
        for (b0, nb) in self.supertiles(self.TCD):
            T = nb * 128
            tok0 = b0 * 128
            g = self.grp(b0)
            oaT, obT, mT = self.nxt(oaTs), self.nxt(obTs), self.nxt(mTs)
            y1 = [self.nxt(y1all) for _ in range(nb)]
            blks = list(range(b0, b0 + nb))
            f.dma("sp", oaT.t[:, :, 0:T], self.oaT[l, :, tok0:tok0 + T].rearrange("(kc p) s -> p kc s", p=128),
                  reads=[("oaT", l, c, b) for c in range(8) for b in blks], writes=oaT.r)
            f.dma("sp", obT.t[:, :, 0:T], self.obT[l, :, tok0:tok0 + T].rearrange("(kc p) s -> p kc s", p=128),
                  reads=[("obT", l, c, b) for c in range(8) for b in blks], writes=obT.r)
            for ct4 in range(4):
                wa = self.load_wb("w_oa", l, 0, 8, ct4 * 512, 512)
                wb_ = self.load_wb("w_ob", l, 0, 8, ct4 * 512, 512)
                for sub in range(4):
                    ct = ct4 * 4 + sub
                    gA, gB = self.nxt(gts), self.nxt(gts)
                    f.dma("sp", gA.t[:, 0:T], self.gaT[l, ct * 128:(ct + 1) * 128, tok0:tok0 + T],
                          reads=[("gaT", l, ct, b) for b in blks], writes=gA.r)
                    f.dma("sp", gB.t[:, 0:T], self.gbT[l, ct * 128:(ct + 1) * 128, tok0:tok0 + T],
                          reads=[("gbT", l, ct, b) for b in blks], writes=gB.r)
                    bA = self.rot.setdefault("gb", 0) % 4
                    bB = (bA + 1) % 4
                    self.rot["gb"] += 2
                    for kc in range(8):
                        self.mm(self.ps[bA][:, 0:T], wa.t[:, kc, sub * 128:(sub + 1) * 128], oaT.t[:, kc, 0:T],
                                kc == 0, kc == 7, wa.r + oaT.r, self.psr(bA))
                    for kc in range(8):
                        self.mm(self.ps[bB][:, 0:T], wb_.t[:, kc, sub * 128:(sub + 1) * 128], obT.t[:, kc, 0:T],
                                kc == 0, kc == 7, wb_.r + obT.r, self.psr(bB))
                    ta, tb_ = self.nxt(tmps), self.nxt(tmps)
                    self.tt("dve", ta.t[:, 0:T], self.ps[bA][:, 0:T], gA.t[:, 0:T], ALU.mult, self.psr(bA) + gA.r, ta.r)
                    self.tt("dve", tb_.t[:, 0:T], self.ps[bB][:, 0:T], gB.t[:, 0:T], ALU.mult, self.psr(bB) + gB.r, tb_.r)
                    self.tt("dve", mT.t[:, ct, 0:T], ta.t[:, 0:T], tb_.t[:, 0:T], ALU.add, ta.r + tb_.r, mT.r)
            for c4 in range(4):
                cs = slice(c4 * 512, (c4 + 1) * 512)
                wo = self.load_wb("w_out", l, 0, KC, c4 * 512, 512)
                gm = self.nxt(pieces)
                a, r = self.mod_piece(l, g, 2, c4)
                f.dma("sp", gm.t[:, :], a, reads=r, writes=gm.r)
                for tb in range(nb):
                    blk = b0 + tb
                    xr = self.nxt(xrs)
                    a, r = self.x_src(l, blk)
                    f.dma("sp", xr.t[:, :], a[:, cs], reads=r, writes=xr.r)
                    bank = self.rot.setdefault("gb", 0) % 4
                    self.rot["gb"] += 1
                    for kc in range(KC):
                        self.mm(self.ps[bank][:, :], mT.t[:, kc, tb * 128:(tb + 1) * 128], wo.t[:, kc, :],
                                kc == 0, kc == KC - 1, mT.r + wo.r, self.psr(bank))
                    t1 = self.nxt(t1s)
                    self.tt("dve", t1.t[:, :], self.ps[bank][:, :], gm.t[:, :], ALU.mult, self.psr(bank) + gm.r, t1.r)
                    self.ts("dve", xr.t[:, :], xr.t[:, :], ALPHA, None, ALU.mult, None, xr.r, xr.r)
                    self.tt("dve", y1[tb].t[:, cs], xr.t[:, :], t1.t[:, :], ALU.add, xr.r + t1.r, y1[tb].r)
            for tb in range(nb):
                blk = b0 + tb
                self.ln_affine_store(y1[tb], self.nxt(sms), self.ln1g[l], self.ln1b[l], pieces,
                                     lambda blk=blk: [(self.x1[l, blk * 128:(blk + 1) * 128, :],
                                                       [("x1", l, blk, c) for c in range(4)])])

    def phase_D(self, l):
        f = self.f
        last = l == DEPTH - 1
        self.phase()
        TM = self.TCD
        x1all = self.atiles("x1t", TM // 128 + 1, [128, D], F32)
        h2T = self.atile("h2T", [128, KC, TM], BF16)
        uT = self.atile("uT", [128, FC, TM], BF16)
        hbs = self.atiles("hbd", 2, [128, D], BF16)
        pieces = self.atiles("pcd", 6, [128, 512], F32)
        tmp = self.atiles("tmpd", 2, [128, 512], F32)
        s1s = self.atiles("s1", 2, [128, TM], F32)
        t1s = self.atiles("t1d", 2, [128, 512], F32)
        sms = [self.small_set(f"smd{i}") for i in range(2)]
        for (b0, nb) in self.supertiles(self.TCD):
            T = nb * 128
            g = self.grp(b0)
            x1t = [self.nxt(x1all) for _ in range(nb)]
            for tb in range(nb):
                blk = b0 + tb
                f.dma("sp", x1t[tb].t[:, :], self.x1[l, blk * 128:(blk + 1) * 128, :],
                      reads=[("x1", l, blk, c) for c in range(4)], writes=x1t[tb].r)
                self.ln_mod_T(l, blk, x1t[tb], self.nxt(sms), pieces, tmp, self.nxt(hbs), 4, 3,
                              lambda c0, m, tb=tb: h2T.t[:, c0:c0 + m, tb * 128:(tb + 1) * 128], h2T.r)
            for f4 in range(DFF // 512):
                w1t = self.load_wb("w1", l, 0, KC, f4 * 512, 512)
                w3t = self.load_wb("w3", l, 0, KC, f4 * 512, 512)
                for sub in range(4):
                    fc = f4 * 4 + sub
                    b1 = self.rot.setdefault("gb", 0) % 4
                    b3 = (b1 + 1) % 4
                    self.rot["gb"] += 2
                    for kc in range(KC):
                        self.mm(self.ps[b1][:, 0:T], w1t.t[:, kc, sub * 128:(sub + 1) * 128], h2T.t[:, kc, 0:T],
                                kc == 0, kc == KC - 1, w1t.r + h2T.r, self.psr(b1))
                    for kc in range(KC):
                        self.mm(self.ps[b3][:, 0:T], w3t.t[:, kc, sub * 128:(sub + 1) * 128], h2T.t[:, kc, 0:T],
                                kc == 0, kc == KC - 1, w3t.r + h2T.r, self.psr(b3))
                    s1 = self.nxt(s1s)
                    self.act(s1.t[:, 0:T], self.ps[b1][:, 0:T], AF.Silu, self.psr(b1), s1.r)
                    self.tt("dve", uT.t[:, fc, 0:T], self.ps[b3][:, 0:T], s1.t[:, 0:T], ALU.mult,
                            self.psr(b3) + s1.r, uT.r)
            kqs = [(0, 16), (16, 16), (32, 12)]
            for c4 in range(4):
                cs = slice(c4 * 512, (c4 + 1) * 512)
                base = 0 if c4 % 2 == 0 else 4
                banks = [base + i for i in range(nb)]
                gf = self.nxt(pieces)
                a, r = self.mod_piece(l, g, 5, c4)
                f.dma("sp", gf.t[:, :], a, reads=r, writes=gf.r)
                for qi, (k0, nk) in enumerate(kqs):
                    wt = self.load_wb("w2", l, k0, nk, c4 * 512, 512)
                    for tb in range(nb):
                        for kc in range(nk):
                            self.mm(self.ps[banks[tb]][:, :], uT.t[:, k0 + kc, tb * 128:(tb + 1) * 128], wt.t[:, kc, :],
                                    qi == 0 and kc == 0, qi == len(kqs) - 1 and kc == nk - 1,
                                    uT.r + wt.r, self.psr(banks[tb]))
                for tb in range(nb):
                    t1 = self.nxt(t1s)
                    self.tt("dve", t1.t[:, :], self.ps[banks[tb]][:, :], gf.t[:, :], ALU.mult,
                            self.psr(banks[tb]) + gf.r, t1.r)
                    self.ts("dve", x1t[tb].t[:, cs], x1t[tb].t[:, cs], ALPHA, None, ALU.mult, None,
                            x1t[tb].r, x1t[tb].r)
                    self.tt("dve", x1t[tb].t[:, cs], x1t[tb].t[:, cs], t1.t[:, :], ALU.add,
                            x1t[tb].r + t1.r, x1t[tb].r)
            for tb in range(nb):
                blk = b0 + tb

                def dsts(blk=blk):
                    if not last:
                        return [(self.x2[l, blk * 128:(blk + 1) * 128, :], [("x2", l, blk, c) for c in range(4)])]
                    if blk < self.NPB:
                        return [(self.y_p[blk * 128:(blk + 1) * 128, :], [])]
                    return [(self.y_s[:, :], [])]
                self.ln_affine_store(x1t[tb], self.nxt(sms), self.ln2g[l], self.ln2b[l], pieces, dsts)

    def build(self, upto=99):
        rest = ("w_oa", "w_ob", "w_out", "w1", "w3", "w2")

        self.attn_hooks[(0, "A", 0)] = lambda: self.convert_weights(0, rest)
        self.attn_hooks[(0, "A", 1)] = lambda: [self.convert_weights(j, ("w_in",) + rest) for j in range(1, DEPTH)]
        steps = [lambda: (self.convert_weights(0, ("w_in",)), self.setup())]
        for l in range(DEPTH):
            steps += [lambda l=l: self.phase_A(l),
                      lambda l=l: self.phase_attn_prompt(l, "A"),
                      lambda l=l: self.phase_attn_prompt(l, "B"),
                      lambda l=l: self.phase_attn_sample(l, "A"),
                      lambda l=l: self.phase_attn_sample(l, "B"),
                      lambda l=l: self.phase_C(l),
                      lambda l=l: self.phase_D(l)]
        for st in steps[:upto]:
            st()
        if self.wplan is None:
            return None, self.wlog
        info = self.f.emit()
        return self.nc, info


def _t5_bucket_np(rel):
    nb = 16
    max_exact = 8
    bucket = (rel > 0).astype(np.int64) * nb
    n = np.abs(rel)
    nf = np.maximum(n, 1).astype(np.float32)
    large = max_exact + (np.log(nf / max_exact) / math.log(128 / max_exact) * (nb - max_exact)).astype(np.int64)
    large = np.minimum(large, nb - 1)
    return bucket + np.where(n < max_exact, n, large)


def _static_tables():
    k = np.arange(128)[:, None]
    c = np.arange(256)[None, :]
    rel = k - c
    idxA = _t5_bucket_np(rel)
    idxB = np.clip(rel, -128, 128) + 128
    q = np.arange(128)[None, :]
    mdiag = ((k // CHUNK) <= (q // CHUNK)).astype(np.float32)
    mask = np.concatenate([mdiag, np.ones((128, 128), np.float32)], axis=1)
    return idxA, idxB, mask


def _bc(v, n=128):
    return np.ascontiguousarray(np.broadcast_to(v[..., None, :], v.shape[:-1] + (n, v.shape[-1])))


_CACHE = {}


def _get_program(S, upto=99):
    if S not in _CACHE:
        _, wplan = Builder(S=S, TA=min(1024, S), TCD=min(512, S)).build(upto)
        b = Builder(S=S, TA=min(1024, S), TCD=min(512, S), wplan=wplan)
        nc, info = b.build(upto)
        _CACHE[S] = nc
    return _CACHE[S]


def make_in_maps(inp, ncores, S):
    f32 = np.float32
    idxA, idxB, mask = _static_tables()
    t5 = np.asarray(inp["t5_bias"], f32)
    relb = np.asarray(inp["rel_bias"], f32)
    biasA = np.ascontiguousarray(np.transpose(t5[idxA], (0, 2, 1)))
    biasB = np.ascontiguousarray(np.transpose(relb[:, idxB], (0, 1, 3, 2)))
    c15A = _bc(t5[15])
    c0B = _bc(relb[:, 0, :])
    lamv = np.stack([inp["lambda_q1"], inp["lambda_k1"], inp["lambda_q2"], inp["lambda_k2"]], axis=1)
    shared = {
        "w_mod": np.asarray(inp["w_mod"], f32), "bmod": _bc(np.asarray(inp["b_mod"], f32)),
        "w_in": np.asarray(inp["w_in"], f32), "lamv": _bc(np.asarray(lamv, f32)),
        "subln": _bc(np.asarray(inp["subln_g"], f32)), "biasA": biasA, "c15A": c15A, "biasB": biasB, "c0B": c0B,
        "mask01": mask, "ident": np.eye(128, dtype=f32),
        "w_oa": np.asarray(inp["w_oa"], f32), "w_ob": np.asarray(inp["w_ob"], f32),
        "w_out": np.asarray(inp["w_out"], f32),
        "ln1g": _bc(np.asarray(inp["ln1_g"], f32)), "ln1b": _bc(np.asarray(inp["ln1_b"], f32)),
        "w1": np.asarray(inp["w1"], f32), "w3": np.asarray(inp["w3"], f32), "w2": np.asarray(inp["w2"], f32),
        "ln2g": _bc(np.asarray(inp["ln2_g"], f32)), "ln2b": _bc(np.asarray(inp["ln2_b"], f32)),
    }
    maps = []
    for i in range(ncores):
        sb = slice(4 * i, 4 * i + 4)
        cs = np.asarray(inp["c_sample"][sb], f32)
        cbc = np.stack([np.broadcast_to(np.asarray(inp["c_prompt"][i], f32)[None, :], (128, D)),
                        np.repeat(cs, 32, axis=0)], axis=0)
        m = dict(shared)
        m.update({
            "xp": np.ascontiguousarray(inp["x_prompt"][i], dtype=f32),
            "xs": np.ascontiguousarray(np.asarray(inp["x_sample"][sb], f32).reshape(128, D)),
            "cak": np.ascontiguousarray(np.asarray(inp["cache_a_k"][:, sb], f32).reshape(DEPTH, 4, -1, 1024)),
            "cav": np.ascontiguousarray(np.asarray(inp["cache_a_v"][:, sb], f32).reshape(DEPTH, 4, -1, 1024)),
            "cbk": np.ascontiguousarray(np.asarray(inp["cache_b_k"][:, sb], f32).reshape(DEPTH, 4, -1, 1024)),
            "cbv": np.ascontiguousarray(np.asarray(inp["cache_b_v"][:, sb], f32).reshape(DEPTH, 4, -1, 1024)),
            "cbc": np.ascontiguousarray(cbc),
        })
        maps.append(m)
    return maps


def assemble(results, ncores, S):
    L = DEPTH
    B = ncores
    bo = min(512, S)
    y_p = np.stack([r["y_p"] for r in results], 0)
    y_s = np.concatenate([r["y_s"].reshape(4, 32, D) for r in results], 0)
    akp = np.stack([r["akp"] for r in results], 1).reshape(L, B, S, 8, 2, 64)
    avp = np.stack([r["avp"] for r in results], 1).reshape(L, B, S, 8, 128)
    bkp = np.stack([r["bkp"] for r in results], 1).reshape(L, B, bo, 8, 128)
    bvp = np.stack([r["bvp"] for r in results], 1).reshape(L, B, bo, 8, 128)
    aks = np.concatenate([r["aks"].reshape(L, 4, 32, 1024) for r in results], 1).reshape(L, 4 * B, 32, 8, 2, 64)
    avs = np.concatenate([r["avs"].reshape(L, 4, 32, 1024) for r in results], 1).reshape(L, 4 * B, 32, 8, 128)
    bks = np.concatenate([r["bks"].reshape(L, 4, 32, 1024) for r in results], 1).reshape(L, 4 * B, 32, 8, 128)
    bvs = np.concatenate([r["bvs"].reshape(L, 4, 32, 1024) for r in results], 1).reshape(L, 4 * B, 32, 8, 128)
    return tuple(np.ascontiguousarray(a, dtype=np.float32) for a in (y_p, y_s, akp, avp, bkp, bvp, aks, avs, bks, bvs))


def kernel(**inputs):
    S = inputs["x_prompt"].shape[1]
    ncores = inputs["x_prompt"].shape[0]
    nc = _get_program(S)
    maps = make_in_maps(inputs, ncores, S)
    res = run_bass_kernel_spmd(nc, maps, core_ids=list(range(ncores)))
    return assemble(res.results, ncores, S)
```
